# Optimizing a Trainium2 kernel written in Bass

```python
import math
import jax, jax.numpy as jnp
from jax import lax
import numpy as np

D_MODEL = 2048
BATCH = 32
SEQ = 256
DEPTH = 2
DEC_BATCH = 8
DEC_SEQ = 2048
PAST_LEN = 256

GRID_W = 64
N_MIXERS = 2
N_SSD_LAYERS = (DEPTH + 1) // 2
N_ATTN_LAYERS = DEPTH // 2
D_INNER = 2 * D_MODEL
SSD_HEAD_DIM = 64
SSD_HEADS = D_INNER // SSD_HEAD_DIM
SSD_GROUPS = 8
SSD_HPG = SSD_HEADS // SSD_GROUPS
D_STATE = 128
D_CONV = 5
CHUNK = 128
CONV_DIM = D_INNER + 2 * SSD_GROUPS * D_STATE
D_IN_PROJ = D_INNER + CONV_DIM + 2 * SSD_HEADS
ATTN_HEAD_DIM = 128
N_HEADS = D_MODEL // ATTN_HEAD_DIM
N_KV_HEADS = 4
KV_REP = N_HEADS // N_KV_HEADS
QKV_DIM = (N_HEADS + 2 * N_KV_HEADS) * ATTN_HEAD_DIM
ROPE_AXIS_DIM = ATTN_HEAD_DIM // 2
ROPE_THETA = 10000.0
Q_BLOCK = 128
D_FF = ((math.ceil(8 * D_MODEL / 3) + 255) // 256) * 256
EPS = 1e-6

kernel_name = 'hybrid_ssd_gqa_diffusion_step'


def rmsnorm(x, g):
    xf = x.astype(jnp.float32)
    y = xf * lax.rsqrt(jnp.mean(xf * xf, axis=-1, keepdims=True) + EPS) * g.astype(jnp.float32)
    return y.astype(x.dtype)


def adaln(cond, w, b):
    m = jax.nn.silu(cond) @ w + b
    return jnp.split(m[:, None, :], 6, axis=-1)


def modulate(h, shift, scale):
    return h * (1.0 + scale) + shift


def swiglu(h, w_gu, w_down):
    g, u = jnp.split(h @ w_gu, 2, axis=-1)
    return (jax.nn.silu(g) * u) @ w_down


def dwconv_centred(u, w, b):
    out = lax.conv_general_dilated(u, w[:, None, :].astype(u.dtype), window_strides=(1,),
                                   padding=[(D_CONV // 2, D_CONV // 2)],
                                   dimension_numbers=('NWC', 'WIO', 'NWC'),
                                   feature_group_count=u.shape[-1])
    return out + b


def ssd_scan(x, dt, A, Bm, Cm, s0):
    b, L = x.shape[:2]
    nc = L // CHUNK

    def chunks(t):
        return t.reshape((b, nc, CHUNK) + t.shape[2:]).swapaxes(0, 1)

    lower = jnp.tril(jnp.ones((CHUNK, CHUNK), dtype=bool))

    def step(S, inp):
        xc, dtc, Bc, Cc = inp
        acs = jnp.cumsum(dtc * A, axis=1)
        xdt = xc * dtc[..., None]
        acs_t = jnp.moveaxis(acs, 1, -1)
        seg = acs_t[..., :, None] - acs_t[..., None, :]
        decay = jnp.exp(jnp.where(lower, seg, -jnp.inf))
        cb = jnp.einsum('blgn,bsgn->bgls', Cc, Bc)
        y = jnp.einsum('bgrls,bsgrp->blgrp', cb[:, :, None] * decay, xdt)
        y = y + jnp.einsum('blgn,bgrpn->blgrp', Cc, S) * jnp.exp(acs)[..., None]
        to_end = jnp.exp(acs[:, -1:] - acs)
        S = S * jnp.exp(acs[:, -1])[..., None, None] + jnp.einsum(
            'bsgn,bsgrp->bgrpn', Bc, xdt * to_end[..., None])
        return S, y

    S, ys = lax.scan(step, s0, (chunks(x), chunks(dt), chunks(Bm), chunks(Cm)))
    return ys.swapaxes(0, 1).reshape(x.shape), S


def gated_rmsnorm(y, z, g):
    yz = y * jax.nn.silu(z.astype(jnp.float32))
    yz = yz.reshape(yz.shape[:-1] + (SSD_GROUPS, D_INNER // SSD_GROUPS))
    yz = yz * lax.rsqrt(jnp.mean(yz * yz, axis=-1, keepdims=True) + EPS)
    return yz.reshape(y.shape) * g.astype(jnp.float32)


def ssd_mixer(h, s0_f, s0_b, w_in, conv_w, conv_b, a_log, dt_bias, d_skip, norm_g, w_out):
    b, L, _ = h.shape
    z, xbc, dt = jnp.split(h @ w_in, [D_INNER, D_INNER + CONV_DIM], axis=-1)
    xbc = jax.nn.silu(dwconv_centred(xbc, conv_w, conv_b)).astype(jnp.float32)
    x, Bm, Cm = jnp.split(xbc, [D_INNER, D_INNER + SSD_GROUPS * D_STATE], axis=-1)
    x = x.reshape(b, L, SSD_GROUPS, SSD_HPG, SSD_HEAD_DIM)
    Bm = Bm.reshape(b, L, SSD_GROUPS, D_STATE)
    Cm = Cm.reshape(b, L, SSD_GROUPS, D_STATE)
    dt = jax.nn.softplus(dt.astype(jnp.float32).reshape(b, L, 2, SSD_GROUPS, SSD_HPG)
                         + dt_bias.astype(jnp.float32).reshape(2, SSD_GROUPS, SSD_HPG))
    A = -jnp.exp(a_log.astype(jnp.float32)).reshape(2, SSD_GROUPS, SSD_HPG)
    dsk = d_skip.astype(jnp.float32).reshape(2, SSD_GROUPS, SSD_HPG)
    st = (b, SSD_GROUPS, SSD_HPG, SSD_HEAD_DIM, D_STATE)
    y_f, s_f = ssd_scan(x, dt[:, :, 0], A[0], Bm, Cm, s0_f.astype(jnp.float32).reshape(st))
    flip = lambda t: jnp.flip(t, axis=1)
    y_b, s_b = ssd_scan(flip(x), flip(dt[:, :, 1]), A[1], flip(Bm), flip(Cm),
                        s0_b.astype(jnp.float32).reshape(st))
    y = y_f + flip(y_b) + (dsk[0] + dsk[1])[..., None] * x
    y = gated_rmsnorm(y.reshape(b, L, D_INNER), z, norm_g).astype(h.dtype)
    hs = (b, SSD_HEADS, SSD_HEAD_DIM, D_STATE)
    return y @ w_out, s_f.reshape(hs).astype(h.dtype), s_b.reshape(hs).astype(h.dtype)


def rope_tables(L):
    rows = L // GRID_W
    row_pos = jnp.repeat(jnp.arange(rows, dtype=jnp.float32), GRID_W)
    col_pos = jnp.arange(L, dtype=jnp.float32) % GRID_W
    inv = ROPE_THETA ** (-jnp.arange(0, ROPE_AXIS_DIM, 2, dtype=jnp.float32) / ROPE_AXIS_DIM)
    ang_r = row_pos[:, None] * inv[None, :]
    ang_c = col_pos[:, None] * inv[None, :]
    return jnp.cos(ang_r), jnp.sin(ang_r), jnp.cos(ang_c), jnp.sin(ang_c)


def rotate(x, cos, sin):
    shp = (x.shape[1],) + (1,) * (x.ndim - 3) + (cos.shape[-1],)
    cos, sin = cos.reshape(shp), sin.reshape(shp)
    x1, x2 = jnp.split(x, 2, axis=-1)
    return jnp.concatenate([x1 * cos - x2 * sin, x1 * sin + x2 * cos], axis=-1)


def rope_2d(x, tabs):
    cr, sr, cc, sc = tabs
    out = jnp.concatenate([rotate(x[..., :ROPE_AXIS_DIM], cr, sr),
                           rotate(x[..., ROPE_AXIS_DIM:], cc, sc)], axis=-1)
    return out.astype(x.dtype)


def gqa_qkv(h, w_qkv, q_norm, k_norm):
    b, L, _ = h.shape
    q, k, v = jnp.split(h @ w_qkv, [N_HEADS * ATTN_HEAD_DIM, (N_HEADS + N_KV_HEADS) * ATTN_HEAD_DIM], axis=-1)
    q = rmsnorm(q.reshape(b, L, N_KV_HEADS, KV_REP, ATTN_HEAD_DIM), q_norm)
    k = rmsnorm(k.reshape(b, L, N_KV_HEADS, ATTN_HEAD_DIM), k_norm)
    v = v.reshape(b, L, N_KV_HEADS, ATTN_HEAD_DIM)
    return q, k, v


def block_attention(q, k, v):
    b, lq = q.shape[:2]
    nb = lq // Q_BLOCK
    qb = q.reshape((b, nb, Q_BLOCK) + q.shape[2:]).swapaxes(0, 1)
    scale = ATTN_HEAD_DIM ** -0.5

    def one(qblk):
        s = jnp.einsum('bqkrd,bskd->bkrqs', qblk, k).astype(jnp.float32) * scale
        p = jax.nn.softmax(s, axis=-1).astype(v.dtype)
        return jnp.einsum('bkrqs,bskd->bqkrd', p, v)

    o = lax.map(one, qb)
    return o.swapaxes(0, 1).reshape(b, lq, N_HEADS * ATTN_HEAD_DIM)


def setup_inputs(seed: int = 0) -> dict:
    key = jax.random.key(seed)
    ks = jax.random.split(key, 32)
    f32 = jnp.float32

    def nrm(k, shape, scale):
        return jax.random.normal(k, shape, f32) * scale

    dt0 = jnp.exp(jax.random.uniform(ks[13], (N_SSD_LAYERS, 2, SSD_HEADS), f32,
                                     math.log(1e-3), math.log(1e-1)))
    return {
        'x_prompt': nrm(ks[0], (BATCH, SEQ, D_MODEL), 1.0),
        'x_sample': nrm(ks[1], (DEC_BATCH, DEC_SEQ, D_MODEL), 1.0),
        'state_ssd_fwd': nrm(ks[2], (DEC_BATCH, N_SSD_LAYERS, SSD_HEADS, SSD_HEAD_DIM, D_STATE), 0.1),
        'state_ssd_bwd': nrm(ks[3], (DEC_BATCH, N_SSD_LAYERS, SSD_HEADS, SSD_HEAD_DIM, D_STATE), 0.1),
        'cache_k': nrm(ks[4], (DEC_BATCH, N_ATTN_LAYERS, PAST_LEN, N_KV_HEADS, ATTN_HEAD_DIM), 1.0),
        'cache_v': nrm(ks[5], (DEC_BATCH, N_ATTN_LAYERS, PAST_LEN, N_KV_HEADS, ATTN_HEAD_DIM), 1.0),
        'c': nrm(ks[6], (DEC_BATCH, D_MODEL), 1.0),
        'c_ctx': nrm(ks[7], (D_MODEL,), 1.0),
        'w_mod': nrm(ks[8], (DEPTH, D_MODEL, 6 * D_MODEL), 0.5 * D_MODEL ** -0.5),
        'b_mod': nrm(ks[9], (DEPTH, 6 * D_MODEL), 0.02),
        'norm_mix': 1.0 + nrm(ks[10], (DEPTH, D_MODEL), 0.02),
        'norm_ffn': 1.0 + nrm(ks[11], (DEPTH, D_MODEL), 0.02),
        'ssd_w_in': nrm(ks[12], (N_SSD_LAYERS, D_MODEL, D_IN_PROJ), D_MODEL ** -0.5),
        'ssd_conv_w': nrm(ks[14], (N_SSD_LAYERS, D_CONV, CONV_DIM), D_CONV ** -0.5),
        'ssd_conv_b': nrm(ks[15], (N_SSD_LAYERS, CONV_DIM), 0.02),
        'ssd_a_log': jnp.log(jax.random.uniform(ks[16], (N_SSD_LAYERS, 2, SSD_HEADS), f32, 1.0, 16.0)),
        'ssd_dt_bias': dt0 + jnp.log(-jnp.expm1(-dt0)),
        'ssd_d': 1.0 + nrm(ks[17], (N_SSD_LAYERS, 2, SSD_HEADS), 0.02),
        'ssd_norm': 1.0 + nrm(ks[18], (N_SSD_LAYERS, D_INNER), 0.02),
        'ssd_w_out': nrm(ks[19], (N_SSD_LAYERS, D_INNER, D_MODEL), D_INNER ** -0.5),
        'attn_w_qkv': nrm(ks[20], (N_ATTN_LAYERS, D_MODEL, QKV_DIM), D_MODEL ** -0.5),
        'attn_q_norm': 1.0 + nrm(ks[21], (N_ATTN_LAYERS, ATTN_HEAD_DIM), 0.02),
        'attn_k_norm': 1.0 + nrm(ks[22], (N_ATTN_LAYERS, ATTN_HEAD_DIM), 0.02),
        'attn_w_o': nrm(ks[23], (N_ATTN_LAYERS, N_HEADS * ATTN_HEAD_DIM, D_MODEL), (N_HEADS * ATTN_HEAD_DIM) ** -0.5),
        'ffn_w_gu': nrm(ks[24], (DEPTH, D_MODEL, 2 * D_FF), D_MODEL ** -0.5),
        'ffn_w_down': nrm(ks[25], (DEPTH, D_FF, D_MODEL), D_FF ** -0.5),
        'final_norm': 1.0 + nrm(ks[26], (D_MODEL,), 0.02),
    }


def reference(x_prompt, x_sample, state_ssd_fwd, state_ssd_bwd, cache_k, cache_v, c, c_ctx,
              w_mod, b_mod, norm_mix, norm_ffn,
              ssd_w_in, ssd_conv_w, ssd_conv_b, ssd_a_log, ssd_dt_bias, ssd_d, ssd_norm, ssd_w_out,
              attn_w_qkv, attn_q_norm, attn_k_norm, attn_w_o,
              ffn_w_gu, ffn_w_down, final_norm):
    xp, xs = x_prompt, x_sample
    bp = xp.shape[0]
    tabs = rope_tables(xs.shape[1])
    new_f, new_b, new_k, new_v = [], [], [], []
    for i in range(DEPTH):
        mp = adaln(c_ctx[None, :], w_mod[i], b_mod[i])
        ms = adaln(c, w_mod[i], b_mod[i])
        hp = modulate(rmsnorm(xp, norm_mix[i]), mp[0], mp[1])
        hs = modulate(rmsnorm(xs, norm_mix[i]), ms[0], ms[1])
        j = i // N_MIXERS
        if i % N_MIXERS == 0:
            prm = (ssd_w_in[j], ssd_conv_w[j], ssd_conv_b[j], ssd_a_log[j], ssd_dt_bias[j],
                   ssd_d[j], ssd_norm[j], ssd_w_out[j])
            zeros = jnp.zeros((bp, SSD_HEADS, SSD_HEAD_DIM, D_STATE), xp.dtype)
            op, sf, sb = ssd_mixer(hp, zeros, zeros, *prm)
            os_, _, _ = ssd_mixer(hs, state_ssd_fwd[:, j], state_ssd_bwd[:, j], *prm)
            new_f.append(sf)
            new_b.append(sb)
        else:
            qp, kp, vp = gqa_qkv(hp, attn_w_qkv[j], attn_q_norm[j], attn_k_norm[j])
            op = block_attention(qp, kp, vp) @ attn_w_o[j]
            qs, ks_, vs = gqa_qkv(hs, attn_w_qkv[j], attn_q_norm[j], attn_k_norm[j])
            qs = rope_2d(qs, tabs)
            ks_ = rope_2d(ks_, tabs)
            k_all = jnp.concatenate([cache_k[:, j].astype(ks_.dtype), ks_], axis=1)
            v_all = jnp.concatenate([cache_v[:, j].astype(vs.dtype), vs], axis=1)
            os_ = block_attention(qs, k_all, v_all) @ attn_w_o[j]
            new_k.append(kp)
            new_v.append(vp)
        xp = xp + mp[2] * op
        xs = xs + ms[2] * os_
        hp = modulate(rmsnorm(xp, norm_ffn[i]), mp[3], mp[4])
        hs = modulate(rmsnorm(xs, norm_ffn[i]), ms[3], ms[4])
        xp = xp + mp[5] * swiglu(hp, ffn_w_gu[i], ffn_w_down[i])
        xs = xs + ms[5] * swiglu(hs, ffn_w_gu[i], ffn_w_down[i])
    y_prompt = rmsnorm(xp, final_norm)
    y_sample = rmsnorm(xs, final_norm)
    return (y_prompt, y_sample, jnp.stack(new_f, axis=1), jnp.stack(new_b, axis=1),
            jnp.stack(new_k, axis=1), jnp.stack(new_v, axis=1))
```

```python
import numpy as np
import concourse.bass as bass
import concourse.mybir as mybir
from concourse.bass_utils import run_bass_kernel_spmd
from contextlib import ExitStack
import os
KSKIP = os.environ.get('KSKIP', '')

F32 = mybir.dt.float32
BF16 = mybir.dt.bfloat16
AF = mybir.ActivationFunctionType
ALU = mybir.AluOpType
AX = mybir.AxisListType


class Res:
    __slots__ = ("name", "w", "r")

    def __init__(self, name):
        self.name = name
        self.w = None
        self.r = {}

    def set_w(self, tok):
        self.w = tok
        self.r = {}


class PRes(Res):
    __slots__ = ()
    excl = True


class DRes:
    __slots__ = ("name", "w", "r")

    def __init__(self, name):
        self.name = name
        self.w = {}
        self.r = {}

    def set_w(self, tok):
        k, v = tok
        if self.w.get(k, 0) < v:
            self.w[k] = v


class _Rec:
    def __init__(self):
        self.calls = []

    def __getattr__(self, name):
        def f(*a, **k):
            self.calls.append((name, a, k))
            return None
        return f


def _replayer(calls):
    def replay(e):
        r = None
        for (n, a, k) in calls:
            r = getattr(e, n)(*a, **k)
        return r
    return replay


class Sched:
    ENG = ("pe", "act", "dve", "pool", "sp")

    def __init__(self, nc):
        self.nc = nc
        self.ops = {e: [] for e in self.ENG}
        self.sems = {}
        self.cnt = {}
        self.waited = {e: {} for e in self.ENG}
        for e in self.ENG:
            self._sem("E_" + e)
        self.nwaits = 0
        self.dmap = {}

    def _sem(self, key):
        if key not in self.sems:
            self.sems[key] = self.nc.alloc_semaphore(key)
            self.cnt[key] = 0
        return self.sems[key]

    def op(self, eng, fn, reads=(), writes=(), ms=True):
        key = "E_" + eng
        rec = _Rec()
        fn(rec)
        fn = _replayer(rec.calls)
        ex = [r for r in reads if getattr(r, "excl", False)]
        if ex:
            writes = list(writes) + [r for r in ex if r not in writes]
            reads = [r for r in reads if not getattr(r, "excl", False)]
        waits = self._deps_compute(eng, reads, writes)
        tokv = self.cnt[key] + 1
        if ms:
            self.cnt[key] = tokv
        self.ops[eng].append((waits, fn, (key, 1) if ms else None))
        tok = (key, tokv)
        for r in reads:
            if r.r.get(key, 0) < tokv:
                r.r[key] = tokv
        for w in writes:
            w.set_w(tok)
        return tok

    def _deps_compute(self, eng, reads, writes, strict=False):
        deps = {}
        mykey = "E_" + eng
        def add(tok, allow_self):
            allow_self = allow_self or strict
            if tok is None:
                return
            k, v = tok
            if k == mykey and not allow_self:
                return
            if deps.get(k, 0) < v:
                deps[k] = v
        for r in reads:
            if isinstance(r.w, dict):
                for k, v in r.w.items():
                    add((k, v), True)
                continue
            add(r.w, eng != "pe")
        for w in writes:
            if not isinstance(w.w, dict):
                add(w.w, False)
            for k, v in w.r.items():
                add((k, v), False)
        out = []
        for k, v in deps.items():
            if self.waited[eng].get(k, 0) >= v:
                continue
            assert self.cnt[k] >= v, f"wait on future milestone {k}>={v} (have {self.cnt[k]}) from {eng}"
            self.waited[eng][k] = v
            out.append((k, v))
        return out

    def dma(self, eng, semkey, out, in_, reads=(), writes=(), **kw):
        if semkey not in self.dmap:
            self.dmap[semkey] = f"D{len(self.dmap)}"
        semkey = self.dmap[semkey]
        self._sem(semkey)
        waits = self._deps_compute(eng, reads, writes, strict=True)
        self.cnt[semkey] += 16
        tok = (semkey, self.cnt[semkey])
        self.ops[eng].append((waits, lambda e: e.dma_start(out=out, in_=in_, **kw), (semkey, 16)))
        for r in reads:
            if r.r.get(semkey, 0) < tok[1]:
                r.r[semkey] = tok[1]
        for w in writes:
            w.set_w(tok)
        return tok

    def barrier(self):
        self._sem("BAR")
        for e in ("pe", "act", "dve", "pool"):
            real = [o for o in self.ops[e] if o[1] is not None]
            if real:
                assert real[-1][2] is not None, f"last op on {e} is not a milestone"
        waits = []
        for k, c in self.cnt.items():
            if k == "BAR":
                continue
            if c > self.waited["sp"].get(k, 0):
                waits.append((k, c))
        self.cnt["BAR"] += 1
        v = self.cnt["BAR"]
        bar = self.sems["BAR"]
        self.ops["sp"].append((waits, lambda e: e.sem_inc(bar, 1), None))
        for e in self.ENG:
            if e != "sp":
                self.ops[e].append(([("BAR", v)], None, None))
            self.waited[e] = dict(self.cnt)
        self.dmap = {}

    def finish(self, final_res):
        waits = self._deps_compute("sp", final_res, ())
        self.ops["sp"].append((waits, None, None))
        nc = self.nc
        emap = {"pe": "tensor", "act": "scalar", "dve": "vector", "pool": "gpsimd", "sp": "sync"}
        with nc.Block() as block:
            for e in self.ENG:
                ops = self.ops[e]
                sems = self.sems

                def body(engine, ops=ops):
                    for waits, fn, inc in ops:
                        for k, v in waits:
                            engine.wait_ge(sems[k], v)
                            self.nwaits += 1
                        if fn is None:
                            continue
                        ins = fn(engine)
                        if inc is not None:
                            ins.then_inc(sems[inc[0]], inc[1])
                getattr(block, emap[e])(body)

D = 2048
T = 3072
TS = 2048
NPR = 4
LP = 256
DI = 4096
DFF = 5632
EPS = 1e-6
TPAD = T + 12


class Buf:
    def __init__(self, h, name):
        self.h = h
        self.r = Res(name)

    def __getitem__(self, k):
        return self.h[k]


class StopBuild(Exception):
    pass


class Phase:
    stop_after = None
    stopped = False

    def __init__(self, S, name):
        self.S = S
        self.nc = S.nc
        self.name = name

    def __enter__(self):
        if Phase.stopped:
            raise StopBuild()
        self.es = ExitStack()
        return self

    def sb(self, name, shape, dt=F32):
        h = self.es.enter_context(self.nc.sbuf_tensor(f"{self.name}_{name}", shape, dt))
        return Buf(h, name)

    def ps(self, name, shape, dt=F32):
        full = [128, 512] if dt == F32 else [128, 1024]
        h = self.es.enter_context(self.nc.psum_tensor(f"{self.name}_{name}", full, dt))
        n = 1
        for d_ in shape[1:]:
            n *= d_
        assert n <= full[1]
        v = h[0:shape[0], 0:n]
        if len(shape) == 3:
            v = v.rearrange("p (a b) -> p a b", b=shape[2])
        b = Buf(v, name)
        b.r = PRes(name)
        return b

    def __exit__(self, *a):
        if KSKIP == 'mem':
            print("phase", self.name, "sbuf bytes remaining", self.nc.sbuf_bytes_remaining)
        if a[0] is None:
            self.S.barrier()
            if Phase.stop_after == self.name:
                Phase.stopped = True
        self.es.close()
        return False


def build_program(stop_after=None, debug=False):
    Phase.stop_after = stop_after
    Phase.stopped = False
    nc = bass.Bass("TRN2", target_bir_lowering=False)
    S = Sched(nc)

    def din(name, shape):
        return nc.dram_tensor(name, shape, F32, kind="ExternalInput").ap()

    def dout(name, shape):
        return nc.dram_tensor(name, shape, F32, kind="ExternalOutput").ap()

    def scr(name, shape, dt):
        if debug:
            return nc.dram_tensor(name, shape, dt, kind="ExternalOutput").ap()
        return nc.dram_tensor(name, shape, dt).ap()

    xs_in = din("xs", [TS, D]); xp_in = din("xp", [NPR * LP, D])
    sf_in = din("sf", [DI, 128]); sb_in = din("sbw", [DI, 128])
    ck_in = din("ck", [256, 512]); cv_in = din("cv", [256, 512])
    c_in = din("c", [1, D]); cctx_in = din("cctx", [1, D])
    w_mod = din("w_mod", [2, D, 6 * D]); b_mod = din("b_mod", [2, 6 * D])
    norm_mix = din("norm_mix", [2, D]); norm_ffn = din("norm_ffn", [2, D])
    ssd_w_in = din("ssd_w_in", [1, D, 10368]); ssd_conv_w = din("ssd_conv_w", [1, 5, 6144])
    ssd_conv_b = din("ssd_conv_b", [1, 6144]); ssd_a_log = din("ssd_a_log", [1, 128])
    ssd_dt_bias = din("ssd_dt_bias", [1, 128]); ssd_d = din("ssd_d", [1, 128])
    ssd_norm = din("ssd_norm", [1, DI]); ssd_w_out = din("ssd_w_out", [1, DI, D])
    attn_w_qkv = din("attn_w_qkv", [1, D, 3072]); attn_q_norm = din("attn_q_norm", [1, 128])
    attn_k_norm = din("attn_k_norm", [1, 128]); attn_w_o = din("attn_w_o", [1, D, D])
    ffn_w_gu = din("ffn_w_gu", [2, D, 2 * DFF]); ffn_w_down = din("ffn_w_down", [2, DFF, D])
    final_norm = din("final_norm", [1, D])
    rope_cos = din("rope_cos", [128, TS]); rope_sin = din("rope_sin", [128, TS]); rope_pm = din("rope_pm", [128, 128])
    yp_out = dout("yp", [NPR * LP, D]); ys_out = dout("ys", [TS, D])
    nf_out = dout("nf", [NPR, DI, 128]); nb_out = dout("nb", [NPR, DI, 128])
    nk_out = dout("nk", [NPR * LP, 512]); nv_out = dout("nv", [NPR * LP, 512])
    X_fm = scr("X_fm", [D, T], F32)
    H_fm = scr("H_fm", [D, T], BF16)
    SZ_tm = scr("SZ_tm", [T, DI], F32)
    U_fm = scr("U_fm", [6144, TPAD], BF16)
    XC_cm = scr("XC_cm", [T // 128, 128, 48, 128], BF16)
    DT_tm = scr("DT_tm", [T, 128], F32)
    ACS_tm = scr("ACS_tm", [T, 128], F32)
    WGT_tm = scr("WGT_tm", [T, 128], F32)
    ETOT = scr("ETOT", [T, 128], F32)
    ACS_cm = scr("ACS_cm", [T // 128, 2, 64, 128], F32)
    Y0_tm = scr("Y0_tm", [T, DI], F32)
    Y_fm = scr("Y_fm", [DI, T], BF16)
    A_fm = scr("A_fm", [DFF, T], BF16)
    QT_fm = scr("QT_fm", [D, T], BF16)
    KT_fm = scr("KT_fm", [512, T], BF16)
    V_tm = scr("V_tm", [T, 512], BF16)
    AO_fm = scr("AO_fm", [D, T], BF16)

    def dbg(name, buf, shape, dt):
        if not debug:
            return
        t = nc.dram_tensor("dbg_" + name, shape, dt, kind="ExternalOutput").ap()
        S.dma("sp", "dbg_" + name, t, buf[:], reads=[buf.r])

    def gbuf(name, shape, dt=F32):
        return Buf(nc.alloc_sbuf_tensor("g_" + name, shape, dt), name)

    ident_bf = gbuf("ident_bf", [128, 128], BF16); ident_f = gbuf("ident_f", [128, 128])
    ones_bf = gbuf("ones_bf", [128, 128], BF16); ones_f = gbuf("ones_f", [128, 128])
    tri_f = gbuf("tri_f", [128, 128]); tri_b = gbuf("tri_b", [128, 128])
    tri_f_bf = gbuf("tri_f_bf", [128, 128], BF16); tri_b_bf = gbuf("tri_b_bf", [128, 128], BF16)
    modv = gbuf("modv", [128, 2, 96, 2])
    gmix = gbuf("gmix", [128, 2, 16]); gffn = gbuf("gffn", [128, 2, 16]); gfin = gbuf("gfin", [128, 16])
    gs = gbuf("gs", [128, 2, 2, 2, 16])
    cw = gbuf("cw", [128, 6, 48])
    qkg = gbuf("qkg", [128, 2])

    def mk_const():
        def msel(buf, pattern, cm, cmp, val=1.0):
            S.op("pool", lambda e: e.memset(buf[:], val), writes=[buf.r])
            S.op("pool", lambda e: e.affine_select(out=buf[:], in_=buf[:], pattern=pattern, compare_op=cmp,
                                                   fill=0.0, base=0, channel_multiplier=cm),
                 reads=[buf.r], writes=[buf.r])
        msel(ident_bf, [[-1, 128]], 1, ALU.is_equal); msel(ident_f, [[-1, 128]], 1, ALU.is_equal)
        msel(tri_f, [[1, 128]], -1, ALU.is_ge); msel(tri_f_bf, [[1, 128]], -1, ALU.is_ge)
        msel(tri_b, [[-1, 128]], 1, ALU.is_ge); msel(tri_b_bf, [[-1, 128]], 1, ALU.is_ge)
        S.op("pool", lambda e: e.memset(ones_bf[:], 1.0), writes=[ones_bf.r])
        S.op("pool", lambda e: e.memset(ones_f[:], 1.0), writes=[ones_f.r])

    def tok_rows(t0, n):
        if t0 < TS:
            return ("s", t0)
        return ("p", t0 - TS)

    with Phase(S, "su") as ph:
        mk_const()
        S.barrier()
        rowb = ph.sb("rowb", [1, 6144]); pc = ph.ps("pc", [128, 512])
        one11 = ones_f

        def row2col(src_row_ap, n, dst_ap, func=None):
            S.dma("sp", "ld_rowb", rowb[0:1, 0:n * 128], src_row_ap, writes=[rowb.r])
            for j in range(n):
                S.op("pe", lambda e, j=j: e.matmul(pc[:, j:j + 1], lhsT=rowb[0:1, j * 128:(j + 1) * 128],
                                                  rhs=one11[0:1, 0:1], start=True, stop=True),
                     reads=[rowb.r, one11.r], writes=[pc.r], ms=(j == n - 1))
            if func is None:
                S.op("dve", lambda e: e.tensor_copy(out=dst_ap, in_=pc[:, 0:n]), reads=[pc.r], writes=[])
            else:
                S.op("act", lambda e: e.activation(out=dst_ap, in_=pc[:, 0:n], func=func), reads=[pc.r], writes=[])

        for i in range(2):
            row2col(norm_mix[i:i + 1, :], 16, gmix[:, i, :])
            row2col(norm_ffn[i:i + 1, :], 16, gffn[:, i, :])
        row2col(final_norm[0:1, :], 16, gfin[:, :])
        for k in range(5):
            row2col(ssd_conv_w[0, k:k + 1, :], 48, cw[:, k, :])
        row2col(ssd_conv_b[0:1, :], 48, cw[:, 5, :])
        row2col(attn_q_norm[0:1, :], 1, qkg[:, 0:1])
        row2col(attn_k_norm[0:1, :], 1, qkg[:, 1:2])
        sc = ph.sb("sc", [128, 16, 2])
        row2col(c_in[0:1, :], 16, sc[:, :, 0], func=AF.Silu)
        row2col(cctx_in[0:1, :], 16, sc[:, :, 1], func=AF.Silu)
        S.barrier()
        sc_bf = ph.sb("sc_bf", [128, 16, 2], BF16)
        S.op("dve", lambda e: e.tensor_copy(out=sc_bf[:], in_=sc[:]), reads=[sc.r], writes=[sc_bf.r])
        wsl = [ph.sb(f"wm{j}", [128, 16, 512], BF16) for j in range(2)]
        wfl = [ph.sb(f"wf{j}", [128, 16, 512]) for j in range(2)]
        bsl = [ph.sb(f"bm{j}", [1, 512]) for j in range(4)]
        pm_ = ph.ps("pm", [128, 192])
        n = 0
        for i in range(2):
            for cb in range(24):
                bb = bsl[n % 4]
                src = w_mod[i, :, cb * 512:(cb + 1) * 512].rearrange("(kc p) n -> p kc n", p=128)
                if n % 2 == 0:
                    w = wsl[(n // 2) % 2]; rhs_sc = sc_bf
                    S.dma("pool", f"ld_wm{(n // 2) % 2}", w[:], src, writes=[w.r])
                else:
                    w = wfl[(n // 2) % 2]; rhs_sc = sc
                    S.dma("sp", f"ld_wf{(n // 2) % 2}", w[:], src, writes=[w.r])
                S.dma("sp", f"ld_bm{n % 4}", bb[0:1, :], b_mod[i:i + 1, cb * 512:(cb + 1) * 512], writes=[bb.r])
                for f in range(4):
                    m = cb * 4 + f
                    for kc in range(16):
                        S.op("pe", lambda e: e.matmul(pm_[:, m * 2:m * 2 + 2], lhsT=w[:, kc, f * 128:(f + 1) * 128],
                                                     rhs=rhs_sc[:, kc, :], start=(kc == 0), stop=False),
                             reads=[w.r, rhs_sc.r], writes=[pm_.r], ms=False)
                    S.op("pe", lambda e: e.matmul(pm_[:, m * 2:m * 2 + 2], lhsT=bb[0:1, f * 128:(f + 1) * 128],
                                                 rhs=ones_f[0:1, 0:2], start=False, stop=True),
                         reads=[bb.r, ones_f.r], writes=[pm_.r], ms=True)
                n += 1
            S.op("dve", lambda e: e.tensor_copy(out=modv[:, i, :, :], in_=pm_[:].rearrange("p (m c) -> p m c", c=2)),
                 reads=[pm_.r], writes=[modv.r])
        for i in range(2):
            for sub in range(2):
                g = gmix if sub == 0 else gffn
                for cnd in range(2):
                    S.op("dve", lambda e, i=i, sub=sub, cnd=cnd, g=g: e.scalar_tensor_tensor(
                        out=gs[:, i, sub, cnd, :], in0=modv[:, i, sub * 48 + 16:sub * 48 + 32, cnd], scalar=1.0,
                        in1=g[:, i, :], op0=ALU.add, op1=ALU.mult), reads=[modv.r], writes=[gs.r])

    def mod_col(i, kind, chunk, cnd):
        return modv[:, i, kind * 16 + chunk, cnd:cnd + 1]

    Xv = X_fm.rearrange("(kc p) t -> p kc t", p=128)
    Hv = H_fm.rearrange("(kc p) t -> p kc t", p=128)

    def convert_phase():
        with Phase(S, "cv") as ph:
            xin = [ph.sb(f"xin{j}", [128, D]) for j in range(2)]
            xf = [ph.sb(f"xf{j}", [128, 16, 128]) for j in range(2)]
            pp = [ph.ps(f"pp{j}", [128, 4, 128]) for j in range(8)]
            for tt in range(T // 128):
                a = xin[tt % 2]; o = xf[tt % 2]
                src = xs_in[tt * 128:(tt + 1) * 128, :] if tt < 16 else xp_in[(tt - 16) * 128:(tt - 15) * 128, :]
                S.dma("sp", f"ld_xin{tt % 2}", a[:], src, writes=[a.r])
                for q in range(4):
                    p_ = pp[(tt * 4 + q) % 8]
                    for j in range(4):
                        kc = q * 4 + j
                        S.op("pe", lambda e, p_=p_, a=a, kc=kc, j=j: e.transpose(out=p_[:, j, :], in_=a[:, kc * 128:(kc + 1) * 128], identity=ident_f[:]),
                             reads=[a.r], writes=[p_.r], ms=(j == 3))
                    eng = "dve" if q % 2 == 0 else "act"
                    if eng == "dve":
                        S.op("dve", lambda e, p_=p_, o=o, q=q: e.tensor_copy(out=o[:, q * 4:(q + 1) * 4, :], in_=p_[:]), reads=[p_.r], writes=[o.r])
                    else:
                        S.op("act", lambda e, p_=p_, o=o, q=q: e.copy(out=o[:, q * 4:(q + 1) * 4, :], in_=p_[:]), reads=[p_.r], writes=[o.r])
                S.dma("sp", f"st_xf{tt % 2}", Xv[:, :, tt * 128:(tt + 1) * 128], o[:], reads=[o.r])

    def norm_phase(tag, i, sub):
        with Phase(S, tag) as ph:
            NT = 256
            NB = T // NT
            xb = [ph.sb(f"x{j}", [128, 16, NT]) for j in range(2)]
            sq = ph.sb("sq", [128, 16, NT], BF16)
            t1b = [ph.sb(f"t1{j}", [128, 16, NT]) for j in range(2)]
            hb = [ph.sb(f"h{j}", [128, 16, NT], BF16) for j in range(2)]
            sd = ph.sb("sd", [128, NT]); R = ph.sb("R", [128, NT])
            pss = [ph.ps(f"ss{j}", [128, NT]) for j in range(2)]

            def stage_a(tb):
                x = xb[tb % 2]; ps_ = pss[tb % 2]; t1 = t1b[tb % 2]
                S.dma("sp", f"ld_nx{tb % 2}", x[:], Xv[:, :, tb * NT:(tb + 1) * NT], writes=[x.r])
                S.op("act", lambda e: e.activation(out=sq[:], in_=x[:], func=AF.Square), reads=[x.r], writes=[sq.r])
                for kc in range(16):
                    S.op("pe", lambda e: e.matmul(ps_[:], lhsT=ones_bf[:], rhs=sq[:, kc, :], start=(kc == 0), stop=(kc == 15)),
                         reads=[sq.r], writes=[ps_.r], ms=(kc == 15))
                S.op("act", lambda e: e.activation(out=sd[:], in_=ps_[:], func=AF.Ln, bias=EPS, scale=1.0 / D), reads=[ps_.r], writes=[sd.r])
                S.op("act", lambda e: e.activation(out=R[:], in_=sd[:], func=AF.Exp, scale=-0.5), reads=[sd.r], writes=[R.r])
                S.op("dve", lambda e: e.tensor_tensor(out=t1[:], in0=x[:], in1=R[:].unsqueeze(1).to_broadcast([128, 16, NT]), op=ALU.mult),
                     reads=[x.r, R.r], writes=[t1.r])

            def stage_b(tb):
                cnd = 0 if tb * NT < TS else 1
                h = hb[tb % 2]; t1 = t1b[tb % 2]
                for kc in range(16):
                    S.op("act", lambda e: e.activation(out=h[:, kc, :], in_=t1[:, kc, :], func=AF.Identity, scale=gs[:, i, sub, cnd, kc:kc + 1],
                                                       bias=modv[:, i, sub * 48 + kc, cnd:cnd + 1]),
                         reads=[t1.r], writes=[h.r], ms=(kc == 15))
                S.dma("sp", f"st_nh{tb % 2}", Hv[:, :, tb * NT:(tb + 1) * NT], h[:], reads=[h.r])

            stage_a(0)
            for tb in range(NB):
                if tb + 1 < NB:
                    stage_a(tb + 1)
                stage_b(tb)

    def gemm(tag, A_scr, KC, blocks, wcols, alloc_extra):
        Av = A_scr.rearrange("(kc p) t -> p kc t", p=128)
        with Phase(S, tag) as ph:
            asl = [ph.sb(f"a{j}", [128, KC, 512], BF16) for j in range(2)]
            wsl = [ph.sb(f"w{j}", [128, KC, wcols], BF16) for j in range(2)]
            wres = [[Res(f"w{j}p{q}") for q in range(4)] for j in range(2)]
            psl = [ph.ps(f"ps{j}", [128, 512]) for j in range(6)]
            ctx = alloc_extra(ph) if alloc_extra else None
            pcount = [0]
            pending = []

            def next_ps():
                p_ = psl[pcount[0] % 6]
                pcount[0] += 1
                return p_

            def load_w(bi):
                blk = blocks[bi]; w = wsl[bi % 2]
                off = 0
                for q, part in enumerate(blk["parts"]):
                    n = part.shape[1]
                    S.dma("pool", f"ld_w{bi % 2}_{q}", w[:, :, off:off + n], part.rearrange("(kc p) n -> p kc n", p=128),
                          writes=[wres[bi % 2][q]])
                    off += n

            na = [0]

            def load_a(tb):
                a = asl[na[0] % 2]
                S.dma("sp", f"ld_a{na[0] % 2}", a[:], Av[:, :, tb * 512:(tb + 1) * 512], writes=[a.r])
                na[0] += 1
                return a

            def run_pending():
                nxt = []
                for c in pending:
                    r = c()
                    if r is not None:
                        nxt.append(r)
                pending[:] = nxt

            load_w(0)
            NTB = T // 512
            a_next = load_a(0)
            for bi, blk in enumerate(blocks):
                w = wsl[bi % 2]
                wr = wres[bi % 2][:len(blk["parts"])]
                if bi + 1 < len(blocks):
                    load_w(bi + 1)
                ncols = sum(p.shape[1] for p in blk["parts"])
                for tb in range(NTB):
                    a = a_next
                    if tb + 1 < NTB or bi + 1 < len(blocks):
                        a_next = load_a((tb + 1) % NTB)
                    mode = blk["mode"]
                    if mode == "FM":
                        for f in range(ncols // 128):
                            p_ = next_ps()
                            for kc in range(KC):
                                S.op("pe", lambda e, p_=p_, w=w, a=a, kc=kc, f=f: e.matmul(p_[:], lhsT=w[:, kc, f * 128:(f + 1) * 128], rhs=a[:, kc, :],
                                                                                        start=(kc == 0), stop=(kc == KC - 1)),
                                     reads=[a.r] + wr, writes=[p_.r], ms=(kc == KC - 1))
                            run_pending()
                            r = blk["epi"](ctx, blk, f, tb, [p_])
                            if r is not None:
                                pending.append(r)
                    elif mode == "PAIR":
                        half = ncols // 2
                        for f in range(half // 128):
                            pp_ = []
                            for hh in range(2):
                                p_ = next_ps()
                                c0 = hh * half + f * 128
                                for kc in range(KC):
                                    S.op("pe", lambda e, p_=p_, w=w, a=a, kc=kc, c0=c0: e.matmul(p_[:], lhsT=w[:, kc, c0:c0 + 128], rhs=a[:, kc, :],
                                                                                              start=(kc == 0), stop=(kc == KC - 1)),
                                         reads=[a.r] + wr, writes=[p_.r], ms=(kc == KC - 1))
                                pp_.append(p_)
                            run_pending()
                            r = blk["epi"](ctx, blk, f, tb, pp_)
                            if r is not None:
                                pending.append(r)
                    else:
                        for t in range(4):
                            p_ = next_ps()
                            for kc in range(KC):
                                S.op("pe", lambda e, p_=p_, w=w, a=a, kc=kc, t=t, ncols=ncols: e.matmul(p_[:, 0:ncols], lhsT=a[:, kc, t * 128:(t + 1) * 128], rhs=w[:, kc, 0:ncols],
                                                                                                     start=(kc == 0), stop=(kc == KC - 1)),
                                     reads=[a.r] + wr, writes=[p_.r], ms=(kc == KC - 1))
                            run_pending()
                            r = blk["epi"](ctx, blk, t, tb, [p_])
                            if r is not None:
                                pending.append(r)
            while pending:
                run_pending()

    def resid_alloc(ph):
        return {"xt": [ph.sb(f"xt{j}", [128, 512]) for j in range(3)], "n": [0]}

    def make_resid_epi(i, kind):
        def epi(ctx, blk, f, tb, ps):
            fc = blk["c0"] // 128 + f
            cnd = 0 if tb < 4 else 1
            xt = ctx["xt"][ctx["n"][0] % 3]; k = ctx["n"][0] % 3
            ctx["n"][0] += 1
            dst = X_fm[fc * 128:(fc + 1) * 128, tb * 512:(tb + 1) * 512]
            S.dma("sp", f"ld_xt{k}", xt[:], dst, writes=[xt.r])
            S.op("dve", lambda e: e.scalar_tensor_tensor(out=xt[:], in0=ps[0][:], scalar=mod_col(i, kind, fc, cnd), in1=xt[:], op0=ALU.mult, op1=ALU.add),
                 reads=[ps[0].r, xt.r], writes=[xt.r])
            S.dma("sp", f"st_xt{k}", dst, xt[:], reads=[xt.r])
            return None
        return epi

    def wblocks(W, c_lo, c_hi, step, mode, epi):
        out = []
        for c0 in range(c_lo, c_hi, step):
            n = min(step, c_hi - c0)
            out.append({"parts": [W[:, c0:c0 + n]], "mode": mode, "epi": epi, "c0": c0 - c_lo})
        return out

    def inproj_alloc(ph):
        ctx = {"ot": [ph.sb(f"ot{j}", [128, 512]) for j in range(3)], "otb": [ph.sb(f"otb{j}", [128, 512], BF16) for j in range(3)], "n": [0],
               "dtb": ph.sb("dtb", [128, 128]), "t": ph.sb("tt", [128, 128])}
        S.dma("sp", "ld_dtb", ctx["dtb"][:], ssd_dt_bias[0, :].partition_broadcast(128), writes=[ctx["dtb"].r])
        return ctx

    def z_epi(ctx, blk, t, tb, ps):
        k = ctx["n"][0] % 3; ot = ctx["ot"][k]; ctx["n"][0] += 1
        S.op("act", lambda e: e.activation(out=ot[:], in_=ps[0][:], func=AF.Silu), reads=[ps[0].r], writes=[ot.r])
        r0 = tb * 512 + t * 128
        S.dma("sp", f"st_ot{k}", SZ_tm[r0:r0 + 128, blk["c0"]:blk["c0"] + 512], ot[:], reads=[ot.r])

    def u_epi(ctx, blk, f, tb, ps):
        k = ctx["n"][0] % 3; ot = ctx["otb"][k]; ctx["n"][0] += 1
        if ctx["n"][0] % 2 == 0:
            S.op("dve", lambda e: e.tensor_copy(out=ot[:], in_=ps[0][:]), reads=[ps[0].r], writes=[ot.r])
        else:
            S.op("act", lambda e: e.copy(out=ot[:], in_=ps[0][:]), reads=[ps[0].r], writes=[ot.r])
        r0 = blk["c0"] + f * 128
        if tb < 4:
            S.dma("sp", f"st_otb{k}", U_fm[r0:r0 + 128, 2 + tb * 512:2 + (tb + 1) * 512], ot[:], reads=[ot.r])
        else:
            c0 = 2052 + (tb - 4) * 516
            S.dma("sp", f"st_otb{k}", U_fm[r0:r0 + 128, c0:c0 + 516].rearrange("p (j i) -> p j i", i=258)[:, :, 0:256],
                  ot[:].rearrange("p (j i) -> p j i", i=256), reads=[ot.r])

    def dt_epi(ctx, blk, t, tb, ps):
        k = ctx["n"][0] % 3; ot = ctx["ot"][k]; ctx["n"][0] += 1
        tt_ = ctx["t"]
        S.op("dve", lambda e: e.tensor_tensor(out=tt_[:], in0=ps[0][:, 0:128], in1=ctx["dtb"][:], op=ALU.add), reads=[ps[0].r, ctx["dtb"].r], writes=[tt_.r])
        S.op("act", lambda e: e.activation(out=tt_[:], in_=tt_[:], func=AF.Exp), reads=[tt_.r], writes=[tt_.r])
        S.op("act", lambda e: e.activation(out=ot[:, 0:128], in_=tt_[:], func=AF.Ln, bias=1.0), reads=[tt_.r], writes=[ot.r])
        r0 = tb * 512 + t * 128
        S.dma("sp", f"st_ot{k}", DT_tm[r0:r0 + 128, :], ot[:, 0:128], reads=[ot.r])

    def ssd_layer(i):
        norm_phase("n0m", i, 0)
        Win = ssd_w_in[0]
        blocks = (wblocks(Win[:, 0:4096], 0, 4096, 512, "TM", z_epi)
                  + wblocks(Win[:, 4096:10240], 0, 6144, 512, "FM", u_epi)
                  + wblocks(Win[:, 10240:10368], 0, 128, 128, "TM", dt_epi))
        gemm("inp", H_fm, 16, blocks, 512, inproj_alloc)

        with Phase(S, "cnv") as ph:
            ub = [ph.sb(f"u{j}", [128, TPAD], BF16) for j in range(3)]
            xq = [ph.sb(f"xq{j}", [128, T // 128, 4, 128], BF16) for j in range(2)]
            dg = [ph.sb(f"dg{j}", [128, 5, 128], BF16) for j in range(2)]
            pcv = [ph.ps(f"pcv{j}", [128, 512]) for j in range(4)]
            npc = 0
            for cc in range(48):
                u = ub[cc % 3]; x4 = xq[(cc // 4) % 2]; dgc = dg[cc % 2]; qi = cc % 4
                S.dma("sp", f"ld_u{cc % 3}", u[:], U_fm[cc * 128:(cc + 1) * 128, :], writes=[u.r])
                S.op("dve", lambda e: e.memset(u[:, 0:2], 0.0), writes=[u.r])
                S.op("dve", lambda e: e.memset(u[:, 2050:2050 + 4 * 258].rearrange("p (q i) -> p q i", i=258)[:, :, 0:2], 0.0), writes=[u.r])
                S.op("dve", lambda e: e.memset(u[:, TPAD - 2:TPAD], 0.0), writes=[u.r])
                for k in range(5):
                    S.op("dve", lambda e: e.tensor_scalar(out=dgc[:, k, :], in0=ident_bf[:], scalar1=cw[:, k, cc:cc + 1], scalar2=None, op0=ALU.mult), writes=[dgc.r])
                blks = [(jb * 512, 512, jb * 4) for jb in range(4)] + [(2050 + 258 * j, 256, 16 + 2 * j) for j in range(NPR)]
                for (u0, n, c0) in blks:
                    p_ = pcv[npc % 4]; npc += 1
                    for k in range(5):
                        S.op("pe", lambda e: e.matmul(p_[:, 0:n], lhsT=dgc[:, k, :], rhs=u[:, u0 + k:u0 + k + n], start=(k == 0), stop=(k == 4)),
                             reads=[dgc.r, u.r], writes=[p_.r], ms=(k == 4))
                    S.op("act", lambda e: e.activation(out=x4[:, c0:c0 + n // 128, qi, :], in_=p_[:, 0:n].rearrange("p (c t) -> p c t", t=128), func=AF.Silu,
                                                       bias=cw[:, 5, cc:cc + 1], scale=1.0), reads=[p_.r], writes=[x4.r])
                if qi == 3:
                    c4 = cc - 3
                    sk = (cc // 4) % 2
                    S.dma("sp", f"st_xq{sk}", XC_cm[:, :, c4:c4 + 4, :].rearrange("c p q t -> p c (q t)"),
                          x4[:].rearrange("p c q t -> p c (q t)"), reads=[x4.r])

        with Phase(S, "cum") as ph:
            A_b = ph.sb("A_b", [128, 128])
            S.dma("sp", "ld_Ab", A_b[:], ssd_a_log[0, :].partition_broadcast(128), writes=[A_b.r])
            S.op("act", lambda e: e.activation(out=A_b[:], in_=A_b[:], func=AF.Exp), reads=[A_b.r], writes=[A_b.r])
            S.op("dve", lambda e: e.tensor_scalar(out=A_b[:], in0=A_b[:], scalar1=-1.0, scalar2=None, op0=ALU.mult), reads=[A_b.r], writes=[A_b.r])
            dts = [ph.sb(f"dt{j}", [128, 128]) for j in range(2)]
            dta = [ph.sb(f"dta{j}", [128, 128]) for j in range(2)]
            acs = [ph.sb(f"acs{j}", [128, 128]) for j in range(2)]
            acf = [ph.sb(f"acf{j}", [64, 2, 128]) for j in range(2)]
            wg = [ph.sb(f"wg{j}", [128, 128]) for j in range(2)]
            et = [ph.sb(f"et{j}", [128, 128]) for j in range(2)]
            p_acs = [ph.ps(f"pacs{j}", [128, 128]) for j in range(2)]
            p_tot = [ph.ps(f"ptot{j}", [128, 128]) for j in range(2)]
            p_acf = [ph.ps(f"pacf{j}", [64, 2, 128]) for j in range(2)]
            tris = [tri_f, tri_b]
            for c in range(T // 128):
                j = c % 2
                r0 = c * 128
                S.dma("sp", f"ld_dt{j}", dts[j][:], DT_tm[r0:r0 + 128, :], writes=[dts[j].r])
                S.op("dve", lambda e, j=j: e.tensor_tensor(out=dta[j][:], in0=dts[j][:], in1=A_b[:], op=ALU.mult), reads=[dts[j].r, A_b.r], writes=[dta[j].r])
                for d in range(2):
                    S.op("pe", lambda e, j=j, d=d: e.matmul(p_acs[j][:, d * 64:(d + 1) * 64], lhsT=tris[d][:], rhs=dta[j][:, d * 64:(d + 1) * 64], start=True, stop=True),
                         reads=[dta[j].r], writes=[p_acs[j].r], ms=(d == 1))
                for d in range(2):
                    S.op("pe", lambda e, j=j, d=d: e.matmul(p_tot[j][:, d * 64:(d + 1) * 64], lhsT=ones_f[:], rhs=dta[j][:, d * 64:(d + 1) * 64], start=True, stop=True),
                         reads=[dta[j].r], writes=[p_tot[j].r], ms=(d == 1))
                for d in range(2):
                    S.op("pe", lambda e, j=j, d=d: e.matmul(p_acf[j][:, d, :], lhsT=dta[j][:, d * 64:(d + 1) * 64], rhs=tris[d][:], start=True, stop=True),
                         reads=[dta[j].r], writes=[p_acf[j].r], ms=(d == 1))
                S.op("act", lambda e, j=j: e.copy(out=acs[j][:], in_=p_acs[j][:]), reads=[p_acs[j].r], writes=[acs[j].r])
                S.op("act", lambda e, j=j: e.copy(out=acf[j][:], in_=p_acf[j][:]), reads=[p_acf[j].r], writes=[acf[j].r])
                S.op("act", lambda e, j=j: e.activation(out=et[j][:], in_=p_tot[j][:], func=AF.Exp), reads=[p_tot[j].r], writes=[et[j].r])
                S.op("dve", lambda e, j=j: e.tensor_tensor(out=wg[j][:], in0=p_tot[j][:], in1=acs[j][:], op=ALU.subtract), reads=[p_tot[j].r, acs[j].r], writes=[wg[j].r])
                S.op("act", lambda e, j=j: e.activation(out=wg[j][:], in_=wg[j][:], func=AF.Exp), reads=[wg[j].r], writes=[wg[j].r])
                S.op("dve", lambda e, j=j: e.tensor_tensor(out=wg[j][:], in0=wg[j][:], in1=dts[j][:], op=ALU.mult), reads=[wg[j].r, dts[j].r], writes=[wg[j].r])
                S.dma("sp", f"st_acs{j}", ACS_tm[r0:r0 + 128, :], acs[j][:], reads=[acs[j].r])
                S.dma("sp", f"st_acf{j}", ACS_cm[c].rearrange("d h t -> h d t"), acf[j][:], reads=[acf[j].r])
                S.dma("sp", f"st_wg{j}", WGT_tm[r0:r0 + 128, :], wg[j][:], reads=[wg[j].r])
                S.dma("sp", f"st_et{j}", ETOT[r0:r0 + 128, :], et[j][:], reads=[et[j].r])

        seqs = [(0, 16, None)] + [(16 + 2 * j, 2, j) for j in range(NPR)]
        Yv = Y_fm.rearrange("(cc p) t -> p cc t", p=128)
        for d in range(2):
            with Phase(S, f"ssd{d}") as ph:
                xfm = ph.sb("xfm", [128, 32, 128], BF16)
                xT2 = [ph.sb(f"xT{j}", [128, DI], BF16) for j in range(2)]
                bcm2 = [ph.sb(f"bcm{j}", [128, 16, 128], BF16) for j in range(2)]
                BT2 = [ph.sb(f"BT{j}", [128, 1024], BF16) for j in range(2)]
                cbTm2 = [ph.sb(f"cbTm{j}", [128, 8, 128], BF16) for j in range(2)]
                xw2 = [ph.sb(f"xw{j}", [128, DI], BF16) for j in range(2)]
                sm2 = [{k: ph.sb(f"{k}{j}", [128, 128]) for k in ("dt", "acs", "wgt", "etot", "nb", "eacs", "lndt")} for j in range(2)]
                bcs = [ph.sb(f"bcs{j}", [128, 8, 128]) for j in range(3)]
                Eb_ = [ph.sb(f"E{j}", [128, 8, 128], BF16) for j in range(2)]
                MT = [ph.sb(f"MT{j}", [128, 8, 128], BF16) for j in range(2)]
                tmp = [ph.sb(f"tmp{j}", [128, 512]) for j in range(2)]
                yg = [ph.sb(f"yg{j}", [128, 512]) for j in range(3)]
                ST = ph.sb("ST", [128, DI]); STb2 = [ph.sb(f"STb{j}", [128, DI], BF16) for j in range(2)]
                y0 = ph.sb("y0", [128, DI])
                if d == 0:
                    dskb = ph.sb("dskb", [128, 128])
                    Dg = ph.sb("Dg", [128, 64, 128], BF16)
                    S.dma("sp", "ld_dsk", dskb[:], ssd_d[0, :].partition_broadcast(128), writes=[dskb.r])
                    S.op("dve", lambda e: e.tensor_tensor(out=dskb[:, 0:64], in0=dskb[:, 0:64], in1=dskb[:, 64:128], op=ALU.add), reads=[dskb.r], writes=[dskb.r])
                    for h in range(64):
                        S.op("dve", lambda e: e.tensor_scalar(out=Dg[:, h, :], in0=ident_bf[:], scalar1=dskb[:, h:h + 1], scalar2=None, op0=ALU.mult), reads=[dskb.r], writes=[Dg.r])
                else:
                    gb = ph.sb("gb", [128, DI], BF16)
                    szg = [ph.sb(f"szg{j}", [128, 512]) for j in range(3)]
                    yn2 = [ph.sb(f"yn{j}", [128, DI], BF16) for j in range(2)]
                    ynT = ph.sb("ynT", [128, 32, 128], BF16)
                    ss = ph.sb("ss", [128, 8]); sd = ph.sb("sd", [128, 8]); rs = ph.sb("rs", [128, 8])
                    junk = ph.sb("junk", [128, 512], BF16)
                    S.dma("pool", "ld_gb", gb[:], ssd_norm[0, :].partition_broadcast(128), writes=[gb.r])
                ptr = [ph.ps(f"ptr{j}", [128, 8, 128], BF16) for j in range(2)]
                pcb = [ph.ps(f"pcb{j}", [128, 4, 128]) for j in range(2)]
                py = [ph.ps(f"py{j}", [128, 512]) for j in range(2)]
                pyi2 = [ph.ps(f"pyi{j}", [128, 512]) for j in range(2)]
                trn = [0]
                mask = tri_f_bf if d == 0 else tri_b_bf
                st_in = sf_in if d == 0 else sb_in
                st_out = nf_out if d == 0 else nb_out
                cnt_ = {"g": 0, "st": 0}

                def evac_copy(dst_ap, p_, dst_res, k):
                    if k % 2 == 0:
                        S.op("dve", lambda e: e.tensor_copy(out=dst_ap, in_=p_[:]), reads=[p_.r], writes=[dst_res])
                    else:
                        S.op("act", lambda e: e.copy(out=dst_ap, in_=p_[:]), reads=[p_.r], writes=[dst_res])

                chunks = []
                for (c_first, nch, pj) in seqs:
                    order = list(range(c_first, c_first + nch))
                    if d == 1:
                        order = order[::-1]
                    for ci, c in enumerate(order):
                        chunks.append({"c": c, "first": ci == 0, "last": ci == nch - 1, "pj": pj,
                                       "need_state": (pj is not None) or (ci < nch - 1)})
                for n_, ck in enumerate(chunks):
                    ck["cs"] = n_ % 2

                def seq_init(ck):
                    STb = STb2[cnt_["st"] % 2]
                    if ck["pj"] is None:
                        y0v = y0[:].rearrange("p (k n) -> p k n", n=128)
                        S.dma("sp", "ld_y0", y0v, st_in.rearrange("(k p) n -> p k n", p=128), writes=[y0.r])
                        for q in range(8):
                            p_ = pcb[q % 2]
                            for j in range(4):
                                k = q * 4 + j
                                S.op("pe", lambda e: e.transpose(out=p_[:, j, :], in_=y0[:, k * 128:(k + 1) * 128], identity=ident_f[:]),
                                     reads=[y0.r], writes=[p_.r], ms=(j == 3))
                            S.op("dve", lambda e: e.tensor_copy(out=ST[:, q * 512:(q + 1) * 512].rearrange("p (j d) -> p j d", d=128), in_=p_[:]), reads=[p_.r], writes=[ST.r])
                        S.op("act", lambda e: e.copy(out=STb[:], in_=ST[:]), reads=[ST.r], writes=[STb.r])
                    else:
                        S.op("dve", lambda e: e.memset(ST[:], 0.0), writes=[ST.r])
                        S.op("dve", lambda e: e.memset(STb[:], 0.0), writes=[STb.r])

                def seq_final(ck):
                    pj = ck["pj"]
                    for q in range(8):
                        p_ = pcb[q % 2]
                        for j in range(4):
                            k = q * 4 + j
                            S.op("pe", lambda e: e.transpose(out=p_[:, j, :], in_=ST[:, k * 128:(k + 1) * 128], identity=ident_f[:]),
                                 reads=[ST.r], writes=[p_.r], ms=(j == 3))
                        S.op("dve", lambda e: e.tensor_copy(out=y0[:, q * 512:(q + 1) * 512].rearrange("p (j d) -> p j d", d=128), in_=p_[:]), reads=[p_.r], writes=[y0.r])
                    S.dma("sp", "st_y0", st_out[pj].rearrange("(k p) n -> p k n", p=128), y0[:].rearrange("p (k n) -> p k n", n=128), reads=[y0.r])

                def preamble(ck):
                    c = ck["c"]; cs = ck["cs"]; r0 = c * 128
                    bcm = bcm2[cs]; BT = BT2[cs]; cbTm = cbTm2[cs]; xw = xw2[cs]; sm = sm2[cs]; xT = xT2[cs]
                    S.dma("sp", "ld_xfm", xfm[:], XC_cm[c, :, 0:32, :], writes=[xfm.r])
                    S.dma("sp", f"ld_bcm{cs}", bcm[:], XC_cm[c, :, 32:48, :], writes=[bcm.r])
                    for k, src in (("dt", DT_tm), ("acs", ACS_tm), ("wgt", WGT_tm), ("etot", ETOT)):
                        S.dma("sp", f"ld_{k}{cs}", sm[k][:], src[r0:r0 + 128, :], writes=[sm[k].r])
                    S.op("act", lambda e: e.activation(out=sm["lndt"][:], in_=sm["dt"][:], func=AF.Ln), reads=[sm["dt"].r], writes=[sm["lndt"].r])
                    S.op("dve", lambda e: e.tensor_tensor(out=sm["nb"][:], in0=sm["lndt"][:], in1=sm["acs"][:], op=ALU.subtract), reads=[sm["lndt"].r, sm["acs"].r], writes=[sm["nb"].r])
                    S.op("act", lambda e: e.activation(out=sm["eacs"][:], in_=sm["acs"][:], func=AF.Exp), reads=[sm["acs"].r], writes=[sm["eacs"].r])
                    for q in range(4):
                        p_ = ptr[trn[0] % 2]; trn[0] += 1
                        for j in range(8):
                            cc = q * 8 + j
                            S.op("pe", lambda e: e.transpose(out=p_[:, j, :], in_=xfm[:, cc, :], identity=ident_bf[:]),
                                 reads=[xfm.r], writes=[p_.r], ms=(j == 7))
                        evac_copy(xT[:, q * 1024:(q + 1) * 1024].rearrange("p (j d) -> p j d", d=128), p_, xT.r, q)
                    p_ = ptr[trn[0] % 2]; trn[0] += 1
                    for g in range(8):
                        S.op("pe", lambda e: e.transpose(out=p_[:, g, :], in_=bcm[:, g, :], identity=ident_bf[:]),
                             reads=[bcm.r], writes=[p_.r], ms=(g == 7))
                    evac_copy(BT[:].rearrange("p (j d) -> p j d", d=128), p_, BT.r, 1)
                    for q in range(2):
                        for j in range(4):
                            g = q * 4 + j
                            S.op("pe", lambda e: e.matmul(pcb[q][:, j, :], lhsT=bcm[:, g, :], rhs=bcm[:, 8 + g, :], start=True, stop=True),
                                 reads=[bcm.r], writes=[pcb[q].r], ms=(j == 3))
                        S.op("dve", lambda e: e.tensor_tensor(out=cbTm[:, q * 4:(q + 1) * 4, :], in0=pcb[q][:], in1=mask[:].unsqueeze(1).to_broadcast([128, 4, 128]), op=ALU.mult),
                             reads=[pcb[q].r], writes=[cbTm.r])
                    if ck["need_state"]:
                        S.op("dve", lambda e: e.tensor_tensor(out=xw[:].rearrange("p (h q) -> p h q", q=64), in0=xT[:].rearrange("p (h q) -> p h q", q=64),
                                                              in1=sm["wgt"][:, d * 64:(d + 1) * 64].unsqueeze(2).to_broadcast([128, 64, 64]), op=ALU.mult),
                             reads=[xT.r, sm["wgt"].r], writes=[xw.r])

                def state_update(ck):
                    cs = ck["cs"]
                    BT = BT2[cs]; xw = xw2[cs]; sm = sm2[cs]
                    if ck["need_state"]:
                        STn = STb2[(cnt_["st"] + 1) % 2]
                        S.op("dve", lambda e: e.tensor_tensor(out=ST[:].rearrange("p (h q) -> p h q", q=64), in0=ST[:].rearrange("p (h q) -> p h q", q=64),
                                                              in1=sm["etot"][:, d * 64:(d + 1) * 64].unsqueeze(2).to_broadcast([128, 64, 64]), op=ALU.mult),
                             reads=[ST.r, sm["etot"].r], writes=[ST.r])
                        for g in range(8):
                            gc0 = g * 512
                            pst = pcb[g % 2]
                            pstv = pst[:].rearrange("p j d -> p (j d)")
                            S.op("pe", lambda e: e.matmul(pstv, lhsT=BT[:, g * 128:(g + 1) * 128], rhs=xw[:, gc0:gc0 + 512], start=True, stop=True),
                                 reads=[BT.r, xw.r], writes=[pst.r], ms=True)
                            S.op("dve", lambda e: e.tensor_tensor(out=ST[:, gc0:gc0 + 512], in0=pstv, in1=ST[:, gc0:gc0 + 512], op=ALU.add), reads=[pst.r, ST.r], writes=[ST.r])
                        S.op("act", lambda e: e.copy(out=STn[:], in_=ST[:]), reads=[ST.r], writes=[STn.r])
                        cnt_["st"] += 1
                    if ck["last"]:
                        if ck["pj"] is not None:
                            seq_final(ck)
                        if not ck["need_state"]:
                            cnt_["st"] += 1

                def g_loads(it):
                    ck, g = it["ck"], it["g"]
                    c = ck["c"]; r0 = c * 128; gc0 = g * 512
                    k3 = it["i"] % 3
                    S.dma("sp", f"ld_bcs{k3}", bcs[k3][:], ACS_cm[c, d, g * 8:(g + 1) * 8, :].partition_broadcast(128), writes=[bcs[k3].r])
                    if d == 1:
                        S.dma("sp", f"ld_szg{k3}", szg[k3][:], SZ_tm[r0:r0 + 128, gc0:gc0 + 512], writes=[szg[k3].r])
                        S.dma("sp", f"ld_yg{k3}", yg[k3][:], Y0_tm[r0:r0 + 128, gc0:gc0 + 512], writes=[yg[k3].r])

                def g_exps(it):
                    ck, g = it["ck"], it["g"]
                    sm = sm2[ck["cs"]]; k = it["i"] % 2; k3 = it["i"] % 3
                    hc0 = d * 64 + g * 8
                    for h in range(8):
                        S.op("act", lambda e: e.activation(out=Eb_[k][:, h, :], in_=bcs[k3][:, h, :], func=AF.Exp, bias=sm["nb"][:, hc0 + h:hc0 + h + 1], scale=1.0),
                             reads=[bcs[k3].r, sm["nb"].r], writes=[Eb_[k].r], ms=(h == 7))

                def g_mt(it):
                    ck, g = it["ck"], it["g"]
                    cbTm = cbTm2[ck["cs"]]; k = it["i"] % 2
                    S.op("dve", lambda e: e.scalar_tensor_tensor(out=MT[k][:], in0=Eb_[k][:], scalar=1e30, in1=cbTm[:, g, :].unsqueeze(1).to_broadcast([128, 8, 128]), op0=ALU.min, op1=ALU.mult),
                         reads=[Eb_[k].r, cbTm.r], writes=[MT[k].r])

                def g_pe(it):
                    ck, g = it["ck"], it["g"]
                    cs = ck["cs"]; bcm = bcm2[cs]; xT = xT2[cs]; STb = ck["STb"]
                    k = it["i"] % 2
                    gc0 = g * 512
                    pyi = pyi2[k]; p_ = py[k]
                    S.op("pe", lambda e: e.matmul(pyi[:], lhsT=bcm[:, 8 + g, :], rhs=STb[:, gc0:gc0 + 512], start=True, stop=True), reads=[bcm.r, STb.r], writes=[pyi.r])
                    for h in range(8):
                        hh = g * 8 + h
                        S.op("pe", lambda e: e.matmul(p_[:, h * 64:(h + 1) * 64], lhsT=MT[k][:, h, :], rhs=xT[:, hh * 64:(hh + 1) * 64], start=True, stop=(d == 1)),
                             reads=[MT[k].r, xT.r], writes=[p_.r], ms=(d == 1 and h == 7))
                        if d == 0:
                            S.op("pe", lambda e: e.matmul(p_[:, h * 64:(h + 1) * 64], lhsT=Dg[:, hh, :], rhs=xT[:, hh * 64:(hh + 1) * 64], start=False, stop=True),
                                 reads=[xT.r, Dg.r], writes=[p_.r], ms=(h == 7))

                def g_evac(it):
                    ck, g = it["ck"], it["g"]
                    c = ck["c"]; cs = ck["cs"]; r0 = c * 128
                    sm = sm2[cs]
                    k = it["i"] % 2; k3 = it["i"] % 3
                    gc0 = g * 512; hc0 = d * 64 + g * 8
                    pyi = pyi2[k]; p_ = py[k]; tm = tmp[k]; y_ = yg[k3]
                    S.op("dve", lambda e: e.tensor_tensor(out=tm[:].rearrange("p (h q) -> p h q", q=64), in0=pyi[:].rearrange("p (h q) -> p h q", q=64),
                                                         in1=sm["eacs"][:, hc0:hc0 + 8].unsqueeze(2).to_broadcast([128, 8, 64]), op=ALU.mult),
                         reads=[pyi.r, sm["eacs"].r], writes=[tm.r])
                    if d == 0:
                        S.op("dve", lambda e: e.tensor_tensor(out=y_[:], in0=p_[:], in1=tm[:], op=ALU.add), reads=[p_.r, tm.r], writes=[y_.r])
                        S.dma("sp", f"st_yg{k3}", Y0_tm[r0:r0 + 128, gc0:gc0 + 512], y_[:], reads=[y_.r])
                    else:
                        sz_ = szg[k3]
                        S.op("dve", lambda e: e.tensor_tensor(out=tm[:], in0=p_[:], in1=tm[:], op=ALU.add), reads=[p_.r, tm.r], writes=[tm.r])
                        S.op("dve", lambda e: e.tensor_tensor(out=y_[:], in0=tm[:], in1=y_[:], op=ALU.add), reads=[tm.r, y_.r], writes=[y_.r])
                        S.op("dve", lambda e: e.tensor_tensor(out=sz_[:], in0=y_[:], in1=sz_[:], op=ALU.mult), reads=[y_.r, sz_.r], writes=[sz_.r])
                        S.op("act", lambda e: e.activation(out=junk[:], in_=sz_[:], func=AF.Square, accum_out=ss[:, g:g + 1]), reads=[sz_.r], writes=[junk.r, ss.r])
                        S.op("act", lambda e: e.activation(out=sd[:, g:g + 1], in_=ss[:, g:g + 1], func=AF.Ln, bias=EPS, scale=1.0 / 512), reads=[ss.r], writes=[sd.r])
                        S.op("act", lambda e: e.activation(out=rs[:, g:g + 1], in_=sd[:, g:g + 1], func=AF.Exp, scale=-0.5), reads=[sd.r], writes=[rs.r])

                def g_evac_b(it):
                    if d == 0:
                        return
                    ck, g = it["ck"], it["g"]
                    k3 = it["i"] % 3; gc0 = g * 512
                    sz_ = szg[k3]; yn = yn2[ck["cs"]]
                    S.op("dve", lambda e: e.scalar_tensor_tensor(out=yn[:, gc0:gc0 + 512], in0=sz_[:], scalar=rs[:, g:g + 1], in1=gb[:, gc0:gc0 + 512],
                                                              op0=ALU.mult, op1=ALU.mult), reads=[sz_.r, rs.r, gb.r], writes=[yn.r])

                def chunk_post(ck):
                    if d == 0:
                        return
                    c = ck["c"]; cs = ck["cs"]; r0 = c * 128
                    yn = yn2[cs]
                    for q in range(4):
                        p_ = ptr[trn[0] % 2]; trn[0] += 1
                        for j in range(8):
                            cc = q * 8 + j
                            S.op("pe", lambda e: e.transpose(out=p_[:, j, :], in_=yn[:, cc * 128:(cc + 1) * 128], identity=ident_bf[:]),
                                 reads=[yn.r], writes=[p_.r], ms=(j == 7))
                        evac_copy(ynT[:, q * 8:(q + 1) * 8, :], p_, ynT.r, q)
                    S.dma("sp", "st_ynT", Yv[:, :, r0:r0 + 128], ynT[:], reads=[ynT.r])

                items = []
                for ck in chunks:
                    for g in range(8):
                        items.append({"ck": ck, "g": g, "i": len(items)})
                NI = len(items)
                seq_init(chunks[0])
                preamble(chunks[0])
                g_loads(items[0]); g_loads(items[1])
                g_exps(items[0]); g_mt(items[0])
                for n_, ck in enumerate(chunks):
                    nxt = chunks[n_ + 1] if n_ + 1 < len(chunks) else None
                    ck["STb"] = STb2[cnt_["st"] % 2]
                    for g in range(8):
                        ii = n_ * 8 + g
                        if g == 1:
                            state_update(ck)
                        if g == 2 and nxt is not None and nxt["first"]:
                            seq_init(nxt)
                        if g == 2 and nxt is not None:
                            preamble(nxt)
                        if g > 0:
                            g_evac_b(items[ii - 1])
                        if ii + 2 < NI:
                            g_loads(items[ii + 2])
                        if ii + 1 < NI:
                            g_exps(items[ii + 1])
                        g_pe(items[ii])
                        g_evac(items[ii])
                        if ii + 1 < NI:
                            g_mt(items[ii + 1])
                    g_evac_b(items[n_ * 8 + 7])
                    chunk_post(ck)
        gemm("opr", Y_fm, 32, wblocks(ssd_w_out[0], 0, D, 512, "FM", make_resid_epi(i, 2)), 512, resid_alloc)

    def ffn(i):
        norm_phase(f"nf{i}", i, 1)
        Wgu = ffn_w_gu[i]

        def alloc(ph):
            return {"sg": [ph.sb(f"sg{j}", [128, 512]) for j in range(2)], "ot": [ph.sb(f"ot{j}", [128, 512], BF16) for j in range(3)], "n": [0]}

        def epi(ctx, blk, f, tb, ps):
            k = ctx["n"][0]; ctx["n"][0] += 1
            sg = ctx["sg"][k % 2]; ot = ctx["ot"][k % 3]
            S.op("act", lambda e: e.activation(out=sg[:], in_=ps[0][:], func=AF.Silu), reads=[ps[0].r], writes=[sg.r])
            S.op("dve", lambda e: e.tensor_tensor(out=ot[:], in0=ps[1][:], in1=sg[:], op=ALU.mult), reads=[ps[1].r, sg.r], writes=[ot.r])
            r0 = blk["c0"] + f * 128
            S.dma("sp", f"st_ot{k % 3}", A_fm[r0:r0 + 128, tb * 512:(tb + 1) * 512], ot[:], reads=[ot.r])

        blocks = [{"parts": [Wgu[:, c0:c0 + 256], Wgu[:, DFF + c0:DFF + c0 + 256]], "mode": "PAIR", "epi": epi, "c0": c0} for c0 in range(0, DFF, 256)]
        gemm(f"gu{i}", H_fm, 16, blocks, 512, alloc)
        gemm(f"dn{i}", A_fm, 44, wblocks(ffn_w_down[i], 0, D, 512, "FM", make_resid_epi(i, 5)), 512, resid_alloc)

    def attn_layer(i):
        norm_phase("n1m", i, 0)
        Wq = attn_w_qkv[0]

        def alloc(ph):
            ctx = {"sq": [ph.sb(f"sq{j}", [128, 512], BF16) for j in range(2)], "qn": [ph.sb(f"qn{j}", [128, 512]) for j in range(2)],
                   "sd": ph.sb("sd", [128, 512]), "rs": ph.sb("rs", [128, 512]), "t1": ph.sb("t1", [128, 512]), "t2": ph.sb("t2", [128, 512]),
                   "ot": [ph.sb(f"ot{j}", [128, 512], BF16) for j in range(2)], "vf": [ph.sb(f"vf{j}", [128, 512]) for j in range(2)],
                   "kk": ph.sb("kk", [128, 4, 128]),
                   "cos": ph.sb("cos", [128, TS]), "sin": ph.sb("sin", [128, TS]), "pm": ph.sb("pm", [128, 128]),
                   "pss": [ph.ps(f"pss{j}", [128, 512]) for j in range(2)], "n": [0]}
            S.dma("sp", "ld_cos", ctx["cos"][:], rope_cos, writes=[ctx["cos"].r])
            S.dma("sp", "ld_sin", ctx["sin"][:], rope_sin, writes=[ctx["sin"].r])
            S.dma("sp", "ld_pm", ctx["pm"][:], rope_pm, writes=[ctx["pm"].r])
            return ctx

        def qk_epi(ctx, blk, f, tb, ps):
            k = ctx["n"][0]; ctx["n"][0] += 1
            is_k = blk.get("is_k", False)
            hc = blk["c0"] // 128 + f
            sq = ctx["sq"][k % 2]; qn = ctx["qn"][k % 2]; ot = ctx["ot"][k % 2]; pss = ctx["pss"][k % 2]
            p0 = ps[0]
            S.op("act", lambda e: e.activation(out=sq[:], in_=p0[:], func=AF.Square), reads=[p0.r], writes=[sq.r])
            dst = (KT_fm if is_k else QT_fm)[hc * 128:(hc + 1) * 128, tb * 512:(tb + 1) * 512]

            def stage2():
                S.op("pe", lambda e: e.matmul(pss[:], lhsT=ones_bf[:], rhs=sq[:], start=True, stop=True), reads=[sq.r], writes=[pss.r])
                S.op("act", lambda e: e.activation(out=ctx["sd"][:], in_=pss[:], func=AF.Ln, bias=EPS, scale=1.0 / 128), reads=[pss.r], writes=[ctx["sd"].r])
                S.op("act", lambda e: e.activation(out=ctx["rs"][:], in_=ctx["sd"][:], func=AF.Exp, scale=-0.5), reads=[ctx["sd"].r], writes=[ctx["rs"].r])
                S.op("dve", lambda e: e.scalar_tensor_tensor(out=qn[:], in0=p0[:], scalar=qkg[:, (1 if is_k else 0):(2 if is_k else 1)], in1=ctx["rs"][:], op0=ALU.mult, op1=ALU.mult),
                     reads=[p0.r, ctx["rs"].r], writes=[qn.r])
                if tb < 4:
                    def stage3():
                        S.op("pe", lambda e: e.matmul(pss[:], lhsT=ctx["pm"][:], rhs=qn[:], start=True, stop=True), reads=[qn.r, ctx["pm"].r], writes=[pss.r])
                        S.op("dve", lambda e: e.tensor_tensor(out=ctx["t1"][:], in0=qn[:], in1=ctx["cos"][:, tb * 512:(tb + 1) * 512], op=ALU.mult), reads=[qn.r, ctx["cos"].r], writes=[ctx["t1"].r])
                        S.op("dve", lambda e: e.tensor_tensor(out=ctx["t2"][:], in0=pss[:], in1=ctx["sin"][:, tb * 512:(tb + 1) * 512], op=ALU.mult), reads=[pss.r, ctx["sin"].r], writes=[ctx["t2"].r])
                        S.op("dve", lambda e: e.tensor_tensor(out=ot[:], in0=ctx["t1"][:], in1=ctx["t2"][:], op=ALU.add), reads=[ctx["t1"].r, ctx["t2"].r], writes=[ot.r])
                        S.dma("sp", f"st_ot{k % 2}", dst, ot[:], reads=[ot.r])
                        return None
                    return stage3
                S.op("act", lambda e: e.copy(out=ot[:], in_=qn[:]), reads=[qn.r], writes=[ot.r])
                S.dma("sp", f"st_ot{k % 2}", dst, ot[:], reads=[ot.r])
                if is_k:
                    def stage3k():
                        for j in range(4):
                            S.op("pe", lambda e, j=j: e.transpose(out=pss[:].rearrange("p (j d) -> p j d", d=128)[:, j, :], in_=qn[:, j * 128:(j + 1) * 128], identity=ident_f[:]),
                                 reads=[qn.r], writes=[pss.r], ms=(j == 3))
                        S.op("dve", lambda e: e.tensor_copy(out=ctx["kk"][:], in_=pss[:].rearrange("p (j d) -> p j d", d=128)), reads=[pss.r], writes=[ctx["kk"].r])
                        S.dma("sp", "st_kk", nk_out[(tb - 4) * 512:(tb - 3) * 512, hc * 128:(hc + 1) * 128].rearrange("(j p) d -> p j d", p=128), ctx["kk"][:], reads=[ctx["kk"].r])
                        return None
                    return stage3k
                return None
            return stage2

        def v_epi(ctx, blk, t, tb, ps):
            k = ctx["n"][0]; ctx["n"][0] += 1
            vf = ctx["vf"][k % 2]; ot = ctx["ot"][k % 2]
            r0 = tb * 512 + t * 128
            S.op("act", lambda e: e.copy(out=vf[:], in_=ps[0][:]), reads=[ps[0].r], writes=[vf.r])
            S.op("dve", lambda e: e.tensor_copy(out=ot[:], in_=ps[0][:]), reads=[ps[0].r], writes=[ot.r])
            S.dma("sp", f"st_ot{k % 2}", V_tm[r0:r0 + 128, :], ot[:], reads=[ot.r])
            if tb >= 4:
                S.dma("sp", f"st_vf{k % 2}", nv_out[r0 - TS:r0 - TS + 128, :], vf[:], reads=[vf.r])

        kb_ = wblocks(Wq[:, 2048:2560], 0, 512, 512, "FM", qk_epi)
        for b_ in kb_:
            b_["is_k"] = True
        blocks = wblocks(Wq[:, 0:2048], 0, 2048, 512, "FM", qk_epi) + kb_ + wblocks(Wq[:, 2560:3072], 0, 512, 512, "TM", v_epi)
        gemm("qkv", H_fm, 16, blocks, 512, alloc)

        SCALE = 128 ** -0.5
        with Phase(S, "att") as ph:
            KTs = ph.sb("KTs", [128, 2304], BF16); Vs = ph.sb("Vs", [128, 18, 128], BF16)
            QTs = [ph.sb(f"QTs{j}", [128, TS], BF16) for j in range(2)]
            Es = [ph.sb(f"E{j}", [128, 512], BF16) for j in range(3)]
            rec = ph.sb("rec", [128, 512]); ots = [ph.sb(f"ot{j}", [128, 512], BF16) for j in range(2)]
            accs = [ph.sb(f"acc{j}", [128, 512]) for j in range(2)]
            ckf = ph.sb("ckf", [128, 2, 512])
            psc = [ph.ps(f"psc{j}", [128, 512]) for j in range(3)]
            pot = [ph.ps(f"pot{j}", [128, 512]) for j in range(2)]
            psm = [ph.ps(f"psm{j}", [128, 512]) for j in range(2)]
            ptc = ph.ps("ptc", [128, 128])
            cnt = {"q": 0, "e": 0, "s": 0, "o": 0}
            S.dma("sp", "ld_ckf", ckf[:], ck_in.rearrange("(j p) c -> p j c", p=128), writes=[ckf.r])

            def attn_block(qt, q0, N, nkb, dst, kb0=0):
                o = cnt["o"]; cnt["o"] += 1
                po = pot[o % 2]; pm2 = psm[o % 2]; ot = ots[o % 2]; acc = accs[o % 2]

                def sc(kb):
                    p_ = psc[cnt["s"] % 3]; cnt["s"] += 1
                    S.op("pe", lambda e: e.matmul(p_[:, 0:N], lhsT=KTs[:, (kb0 + kb) * 128:(kb0 + kb + 1) * 128], rhs=qt[:, q0:q0 + N], start=True, stop=True),
                         reads=[KTs.r, qt.r], writes=[p_.r])
                    return p_
                pq = [sc(0)]
                if nkb > 1:
                    pq.append(sc(1))
                for kb in range(nkb):
                    p_cur = pq.pop(0)
                    E = Es[cnt["e"] % 3]; cnt["e"] += 1
                    S.op("act", lambda e: e.activation(out=E[:, 0:N], in_=p_cur[:, 0:N], func=AF.Exp, scale=SCALE), reads=[p_cur.r], writes=[E.r])
                    S.op("pe", lambda e: e.matmul(po[:, 0:N], lhsT=Vs[:, kb0 + kb, :], rhs=E[:, 0:N], start=(kb == 0), stop=(kb == nkb - 1)),
                         reads=[Vs.r, E.r], writes=[po.r], ms=True)
                    if kb == 0:
                        S.op("dve", lambda e: e.tensor_copy(out=acc[:, 0:N], in_=E[:, 0:N]), reads=[E.r], writes=[acc.r])
                    else:
                        S.op("dve", lambda e: e.tensor_tensor(out=acc[:, 0:N], in0=acc[:, 0:N], in1=E[:, 0:N], op=ALU.add), reads=[E.r, acc.r], writes=[acc.r])
                    if kb + 2 < nkb:
                        pq.append(sc(kb + 2))
                S.op("pe", lambda e: e.matmul(pm2[:, 0:N], lhsT=ones_f[:], rhs=acc[:, 0:N], start=True, stop=True), reads=[acc.r], writes=[pm2.r])
                S.op("dve", lambda e: e.reciprocal(out=rec[:, 0:N], in_=pm2[:, 0:N]), reads=[pm2.r], writes=[rec.r])
                S.op("dve", lambda e: e.tensor_tensor(out=ot[:, 0:N], in0=po[:, 0:N], in1=rec[:, 0:N], op=ALU.mult), reads=[po.r, rec.r], writes=[ot.r])
                S.dma("sp", f"st_ao{o % 2}", dst, ot[:, 0:N], reads=[ot.r])

            for kvh in range(4):
                for j in range(2):
                    S.op("pe", lambda e, j=j, kvh=kvh: e.transpose(out=ptc[:], in_=ckf[:, j, kvh * 128:(kvh + 1) * 128], identity=ident_f[:]), reads=[ckf.r], writes=[ptc.r])
                    S.op("dve", lambda e, j=j: e.tensor_copy(out=KTs[:, j * 128:(j + 1) * 128], in_=ptc[:]), reads=[ptc.r], writes=[KTs.r])
                S.dma("sp", "ld_kts", KTs[:, 256:2304], KT_fm[kvh * 128:(kvh + 1) * 128, 0:TS], writes=[KTs.r])
                S.dma("pool", "ld_vsc", Vs[:, 0:2, :], cv_in[:, kvh * 128:(kvh + 1) * 128].rearrange("(j p) d -> p j d", p=128), writes=[Vs.r])
                S.dma("sp", "ld_vs", Vs[:, 2:18, :], V_tm[0:TS, kvh * 128:(kvh + 1) * 128].rearrange("(j p) d -> p j d", p=128), writes=[Vs.r])
                for r in range(4):
                    h = kvh * 4 + r
                    qt = QTs[cnt["q"] % 2]; k = cnt["q"] % 2; cnt["q"] += 1
                    S.dma("sp", f"ld_qt{k}", qt[:], QT_fm[h * 128:(h + 1) * 128, 0:TS], writes=[qt.r])
                    for qb in range(4):
                        attn_block(qt, qb * 512, 512, 18, AO_fm[h * 128:(h + 1) * 128, qb * 512:(qb + 1) * 512])
            def p_scores(qt, pj):
                p_ = psc[cnt["s"] % 3]; cnt["s"] += 1
                for kb in range(2):
                    S.op("pe", lambda e: e.matmul(p_[:, kb * LP:(kb + 1) * LP], lhsT=KTs[:, (2 * pj + kb) * 128:(2 * pj + kb + 1) * 128], rhs=qt[:, pj * LP:(pj + 1) * LP], start=True, stop=True),
                         reads=[KTs.r, qt.r], writes=[p_.r], ms=(kb == 1))
                return p_

            def p_rest(p_, pj, dst):
                o = cnt["o"]; cnt["o"] += 1
                po = pot[o % 2]; pm2 = psm[o % 2]; ot = ots[o % 2]
                E = Es[cnt["e"] % 3]; cnt["e"] += 1
                S.op("act", lambda e: e.activation(out=E[:], in_=p_[:], func=AF.Exp, scale=SCALE), reads=[p_.r], writes=[E.r])
                for kb in range(2):
                    S.op("pe", lambda e: e.matmul(po[:, 0:LP], lhsT=Vs[:, 2 * pj + kb, :], rhs=E[:, kb * LP:(kb + 1) * LP], start=(kb == 0), stop=(kb == 1)),
                         reads=[Vs.r, E.r], writes=[po.r], ms=False)
                for kb in range(2):
                    S.op("pe", lambda e: e.matmul(pm2[:, 0:LP], lhsT=ones_bf[:], rhs=E[:, kb * LP:(kb + 1) * LP], start=(kb == 0), stop=(kb == 1)),
                         reads=[E.r], writes=[pm2.r, po.r], ms=(kb == 1))
                S.op("dve", lambda e: e.reciprocal(out=rec[:, 0:LP], in_=pm2[:, 0:LP]), reads=[pm2.r], writes=[rec.r])
                S.op("dve", lambda e: e.tensor_tensor(out=ot[:, 0:LP], in0=po[:, 0:LP], in1=rec[:, 0:LP], op=ALU.mult), reads=[po.r, rec.r], writes=[ot.r])
                S.dma("sp", f"st_ao{o % 2}", dst, ot[:, 0:LP], reads=[ot.r])

            work = []
            for kvh in range(4):
                for r in range(4):
                    for pj in range(NPR):
                        work.append((kvh, r, pj))
            pend = None
            for (kvh, r, pj) in work:
                h = kvh * 4 + r
                if r == 0 and pj == 0:
                    if pend is not None:
                        p_rest(*pend); pend = None
                    S.dma("sp", "ld_kts", KTs[:, 0:NPR * LP], KT_fm[kvh * 128:(kvh + 1) * 128, TS:T], writes=[KTs.r])
                    S.dma("sp", "ld_vs", Vs[:, 0:8, :], V_tm[TS:T, kvh * 128:(kvh + 1) * 128].rearrange("(j p) d -> p j d", p=128), writes=[Vs.r])
                if pj == 0:
                    qt = QTs[cnt["q"] % 2]; k = cnt["q"] % 2; cnt["q"] += 1
                    S.dma("sp", f"ld_qt{k}", qt[:, 0:NPR * LP], QT_fm[h * 128:(h + 1) * 128, TS:T], writes=[qt.r])
                p_ = p_scores(qt, pj)
                if pend is not None:
                    p_rest(*pend)
                pend = (p_, pj, AO_fm[h * 128:(h + 1) * 128, TS + pj * LP:TS + (pj + 1) * LP])
            p_rest(*pend)

        gemm("opj", AO_fm, 16, wblocks(attn_w_o[0], 0, D, 512, "FM", make_resid_epi(i, 2)), 512, resid_alloc)

    def final_phase():
        with Phase(S, "fin") as ph:
            NT = 256
            NB = T // NT
            xb = [ph.sb(f"x{j}", [128, 16, NT]) for j in range(2)]
            sq = ph.sb("sq", [128, 16, NT], BF16)
            t1 = ph.sb("t1", [128, 16, NT]); t2b = [ph.sb(f"t2{j}", [128, 16, NT]) for j in range(2)]
            sd = ph.sb("sd", [128, NT]); R = ph.sb("R", [128, NT])
            yo = [ph.sb(f"yo{j}", [128, D]) for j in range(2)]
            pss = [ph.ps(f"ss{j}", [128, NT]) for j in range(2)]
            pp = [ph.ps(f"pp{j}", [128, 4, 128]) for j in range(6)]
            cn = {"n": 0, "m": 0}

            def stage_a(tb):
                x = xb[tb % 2]; ps_ = pss[tb % 2]; t2 = t2b[tb % 2]
                S.dma("sp", f"ld_nx{tb % 2}", x[:], Xv[:, :, tb * NT:(tb + 1) * NT], writes=[x.r])
                S.op("act", lambda e: e.activation(out=sq[:], in_=x[:], func=AF.Square), reads=[x.r], writes=[sq.r])
                for kc in range(16):
                    S.op("pe", lambda e: e.matmul(ps_[:], lhsT=ones_bf[:], rhs=sq[:, kc, :], start=(kc == 0), stop=(kc == 15)),
                         reads=[sq.r], writes=[ps_.r], ms=(kc == 15))
                S.op("act", lambda e: e.activation(out=sd[:], in_=ps_[:], func=AF.Ln, bias=EPS, scale=1.0 / D), reads=[ps_.r], writes=[sd.r])
                S.op("act", lambda e: e.activation(out=R[:], in_=sd[:], func=AF.Exp, scale=-0.5), reads=[sd.r], writes=[R.r])
                S.op("dve", lambda e: e.tensor_tensor(out=t1[:], in0=x[:], in1=gfin[:, :].unsqueeze(2).to_broadcast([128, 16, NT]), op=ALU.mult), reads=[x.r], writes=[t1.r])
                S.op("dve", lambda e: e.tensor_tensor(out=t2[:], in0=t1[:], in1=R[:].unsqueeze(1).to_broadcast([128, 16, NT]), op=ALU.mult), reads=[t1.r, R.r], writes=[t2.r])

            def stage_b(tb):
                t2 = t2b[tb % 2]
                for hf in range(NT // 128):
                    y = yo[cn["m"] % 2]; km = cn["m"] % 2; cn["m"] += 1
                    for q in range(4):
                        p_ = pp[cn["n"] % 6]; cn["n"] += 1
                        for j in range(4):
                            kc = q * 4 + j
                            S.op("pe", lambda e: e.transpose(out=p_[:, j, :], in_=t2[:, kc, hf * 128:(hf + 1) * 128], identity=ident_f[:]),
                                 reads=[t2.r], writes=[p_.r], ms=(j == 3))
                        dsl = y[:, q * 512:(q + 1) * 512].rearrange("p (j d) -> p j d", d=128)
                        if q % 2 == 0:
                            S.op("dve", lambda e: e.tensor_copy(out=dsl, in_=p_[:]), reads=[p_.r], writes=[y.r])
                        else:
                            S.op("act", lambda e: e.copy(out=dsl, in_=p_[:]), reads=[p_.r], writes=[y.r])
                    tok0 = tb * NT + hf * 128
                    dst = ys_out[tok0:tok0 + 128, :] if tok0 < TS else yp_out[tok0 - TS:tok0 - TS + 128, :]
                    S.dma("sp", f"st_yo{km}", dst, y[:], reads=[y.r], writes=[OUTR])

            stage_a(0)
            for tb in range(NB):
                if tb + 1 < NB:
                    stage_a(tb + 1)
                stage_b(tb)

    OUTR = DRes("outputs")
    try:
        convert_phase()
        ssd_layer(0)
        ffn(0)
        attn_layer(1)
        ffn(1)
        final_phase()
    except StopBuild:
        pass
    S.finish([OUTR])
    return nc


def _rope_consts():
    inv = (10000.0 ** (-np.arange(0, 64, 2, dtype=np.float32) / 64)).astype(np.float32)
    t = np.arange(TS)
    row = (t // 64).astype(np.float32); col = (t % 64).astype(np.float32)
    cos = np.zeros((128, TS), np.float32); sin = np.zeros((128, TS), np.float32)
    pm = np.zeros((128, 128), np.float32)
    for m in range(128):
        pos = row if m < 64 else col
        ang = (pos * inv[m % 32]).astype(np.float32)
        first = (m % 64) < 32
        cos[m] = np.cos(ang)
        sin[m] = -np.sin(ang) if first else np.sin(ang)
        pm[(m + 32) if first else (m - 32), m] = 1.0
    return cos, sin, pm


_PROG = None


def kernel(**inp):
    global _PROG
    if _PROG is None:
        _PROG = build_program()
    nc = _PROG
    f = lambda a: np.ascontiguousarray(np.asarray(a, dtype=np.float32))
    cos, sin, pm = _rope_consts()
    shared = {k: f(inp[k]) for k in ("w_mod", "b_mod", "norm_mix", "norm_ffn", "ssd_w_in", "ssd_conv_w", "ssd_conv_b", "ssd_norm",
                                     "ssd_w_out", "attn_w_qkv", "attn_q_norm", "attn_k_norm", "attn_w_o", "ffn_w_gu", "ffn_w_down")}
    shared["ssd_a_log"] = f(inp["ssd_a_log"]).reshape(1, 128)
    shared["ssd_dt_bias"] = f(inp["ssd_dt_bias"]).reshape(1, 128)
    shared["ssd_d"] = f(inp["ssd_d"]).reshape(1, 128)
    shared["final_norm"] = f(inp["final_norm"]).reshape(1, D)
    shared["cctx"] = f(inp["c_ctx"]).reshape(1, D)
    shared["rope_cos"] = cos; shared["rope_sin"] = sin; shared["rope_pm"] = pm
    xs = f(inp["x_sample"]); xp = f(inp["x_prompt"]); sf = f(inp["state_ssd_fwd"]); sb = f(inp["state_ssd_bwd"])
    ck = f(inp["cache_k"]); cv = f(inp["cache_v"]); c = f(inp["c"])
    in_maps = []
    for b in range(8):
        m = dict(shared)
        m["xs"] = xs[b]; m["xp"] = xp[4 * b:4 * b + 4].reshape(NPR * LP, D)
        m["sf"] = sf[b, 0].reshape(DI, 128); m["sbw"] = sb[b, 0].reshape(DI, 128)
        m["ck"] = ck[b, 0].reshape(256, 512); m["cv"] = cv[b, 0].reshape(256, 512)
        m["c"] = c[b:b + 1]
        in_maps.append(m)
    res = run_bass_kernel_spmd(nc, in_maps, core_ids=list(range(8)))
    R = res.results
    y_prompt = np.concatenate([r["yp"].reshape(NPR, LP, D) for r in R], 0)
    y_sample = np.stack([r["ys"] for r in R], 0)
    new_f = np.concatenate([r["nf"].reshape(NPR, 1, 64, 64, 128) for r in R], 0)
    new_b = np.concatenate([r["nb"].reshape(NPR, 1, 64, 64, 128) for r in R], 0)
    new_k = np.concatenate([r["nk"].reshape(NPR, 1, LP, 4, 128) for r in R], 0)
    new_v = np.concatenate([r["nv"].reshape(NPR, 1, LP, 4, 128) for r in R], 0)
    return tuple(np.ascontiguousarray(a.astype(np.float32)) for a in (y_prompt, y_sample, new_f, new_b, new_k, new_v))
```

```python
import numpy as np
import concourse.bass as bass
import concourse.mybir as mybir
from concourse.bass_utils import run_bass_kernel_spmd
from contextlib import ExitStack
import os
KSKIP = os.environ.get('KSKIP', '')

F32 = mybir.dt.float32
BF16 = mybir.dt.bfloat16
AF = mybir.ActivationFunctionType
ALU = mybir.AluOpType
AX = mybir.AxisListType


class Res:
    __slots__ = ("name", "w", "r")

    def __init__(self, name):
        self.name = name
        self.w = None
        self.r = {}

    def set_w(self, tok):
        self.w = tok
        self.r = {}


class PRes(Res):
    __slots__ = ()
    excl = True


class DRes:
    __slots__ = ("name", "w", "r")

    def __init__(self, name):
        self.name = name
        self.w = {}
        self.r = {}

    def set_w(self, tok):
        k, v = tok
        if self.w.get(k, 0) < v:
            self.w[k] = v


class _Rec:
    def __init__(self):
        self.calls = []

    def __getattr__(self, name):
        def f(*a, **k):
            self.calls.append((name, a, k))
            return None
        return f


def _replayer(calls):
    def replay(e):
        r = None
        for (n, a, k) in calls:
            r = getattr(e, n)(*a, **k)
        return r
    return replay


class Sched:
    ENG = ("pe", "act", "dve", "pool", "sp")

    def __init__(self, nc):
        self.nc = nc
        self.ops = {e: [] for e in self.ENG}
        self.sems = {}
        self.cnt = {}
        self.waited = {e: {} for e in self.ENG}
        for e in self.ENG:
            self._sem("E_" + e)
        self.nwaits = 0
        self.dmap = {}

    def _sem(self, key):
        if key not in self.sems:
            self.sems[key] = self.nc.alloc_semaphore(key)
            self.cnt[key] = 0
        return self.sems[key]

    def op(self, eng, fn, reads=(), writes=(), ms=True):
        key = "E_" + eng
        rec = _Rec()
        fn(rec)
        fn = _replayer(rec.calls)
        ex = [r for r in reads if getattr(r, "excl", False)]
        if ex:
            writes = list(writes) + [r for r in ex if r not in writes]
            reads = [r for r in reads if not getattr(r, "excl", False)]
        waits = self._deps_compute(eng, reads, writes)
        tokv = self.cnt[key] + 1
        if ms:
            self.cnt[key] = tokv
        self.ops[eng].append((waits, fn, (key, 1) if ms else None))
        tok = (key, tokv)
        for r in reads:
            if r.r.get(key, 0) < tokv:
                r.r[key] = tokv
        for w in writes:
            w.set_w(tok)
        return tok

    def _deps_compute(self, eng, reads, writes, strict=False):
        deps = {}
        mykey = "E_" + eng
        def add(tok, allow_self):
            allow_self = allow_self or strict
            if tok is None:
                return
            k, v = tok
            if k == mykey and not allow_self:
                return
            if deps.get(k, 0) < v:
                deps[k] = v
        for r in reads:
            if isinstance(r.w, dict):
                for k, v in r.w.items():
                    add((k, v), True)
                continue
            add(r.w, eng != "pe")
        for w in writes:
            if not isinstance(w.w, dict):
                add(w.w, False)
            for k, v in w.r.items():
                add((k, v), False)
        out = []
        for k, v in deps.items():
            if self.waited[eng].get(k, 0) >= v:
                continue
            assert self.cnt[k] >= v, f"wait on future milestone {k}>={v} (have {self.cnt[k]}) from {eng}"
            self.waited[eng][k] = v
            out.append((k, v))
        return out

    def dma(self, eng, semkey, out, in_, reads=(), writes=(), **kw):
        if semkey not in self.dmap:
            self.dmap[semkey] = f"D{len(self.dmap)}"
        semkey = self.dmap[semkey]
        self._sem(semkey)
        waits = self._deps_compute(eng, reads, writes, strict=True)
        self.cnt[semkey] += 16
        tok = (semkey, self.cnt[semkey])
        self.ops[eng].append((waits, lambda e: e.dma_start(out=out, in_=in_, **kw), (semkey, 16)))
        for r in reads:
            if r.r.get(semkey, 0) < tok[1]:
                r.r[semkey] = tok[1]
        for w in writes:
            w.set_w(tok)
        return tok

    def barrier(self):
        self._sem("BAR")
        for e in ("pe", "act", "dve", "pool"):
            real = [o for o in self.ops[e] if o[1] is not None]
            if real:
                assert real[-1][2] is not None, f"last op on {e} is not a milestone"
        waits = []
        for k, c in self.cnt.items():
            if k == "BAR":
                continue
            if c > self.waited["sp"].get(k, 0):
                waits.append((k, c))
        self.cnt["BAR"] += 1
        v = self.cnt["BAR"]
        bar = self.sems["BAR"]
        self.ops["sp"].append((waits, lambda e: e.sem_inc(bar, 1), None))
        for e in self.ENG:
            if e != "sp":
                self.ops[e].append(([("BAR", v)], None, None))
            self.waited[e] = dict(self.cnt)
        self.dmap = {}

    def finish(self, final_res):
        waits = self._deps_compute("sp", final_res, ())
        self.ops["sp"].append((waits, None, None))
        nc = self.nc
        emap = {"pe": "tensor", "act": "scalar", "dve": "vector", "pool": "gpsimd", "sp": "sync"}
        with nc.Block() as block:
            for e in self.ENG:
                ops = self.ops[e]
                sems = self.sems

                def body(engine, ops=ops):
                    for waits, fn, inc in ops:
                        for k, v in waits:
                            engine.wait_ge(sems[k], v)
                            self.nwaits += 1
                        if fn is None:
                            continue
                        ins = fn(engine)
                        if inc is not None:
                            ins.then_inc(sems[inc[0]], inc[1])
                getattr(block, emap[e])(body)

D = 2048
T = 3072
TS = 2048
NPR = 4
LP = 256
DI = 4096
DFF = 5632
EPS = 1e-6
TPAD = T + 12


class Buf:
    def __init__(self, h, name):
        self.h = h
        self.r = Res(name)

    def __getitem__(self, k):
        return self.h[k]


class StopBuild(Exception):
    pass


class Phase:
    stop_after = None
    stopped = False

    def __init__(self, S, name):
        self.S = S
        self.nc = S.nc
        self.name = name

    def __enter__(self):
        if Phase.stopped:
            raise StopBuild()
        self.es = ExitStack()
        return self

    def sb(self, name, shape, dt=F32):
        h = self.es.enter_context(self.nc.sbuf_tensor(f"{self.name}_{name}", shape, dt))
        return Buf(h, name)

    def ps(self, name, shape, dt=F32):
        full = [128, 512] if dt == F32 else [128, 1024]
        h = self.es.enter_context(self.nc.psum_tensor(f"{self.name}_{name}", full, dt))
        n = 1
        for d_ in shape[1:]:
            n *= d_
        assert n <= full[1]
        v = h[0:shape[0], 0:n]
        if len(shape) == 3:
            v = v.rearrange("p (a b) -> p a b", b=shape[2])
        b = Buf(v, name)
        b.r = PRes(name)
        return b

    def __exit__(self, *a):
        if KSKIP == 'mem':
            print("phase", self.name, "sbuf bytes remaining", self.nc.sbuf_bytes_remaining)
        if a[0] is None:
            self.S.barrier()
            if Phase.stop_after == self.name:
                Phase.stopped = True
        self.es.close()
        return False


def build_program(stop_after=None, debug=False):
    Phase.stop_after = stop_after
    Phase.stopped = False
    nc = bass.Bass("TRN2", target_bir_lowering=False)
    S = Sched(nc)

    def din(name, shape):
        return nc.dram_tensor(name, shape, F32, kind="ExternalInput").ap()

    def dout(name, shape):
        return nc.dram_tensor(name, shape, F32, kind="ExternalOutput").ap()

    def scr(name, shape, dt):
        if debug:
            return nc.dram_tensor(name, shape, dt, kind="ExternalOutput").ap()
        return nc.dram_tensor(name, shape, dt).ap()

    xs_in = din("xs", [TS, D]); xp_in = din("xp", [NPR * LP, D])
    sf_in = din("sf", [DI, 128]); sb_in = din("sbw", [DI, 128])
    ck_in = din("ck", [256, 512]); cv_in = din("cv", [256, 512])
    c_in = din("c", [1, D]); cctx_in = din("cctx", [1, D])
    w_mod = din("w_mod", [2, D, 6 * D]); b_mod = din("b_mod", [2, 6 * D])
    norm_mix = din("norm_mix", [2, D]); norm_ffn = din("norm_ffn", [2, D])
    ssd_w_in = din("ssd_w_in", [1, D, 10368]); ssd_conv_w = din("ssd_conv_w", [1, 5, 6144])
    ssd_conv_b = din("ssd_conv_b", [1, 6144]); ssd_a_log = din("ssd_a_log", [1, 128])
    ssd_dt_bias = din("ssd_dt_bias", [1, 128]); ssd_d = din("ssd_d", [1, 128])
    ssd_norm = din("ssd_norm", [1, DI]); ssd_w_out = din("ssd_w_out", [1, DI, D])
    attn_w_qkv = din("attn_w_qkv", [1, D, 3072]); attn_q_norm = din("attn_q_norm", [1, 128])
    attn_k_norm = din("attn_k_norm", [1, 128]); attn_w_o = din("attn_w_o", [1, D, D])
    ffn_w_gu = din("ffn_w_gu", [2, D, 2 * DFF]); ffn_w_down = din("ffn_w_down", [2, DFF, D])
    final_norm = din("final_norm", [1, D])
    rope_cos = din("rope_cos", [128, TS]); rope_sin = din("rope_sin", [128, TS]); rope_pm = din("rope_pm", [128, 128])
    yp_out = dout("yp", [NPR * LP, D]); ys_out = dout("ys", [TS, D])
    nf_out = dout("nf", [NPR, DI, 128]); nb_out = dout("nb", [NPR, DI, 128])
    nk_out = dout("nk", [NPR * LP, 512]); nv_out = dout("nv", [NPR * LP, 512])
    X_fm = scr("X_fm", [D, T], F32)
    H_fm = scr("H_fm", [D, T], BF16)
    SZ_tm = scr("SZ_tm", [T, DI], F32)
    U_fm = scr("U_fm", [6144, TPAD], BF16)
    XC_cm = scr("XC_cm", [T // 128, 128, 48, 128], BF16)
    DT_tm = scr("DT_tm", [T, 128], F32)
    ACS_tm = scr("ACS_tm", [T, 128], F32)
    WGT_tm = scr("WGT_tm", [T, 128], F32)
    ETOT = scr("ETOT", [T, 128], F32)
    ACS_cm = scr("ACS_cm", [T // 128, 2, 64, 128], F32)
    Y0_tm = scr("Y0_tm", [T, DI], F32)
    Y_fm = scr("Y_fm", [DI, T], BF16)
    A_fm = scr("A_fm", [DFF, T], BF16)
    QT_fm = scr("QT_fm", [D, T], BF16)
    KT_fm = scr("KT_fm", [512, T], BF16)
    V_tm = scr("V_tm", [T, 512], BF16)
    AO_fm = scr("AO_fm", [D, T], BF16)

    def dbg(name, buf, shape, dt):
        if not debug:
            return
        t = nc.dram_tensor("dbg_" + name, shape, dt, kind="ExternalOutput").ap()
        S.dma("sp", "dbg_" + name, t, buf[:], reads=[buf.r])

    def gbuf(name, shape, dt=F32):
        return Buf(nc.alloc_sbuf_tensor("g_" + name, shape, dt), name)

    ident_bf = gbuf("ident_bf", [128, 128], BF16); ident_f = gbuf("ident_f", [128, 128])
    ones_bf = gbuf("ones_bf", [128, 128], BF16); ones_f = gbuf("ones_f", [128, 128])
    tri_f = gbuf("tri_f", [128, 128]); tri_b = gbuf("tri_b", [128, 128])
    tri_f_bf = gbuf("tri_f_bf", [128, 128], BF16); tri_b_bf = gbuf("tri_b_bf", [128, 128], BF16)
    modv = gbuf("modv", [128, 2, 96, 2])
    gmix = gbuf("gmix", [128, 2, 16]); gffn = gbuf("gffn", [128, 2, 16]); gfin = gbuf("gfin", [128, 16])
    gs = gbuf("gs", [128, 2, 2, 2, 16])
    cw = gbuf("cw", [128, 6, 48])
    qkg = gbuf("qkg", [128, 2])

    def mk_const():
        def msel(buf, pattern, cm, cmp, val=1.0):
            S.op("pool", lambda e: e.memset(buf[:], val), writes=[buf.r])
            S.op("pool", lambda e: e.affine_select(out=buf[:], in_=buf[:], pattern=pattern, compare_op=cmp,
                                                   fill=0.0, base=0, channel_multiplier=cm),
                 reads=[buf.r], writes=[buf.r])
        msel(ident_bf, [[-1, 128]], 1, ALU.is_equal); msel(ident_f, [[-1, 128]], 1, ALU.is_equal)
        msel(tri_f, [[1, 128]], -1, ALU.is_ge); msel(tri_f_bf, [[1, 128]], -1, ALU.is_ge)
        msel(tri_b, [[-1, 128]], 1, ALU.is_ge); msel(tri_b_bf, [[-1, 128]], 1, ALU.is_ge)
        S.op("pool", lambda e: e.memset(ones_bf[:], 1.0), writes=[ones_bf.r])
        S.op("pool", lambda e: e.memset(ones_f[:], 1.0), writes=[ones_f.r])

    def tok_rows(t0, n):
        if t0 < TS:
            return ("s", t0)
        return ("p", t0 - TS)

    with Phase(S, "su") as ph:
        mk_const()
        S.barrier()
        rowb = ph.sb("rowb", [1, 6144]); pc = ph.ps("pc", [128, 512])
        one11 = ones_f

        def row2col(src_row_ap, n, dst_ap, func=None):
            S.dma("sp", "ld_rowb", rowb[0:1, 0:n * 128], src_row_ap, writes=[rowb.r])
            for j in range(n):
                S.op("pe", lambda e, j=j: e.matmul(pc[:, j:j + 1], lhsT=rowb[0:1, j * 128:(j + 1) * 128],
                                                  rhs=one11[0:1, 0:1], start=True, stop=True),
                     reads=[rowb.r, one11.r], writes=[pc.r], ms=(j == n - 1))
            if func is None:
                S.op("dve", lambda e: e.tensor_copy(out=dst_ap, in_=pc[:, 0:n]), reads=[pc.r], writes=[])
            else:
                S.op("act", lambda e: e.activation(out=dst_ap, in_=pc[:, 0:n], func=func), reads=[pc.r], writes=[])

        for i in range(2):
            row2col(norm_mix[i:i + 1, :], 16, gmix[:, i, :])
            row2col(norm_ffn[i:i + 1, :], 16, gffn[:, i, :])
        row2col(final_norm[0:1, :], 16, gfin[:, :])
        for k in range(5):
            row2col(ssd_conv_w[0, k:k + 1, :], 48, cw[:, k, :])
        row2col(ssd_conv_b[0:1, :], 48, cw[:, 5, :])
        row2col(attn_q_norm[0:1, :], 1, qkg[:, 0:1])
        row2col(attn_k_norm[0:1, :], 1, qkg[:, 1:2])
        sc = ph.sb("sc", [128, 16, 2])
        row2col(c_in[0:1, :], 16, sc[:, :, 0], func=AF.Silu)
        row2col(cctx_in[0:1, :], 16, sc[:, :, 1], func=AF.Silu)
        S.barrier()
        sc_bf = ph.sb("sc_bf", [128, 16, 2], BF16)
        S.op("dve", lambda e: e.tensor_copy(out=sc_bf[:], in_=sc[:]), reads=[sc.r], writes=[sc_bf.r])
        wsl = [ph.sb(f"wm{j}", [128, 16, 512], BF16) for j in range(4)]
        wfl = [ph.sb(f"wf{j}", [128, 16, 512]) for j in range(2)]
        bsl = [ph.sb(f"bm{j}", [1, 512]) for j in range(4)]
        pm_ = ph.ps("pm", [128, 192])
        n = 0; nf_ = 0
        for i in range(2):
            for cb in range(24):
                bb = bsl[n % 4]; w = wsl[n % 4]
                src = w_mod[i, :, cb * 512:(cb + 1) * 512].rearrange("(kc p) n -> p kc n", p=128)
                if n % 3 == 0:
                    S.dma("pool", f"ld_wm{n % 4}", w[:], src, writes=[w.r])
                else:
                    wf = wfl[nf_ % 2]
                    S.dma("sp", f"ld_wf{nf_ % 2}", wf[:], src, writes=[wf.r])
                    if nf_ % 2 == 0:
                        S.op("act", lambda e: e.copy(out=w[:], in_=wf[:]), reads=[wf.r], writes=[w.r])
                    else:
                        S.op("dve", lambda e: e.tensor_copy(out=w[:], in_=wf[:]), reads=[wf.r], writes=[w.r])
                    nf_ += 1
                S.dma("sp", f"ld_bm{n % 4}", bb[0:1, :], b_mod[i:i + 1, cb * 512:(cb + 1) * 512], writes=[bb.r])
                for f in range(4):
                    m = cb * 4 + f
                    for kc in range(16):
                        S.op("pe", lambda e: e.matmul(pm_[:, m * 2:m * 2 + 2], lhsT=w[:, kc, f * 128:(f + 1) * 128],
                                                     rhs=sc_bf[:, kc, :], start=(kc == 0), stop=False),
                             reads=[w.r, sc_bf.r], writes=[pm_.r], ms=False)
                    S.op("pe", lambda e: e.matmul(pm_[:, m * 2:m * 2 + 2], lhsT=bb[0:1, f * 128:(f + 1) * 128],
                                                 rhs=ones_f[0:1, 0:2], start=False, stop=True),
                         reads=[bb.r, ones_f.r], writes=[pm_.r], ms=True)
                n += 1
            S.op("dve", lambda e: e.tensor_copy(out=modv[:, i, :, :], in_=pm_[:].rearrange("p (m c) -> p m c", c=2)),
                 reads=[pm_.r], writes=[modv.r])
        for i in range(2):
            for sub in range(2):
                g = gmix if sub == 0 else gffn
                for cnd in range(2):
                    S.op("dve", lambda e, i=i, sub=sub, cnd=cnd, g=g: e.scalar_tensor_tensor(
                        out=gs[:, i, sub, cnd, :], in0=modv[:, i, sub * 48 + 16:sub * 48 + 32, cnd], scalar=1.0,
                        in1=g[:, i, :], op0=ALU.add, op1=ALU.mult), reads=[modv.r], writes=[gs.r])

    def mod_col(i, kind, chunk, cnd):
        return modv[:, i, kind * 16 + chunk, cnd:cnd + 1]

    Xv = X_fm.rearrange("(kc p) t -> p kc t", p=128)
    Hv = H_fm.rearrange("(kc p) t -> p kc t", p=128)

    def convert_phase():
        with Phase(S, "cv") as ph:
            xin = [ph.sb(f"xin{j}", [128, D]) for j in range(2)]
            xf = [ph.sb(f"xf{j}", [128, 16, 128]) for j in range(2)]
            pp = [ph.ps(f"pp{j}", [128, 4, 128]) for j in range(8)]
            for tt in range(T // 128):
                a = xin[tt % 2]; o = xf[tt % 2]
                src = xs_in[tt * 128:(tt + 1) * 128, :] if tt < 16 else xp_in[(tt - 16) * 128:(tt - 15) * 128, :]
                S.dma("sp", f"ld_xin{tt % 2}", a[:], src, writes=[a.r])
                for q in range(4):
                    p_ = pp[(tt * 4 + q) % 8]
                    for j in range(4):
                        kc = q * 4 + j
                        S.op("pe", lambda e, p_=p_, a=a, kc=kc, j=j: e.transpose(out=p_[:, j, :], in_=a[:, kc * 128:(kc + 1) * 128], identity=ident_f[:]),
                             reads=[a.r], writes=[p_.r], ms=(j == 3))
                    eng = "dve" if q % 2 == 0 else "act"
                    if eng == "dve":
                        S.op("dve", lambda e, p_=p_, o=o, q=q: e.tensor_copy(out=o[:, q * 4:(q + 1) * 4, :], in_=p_[:]), reads=[p_.r], writes=[o.r])
                    else:
                        S.op("act", lambda e, p_=p_, o=o, q=q: e.copy(out=o[:, q * 4:(q + 1) * 4, :], in_=p_[:]), reads=[p_.r], writes=[o.r])
                S.dma("sp", f"st_xf{tt % 2}", Xv[:, :, tt * 128:(tt + 1) * 128], o[:], reads=[o.r])

    def norm_phase(tag, i, sub):
        with Phase(S, tag) as ph:
            NT = 256
            NB = T // NT
            xb = [ph.sb(f"x{j}", [128, 16, NT]) for j in range(2)]
            sq = ph.sb("sq", [128, 16, NT], BF16)
            t1b = [ph.sb(f"t1{j}", [128, 16, NT]) for j in range(2)]
            hb = [ph.sb(f"h{j}", [128, 16, NT], BF16) for j in range(2)]
            sd = ph.sb("sd", [128, NT]); R = ph.sb("R", [128, NT])
            pss = [ph.ps(f"ss{j}", [128, NT]) for j in range(2)]

            def stage_a(tb):
                x = xb[tb % 2]; ps_ = pss[tb % 2]; t1 = t1b[tb % 2]
                S.dma("sp", f"ld_nx{tb % 2}", x[:], Xv[:, :, tb * NT:(tb + 1) * NT], writes=[x.r])
                S.op("act", lambda e: e.activation(out=sq[:], in_=x[:], func=AF.Square), reads=[x.r], writes=[sq.r])
                for kc in range(16):
                    S.op("pe", lambda e: e.matmul(ps_[:], lhsT=ones_bf[:], rhs=sq[:, kc, :], start=(kc == 0), stop=(kc == 15)),
                         reads=[sq.r], writes=[ps_.r], ms=(kc == 15))
                S.op("act", lambda e: e.activation(out=sd[:], in_=ps_[:], func=AF.Ln, bias=EPS, scale=1.0 / D), reads=[ps_.r], writes=[sd.r])
                S.op("act", lambda e: e.activation(out=R[:], in_=sd[:], func=AF.Exp, scale=-0.5), reads=[sd.r], writes=[R.r])
                S.op("dve", lambda e: e.tensor_tensor(out=t1[:], in0=x[:], in1=R[:].unsqueeze(1).to_broadcast([128, 16, NT]), op=ALU.mult),
                     reads=[x.r, R.r], writes=[t1.r])

            def stage_b(tb):
                cnd = 0 if tb * NT < TS else 1
                h = hb[tb % 2]; t1 = t1b[tb % 2]
                for kc in range(16):
                    S.op("act", lambda e: e.activation(out=h[:, kc, :], in_=t1[:, kc, :], func=AF.Identity, scale=gs[:, i, sub, cnd, kc:kc + 1],
                                                       bias=modv[:, i, sub * 48 + kc, cnd:cnd + 1]),
                         reads=[t1.r], writes=[h.r], ms=(kc == 15))
                S.dma("sp", f"st_nh{tb % 2}", Hv[:, :, tb * NT:(tb + 1) * NT], h[:], reads=[h.r])

            stage_a(0)
            for tb in range(NB):
                if tb + 1 < NB:
                    stage_a(tb + 1)
                stage_b(tb)

    def gemm(tag, A_scr, KC, blocks, wcols, alloc_extra):
        Av = A_scr.rearrange("(kc p) t -> p kc t", p=128)
        with Phase(S, tag) as ph:
            asl = [ph.sb(f"a{j}", [128, KC, 512], BF16) for j in range(2)]
            wsl = [ph.sb(f"w{j}", [128, KC, wcols], BF16) for j in range(2)]
            wres = [[Res(f"w{j}p{q}") for q in range(4)] for j in range(2)]
            psl = [ph.ps(f"ps{j}", [128, 512]) for j in range(6)]
            ctx = alloc_extra(ph) if alloc_extra else None
            pcount = [0]
            pending = []

            def next_ps():
                p_ = psl[pcount[0] % 6]
                pcount[0] += 1
                return p_

            def load_w(bi):
                blk = blocks[bi]; w = wsl[bi % 2]
                off = 0
                for q, part in enumerate(blk["parts"]):
                    n = part.shape[1]
                    S.dma("pool", f"ld_w{bi % 2}_{q}", w[:, :, off:off + n], part.rearrange("(kc p) n -> p kc n", p=128),
                          writes=[wres[bi % 2][q]])
                    off += n

            na = [0]

            def load_a(tb):
                a = asl[na[0] % 2]
                S.dma("sp", f"ld_a{na[0] % 2}", a[:], Av[:, :, tb * 512:(tb + 1) * 512], writes=[a.r])
                na[0] += 1
                return a

            def run_pending():
                nxt = []
                for c in pending:
                    r = c()
                    if r is not None:
                        nxt.append(r)
                pending[:] = nxt

            load_w(0)
            NTB = T // 512
            a_next = load_a(0)
            for bi, blk in enumerate(blocks):
                w = wsl[bi % 2]
                wr = wres[bi % 2][:len(blk["parts"])]
                if bi + 1 < len(blocks):
                    load_w(bi + 1)
                ncols = sum(p.shape[1] for p in blk["parts"])
                for tb in range(NTB):
                    a = a_next
                    if tb + 1 < NTB or bi + 1 < len(blocks):
                        a_next = load_a((tb + 1) % NTB)
                    mode = blk["mode"]
                    if mode == "FM":
                        for f in range(ncols // 128):
                            p_ = next_ps()
                            for kc in range(KC):
                                S.op("pe", lambda e, p_=p_, w=w, a=a, kc=kc, f=f: e.matmul(p_[:], lhsT=w[:, kc, f * 128:(f + 1) * 128], rhs=a[:, kc, :],
                                                                                        start=(kc == 0), stop=(kc == KC - 1)),
                                     reads=[a.r] + wr, writes=[p_.r], ms=(kc == KC - 1))
                            run_pending()
                            r = blk["epi"](ctx, blk, f, tb, [p_])
                            if r is not None:
                                pending.append(r)
                    elif mode == "PAIR":
                        half = ncols // 2
                        for f in range(half // 128):
                            pp_ = []
                            for hh in range(2):
                                p_ = next_ps()
                                c0 = hh * half + f * 128
                                for kc in range(KC):
                                    S.op("pe", lambda e, p_=p_, w=w, a=a, kc=kc, c0=c0: e.matmul(p_[:], lhsT=w[:, kc, c0:c0 + 128], rhs=a[:, kc, :],
                                                                                              start=(kc == 0), stop=(kc == KC - 1)),
                                         reads=[a.r] + wr, writes=[p_.r], ms=(kc == KC - 1))
                                pp_.append(p_)
                            run_pending()
                            r = blk["epi"](ctx, blk, f, tb, pp_)
                            if r is not None:
                                pending.append(r)
                    else:
                        for t in range(4):
                            p_ = next_ps()
                            for kc in range(KC):
                                S.op("pe", lambda e, p_=p_, w=w, a=a, kc=kc, t=t, ncols=ncols: e.matmul(p_[:, 0:ncols], lhsT=a[:, kc, t * 128:(t + 1) * 128], rhs=w[:, kc, 0:ncols],
                                                                                                     start=(kc == 0), stop=(kc == KC - 1)),
                                     reads=[a.r] + wr, writes=[p_.r], ms=(kc == KC - 1))
                            run_pending()
                            r = blk["epi"](ctx, blk, t, tb, [p_])
                            if r is not None:
                                pending.append(r)
            while pending:
                run_pending()

    def resid_alloc(ph):
        return {"xt": [ph.sb(f"xt{j}", [128, 512]) for j in range(3)], "n": [0]}

    def make_resid_epi(i, kind):
        def epi(ctx, blk, f, tb, ps):
            fc = blk["c0"] // 128 + f
            cnd = 0 if tb < 4 else 1
            xt = ctx["xt"][ctx["n"][0] % 3]; k = ctx["n"][0] % 3
            ctx["n"][0] += 1
            dst = X_fm[fc * 128:(fc + 1) * 128, tb * 512:(tb + 1) * 512]
            S.dma("sp", f"ld_xt{k}", xt[:], dst, writes=[xt.r])
            S.op("dve", lambda e: e.scalar_tensor_tensor(out=xt[:], in0=ps[0][:], scalar=mod_col(i, kind, fc, cnd), in1=xt[:], op0=ALU.mult, op1=ALU.add),
                 reads=[ps[0].r, xt.r], writes=[xt.r])
            S.dma("sp", f"st_xt{k}", dst, xt[:], reads=[xt.r])
            return None
        return epi

    def wblocks(W, c_lo, c_hi, step, mode, epi):
        out = []
        for c0 in range(c_lo, c_hi, step):
            n = min(step, c_hi - c0)
            out.append({"parts": [W[:, c0:c0 + n]], "mode": mode, "epi": epi, "c0": c0 - c_lo})
        return out

    def inproj_alloc(ph):
        ctx = {"ot": [ph.sb(f"ot{j}", [128, 512]) for j in range(3)], "otb": [ph.sb(f"otb{j}", [128, 512], BF16) for j in range(3)], "n": [0],
               "dtb": ph.sb("dtb", [128, 128]), "t": ph.sb("tt", [128, 128])}
        S.dma("sp", "ld_dtb", ctx["dtb"][:], ssd_dt_bias[0, :].partition_broadcast(128), writes=[ctx["dtb"].r])
        return ctx

    def z_epi(ctx, blk, t, tb, ps):
        k = ctx["n"][0] % 3; ot = ctx["ot"][k]; ctx["n"][0] += 1
        S.op("act", lambda e: e.activation(out=ot[:], in_=ps[0][:], func=AF.Silu), reads=[ps[0].r], writes=[ot.r])
        r0 = tb * 512 + t * 128
        S.dma("sp", f"st_ot{k}", SZ_tm[r0:r0 + 128, blk["c0"]:blk["c0"] + 512], ot[:], reads=[ot.r])

    def u_epi(ctx, blk, f, tb, ps):
        k = ctx["n"][0] % 3; ot = ctx["otb"][k]; ctx["n"][0] += 1
        if ctx["n"][0] % 2 == 0:
            S.op("dve", lambda e: e.tensor_copy(out=ot[:], in_=ps[0][:]), reads=[ps[0].r], writes=[ot.r])
        else:
            S.op("act", lambda e: e.copy(out=ot[:], in_=ps[0][:]), reads=[ps[0].r], writes=[ot.r])
        r0 = blk["c0"] + f * 128
        if tb < 4:
            S.dma("sp", f"st_otb{k}", U_fm[r0:r0 + 128, 2 + tb * 512:2 + (tb + 1) * 512], ot[:], reads=[ot.r])
        else:
            c0 = 2052 + (tb - 4) * 516
            S.dma("sp", f"st_otb{k}", U_fm[r0:r0 + 128, c0:c0 + 516].rearrange("p (j i) -> p j i", i=258)[:, :, 0:256],
                  ot[:].rearrange("p (j i) -> p j i", i=256), reads=[ot.r])

    def dt_epi(ctx, blk, t, tb, ps):
        k = ctx["n"][0] % 3; ot = ctx["ot"][k]; ctx["n"][0] += 1
        tt_ = ctx["t"]
        S.op("dve", lambda e: e.tensor_tensor(out=tt_[:], in0=ps[0][:, 0:128], in1=ctx["dtb"][:], op=ALU.add), reads=[ps[0].r, ctx["dtb"].r], writes=[tt_.r])
        S.op("act", lambda e: e.activation(out=tt_[:], in_=tt_[:], func=AF.Exp), reads=[tt_.r], writes=[tt_.r])
        S.op("act", lambda e: e.activation(out=ot[:, 0:128], in_=tt_[:], func=AF.Ln, bias=1.0), reads=[tt_.r], writes=[ot.r])
        r0 = tb * 512 + t * 128
        S.dma("sp", f"st_ot{k}", DT_tm[r0:r0 + 128, :], ot[:, 0:128], reads=[ot.r])

    def ssd_layer(i):
        norm_phase("n0m", i, 0)
        Win = ssd_w_in[0]
        blocks = (wblocks(Win[:, 0:4096], 0, 4096, 512, "TM", z_epi)
                  + wblocks(Win[:, 4096:10240], 0, 6144, 512, "FM", u_epi)
                  + wblocks(Win[:, 10240:10368], 0, 128, 128, "TM", dt_epi))
        gemm("inp", H_fm, 16, blocks, 512, inproj_alloc)

        with Phase(S, "cnv") as ph:
            ub = [ph.sb(f"u{j}", [128, TPAD], BF16) for j in range(3)]
            xq = [ph.sb(f"xq{j}", [128, T // 128, 4, 128], BF16) for j in range(2)]
            dg = [ph.sb(f"dg{j}", [128, 5, 128], BF16) for j in range(2)]
            pcv = [ph.ps(f"pcv{j}", [128, 512]) for j in range(4)]
            npc = 0
            for cc in range(48):
                u = ub[cc % 3]; x4 = xq[(cc // 4) % 2]; dgc = dg[cc % 2]; qi = cc % 4
                S.dma("sp", f"ld_u{cc % 3}", u[:], U_fm[cc * 128:(cc + 1) * 128, :], writes=[u.r])
                S.op("dve", lambda e: e.memset(u[:, 0:2], 0.0), writes=[u.r])
                S.op("dve", lambda e: e.memset(u[:, 2050:2050 + 4 * 258].rearrange("p (q i) -> p q i", i=258)[:, :, 0:2], 0.0), writes=[u.r])
                S.op("dve", lambda e: e.memset(u[:, TPAD - 2:TPAD], 0.0), writes=[u.r])
                for k in range(5):
                    S.op("dve", lambda e: e.tensor_scalar(out=dgc[:, k, :], in0=ident_bf[:], scalar1=cw[:, k, cc:cc + 1], scalar2=None, op0=ALU.mult), writes=[dgc.r])
                blks = [(jb * 512, 512, jb * 4) for jb in range(4)] + [(2050 + 258 * j, 256, 16 + 2 * j) for j in range(NPR)]
                for (u0, n, c0) in blks:
                    p_ = pcv[npc % 4]; npc += 1
                    for k in range(5):
                        S.op("pe", lambda e: e.matmul(p_[:, 0:n], lhsT=dgc[:, k, :], rhs=u[:, u0 + k:u0 + k + n], start=(k == 0), stop=(k == 4)),
                             reads=[dgc.r, u.r], writes=[p_.r], ms=(k == 4))
                    S.op("act", lambda e: e.activation(out=x4[:, c0:c0 + n // 128, qi, :], in_=p_[:, 0:n].rearrange("p (c t) -> p c t", t=128), func=AF.Silu,
                                                       bias=cw[:, 5, cc:cc + 1], scale=1.0), reads=[p_.r], writes=[x4.r])
                if qi == 3:
                    c4 = cc - 3
                    sk = (cc // 4) % 2
                    S.dma("sp", f"st_xq{sk}", XC_cm[:, :, c4:c4 + 4, :].rearrange("c p q t -> p c (q t)"),
                          x4[:].rearrange("p c q t -> p c (q t)"), reads=[x4.r])

        with Phase(S, "cum") as ph:
            A_b = ph.sb("A_b", [128, 128])
            S.dma("sp", "ld_Ab", A_b[:], ssd_a_log[0, :].partition_broadcast(128), writes=[A_b.r])
            S.op("act", lambda e: e.activation(out=A_b[:], in_=A_b[:], func=AF.Exp), reads=[A_b.r], writes=[A_b.r])
            S.op("dve", lambda e: e.tensor_scalar(out=A_b[:], in0=A_b[:], scalar1=-1.0, scalar2=None, op0=ALU.mult), reads=[A_b.r], writes=[A_b.r])
            dts = [ph.sb(f"dt{j}", [128, 128]) for j in range(2)]
            dta = [ph.sb(f"dta{j}", [128, 128]) for j in range(2)]
            acs = [ph.sb(f"acs{j}", [128, 128]) for j in range(2)]
            acf = [ph.sb(f"acf{j}", [64, 2, 128]) for j in range(2)]
            wg = [ph.sb(f"wg{j}", [128, 128]) for j in range(2)]
            et = [ph.sb(f"et{j}", [128, 128]) for j in range(2)]
            p_acs = [ph.ps(f"pacs{j}", [128, 128]) for j in range(2)]
            p_tot = [ph.ps(f"ptot{j}", [128, 128]) for j in range(2)]
            p_acf = [ph.ps(f"pacf{j}", [64, 2, 128]) for j in range(2)]
            tris = [tri_f, tri_b]
            for c in range(T // 128):
                j = c % 2
                r0 = c * 128
                S.dma("sp", f"ld_dt{j}", dts[j][:], DT_tm[r0:r0 + 128, :], writes=[dts[j].r])
                S.op("dve", lambda e, j=j: e.tensor_tensor(out=dta[j][:], in0=dts[j][:], in1=A_b[:], op=ALU.mult), reads=[dts[j].r, A_b.r], writes=[dta[j].r])
                for d in range(2):
                    S.op("pe", lambda e, j=j, d=d: e.matmul(p_acs[j][:, d * 64:(d + 1) * 64], lhsT=tris[d][:], rhs=dta[j][:, d * 64:(d + 1) * 64], start=True, stop=True),
                         reads=[dta[j].r], writes=[p_acs[j].r], ms=(d == 1))
                for d in range(2):
                    S.op("pe", lambda e, j=j, d=d: e.matmul(p_tot[j][:, d * 64:(d + 1) * 64], lhsT=ones_f[:], rhs=dta[j][:, d * 64:(d + 1) * 64], start=True, stop=True),
                         reads=[dta[j].r], writes=[p_tot[j].r], ms=(d == 1))
                for d in range(2):
                    S.op("pe", lambda e, j=j, d=d: e.matmul(p_acf[j][:, d, :], lhsT=dta[j][:, d * 64:(d + 1) * 64], rhs=tris[d][:], start=True, stop=True),
                         reads=[dta[j].r], writes=[p_acf[j].r], ms=(d == 1))
                S.op("act", lambda e, j=j: e.copy(out=acs[j][:], in_=p_acs[j][:]), reads=[p_acs[j].r], writes=[acs[j].r])
                S.op("act", lambda e, j=j: e.copy(out=acf[j][:], in_=p_acf[j][:]), reads=[p_acf[j].r], writes=[acf[j].r])
                S.op("act", lambda e, j=j: e.activation(out=et[j][:], in_=p_tot[j][:], func=AF.Exp), reads=[p_tot[j].r], writes=[et[j].r])
                S.op("dve", lambda e, j=j: e.tensor_tensor(out=wg[j][:], in0=p_tot[j][:], in1=acs[j][:], op=ALU.subtract), reads=[p_tot[j].r, acs[j].r], writes=[wg[j].r])
                S.op("act", lambda e, j=j: e.activation(out=wg[j][:], in_=wg[j][:], func=AF.Exp), reads=[wg[j].r], writes=[wg[j].r])
                S.op("dve", lambda e, j=j: e.tensor_tensor(out=wg[j][:], in0=wg[j][:], in1=dts[j][:], op=ALU.mult), reads=[wg[j].r, dts[j].r], writes=[wg[j].r])
                S.dma("sp", f"st_acs{j}", ACS_tm[r0:r0 + 128, :], acs[j][:], reads=[acs[j].r])
                S.dma("sp", f"st_acf{j}", ACS_cm[c].rearrange("d h t -> h d t"), acf[j][:], reads=[acf[j].r])
                S.dma("sp", f"st_wg{j}", WGT_tm[r0:r0 + 128, :], wg[j][:], reads=[wg[j].r])
                S.dma("sp", f"st_et{j}", ETOT[r0:r0 + 128, :], et[j][:], reads=[et[j].r])

        seqs = [(0, 16, None)] + [(16 + 2 * j, 2, j) for j in range(NPR)]
        Yv = Y_fm.rearrange("(cc p) t -> p cc t", p=128)
        for d in range(2):
            with Phase(S, f"ssd{d}") as ph:
                xfm = ph.sb("xfm", [128, 32, 128], BF16)
                xT2 = [ph.sb(f"xT{j}", [128, DI], BF16) for j in range(2)]
                bcm2 = [ph.sb(f"bcm{j}", [128, 16, 128], BF16) for j in range(2)]
                BT2 = [ph.sb(f"BT{j}", [128, 1024], BF16) for j in range(2)]
                cbTm2 = [ph.sb(f"cbTm{j}", [128, 8, 128], BF16) for j in range(2)]
                xw2 = [ph.sb(f"xw{j}", [128, DI], BF16) for j in range(2)]
                sm2 = [{k: ph.sb(f"{k}{j}", [128, 128]) for k in ("dt", "acs", "wgt", "etot", "nb", "eacs", "lndt")} for j in range(2)]
                bcs = [ph.sb(f"bcs{j}", [128, 8, 128]) for j in range(3)]
                Eb_ = [ph.sb(f"E{j}", [128, 8, 128], BF16) for j in range(2)]
                MT = [ph.sb(f"MT{j}", [128, 8, 128], BF16) for j in range(2)]
                tmp = [ph.sb(f"tmp{j}", [128, 512]) for j in range(2)]
                yg = [ph.sb(f"yg{j}", [128, 512]) for j in range(3)]
                ST = ph.sb("ST", [128, DI]); STb2 = [ph.sb(f"STb{j}", [128, DI], BF16) for j in range(2)]
                y0 = ph.sb("y0", [128, DI])
                if d == 0:
                    dskb = ph.sb("dskb", [128, 128])
                    Dg = ph.sb("Dg", [128, 64, 128], BF16)
                    S.dma("sp", "ld_dsk", dskb[:], ssd_d[0, :].partition_broadcast(128), writes=[dskb.r])
                    S.op("dve", lambda e: e.tensor_tensor(out=dskb[:, 0:64], in0=dskb[:, 0:64], in1=dskb[:, 64:128], op=ALU.add), reads=[dskb.r], writes=[dskb.r])
                    for h in range(64):
                        S.op("dve", lambda e: e.tensor_scalar(out=Dg[:, h, :], in0=ident_bf[:], scalar1=dskb[:, h:h + 1], scalar2=None, op0=ALU.mult), reads=[dskb.r], writes=[Dg.r])
                else:
                    gb = ph.sb("gb", [128, DI], BF16)
                    szg = [ph.sb(f"szg{j}", [128, 512]) for j in range(3)]
                    yn2 = [ph.sb(f"yn{j}", [128, DI], BF16) for j in range(2)]
                    ynT = ph.sb("ynT", [128, 32, 128], BF16)
                    ss = ph.sb("ss", [128, 8]); sd = ph.sb("sd", [128, 8]); rs = ph.sb("rs", [128, 8])
                    junk = ph.sb("junk", [128, 512], BF16)
                    S.dma("pool", "ld_gb", gb[:], ssd_norm[0, :].partition_broadcast(128), writes=[gb.r])
                ptr = [ph.ps(f"ptr{j}", [128, 8, 128], BF16) for j in range(2)]
                pcb = [ph.ps(f"pcb{j}", [128, 4, 128]) for j in range(2)]
                py = [ph.ps(f"py{j}", [128, 512]) for j in range(2)]
                pyi2 = [ph.ps(f"pyi{j}", [128, 512]) for j in range(2)]
                trn = [0]
                mask = tri_f_bf if d == 0 else tri_b_bf
                st_in = sf_in if d == 0 else sb_in
                st_out = nf_out if d == 0 else nb_out
                cnt_ = {"g": 0, "st": 0}

                def evac_copy(dst_ap, p_, dst_res, k):
                    if k % 2 == 0:
                        S.op("dve", lambda e: e.tensor_copy(out=dst_ap, in_=p_[:]), reads=[p_.r], writes=[dst_res])
                    else:
                        S.op("act", lambda e: e.copy(out=dst_ap, in_=p_[:]), reads=[p_.r], writes=[dst_res])

                chunks = []
                for (c_first, nch, pj) in seqs:
                    order = list(range(c_first, c_first + nch))
                    if d == 1:
                        order = order[::-1]
                    for ci, c in enumerate(order):
                        chunks.append({"c": c, "first": ci == 0, "last": ci == nch - 1, "pj": pj,
                                       "need_state": (pj is not None) or (ci < nch - 1)})
                for n_, ck in enumerate(chunks):
                    ck["cs"] = n_ % 2

                def seq_init(ck):
                    STb = STb2[cnt_["st"] % 2]
                    if ck["pj"] is None:
                        y0v = y0[:].rearrange("p (k n) -> p k n", n=128)
                        S.dma("sp", "ld_y0", y0v, st_in.rearrange("(k p) n -> p k n", p=128), writes=[y0.r])
                        for q in range(8):
                            p_ = pcb[q % 2]
                            for j in range(4):
                                k = q * 4 + j
                                S.op("pe", lambda e: e.transpose(out=p_[:, j, :], in_=y0[:, k * 128:(k + 1) * 128], identity=ident_f[:]),
                                     reads=[y0.r], writes=[p_.r], ms=(j == 3))
                            S.op("dve", lambda e: e.tensor_copy(out=ST[:, q * 512:(q + 1) * 512].rearrange("p (j d) -> p j d", d=128), in_=p_[:]), reads=[p_.r], writes=[ST.r])
                        S.op("act", lambda e: e.copy(out=STb[:], in_=ST[:]), reads=[ST.r], writes=[STb.r])
                    else:
                        S.op("dve", lambda e: e.memset(ST[:], 0.0), writes=[ST.r])
                        S.op("dve", lambda e: e.memset(STb[:], 0.0), writes=[STb.r])

                def seq_final(ck):
                    pj = ck["pj"]
                    for q in range(8):
                        p_ = pcb[q % 2]
                        for j in range(4):
                            k = q * 4 + j
                            S.op("pe", lambda e: e.transpose(out=p_[:, j, :], in_=ST[:, k * 128:(k + 1) * 128], identity=ident_f[:]),
                                 reads=[ST.r], writes=[p_.r], ms=(j == 3))
                        S.op("dve", lambda e: e.tensor_copy(out=y0[:, q * 512:(q + 1) * 512].rearrange("p (j d) -> p j d", d=128), in_=p_[:]), reads=[p_.r], writes=[y0.r])
                    S.dma("sp", "st_y0", st_out[pj].rearrange("(k p) n -> p k n", p=128), y0[:].rearrange("p (k n) -> p k n", n=128), reads=[y0.r])

                def preamble(ck):
                    c = ck["c"]; cs = ck["cs"]; r0 = c * 128
                    bcm = bcm2[cs]; BT = BT2[cs]; cbTm = cbTm2[cs]; xw = xw2[cs]; sm = sm2[cs]; xT = xT2[cs]
                    S.dma("sp", "ld_xfm", xfm[:], XC_cm[c, :, 0:32, :], writes=[xfm.r])
                    S.dma("sp", f"ld_bcm{cs}", bcm[:], XC_cm[c, :, 32:48, :], writes=[bcm.r])
                    for k, src in (("dt", DT_tm), ("acs", ACS_tm), ("wgt", WGT_tm), ("etot", ETOT)):
                        S.dma("sp", f"ld_{k}{cs}", sm[k][:], src[r0:r0 + 128, :], writes=[sm[k].r])
                    S.op("act", lambda e: e.activation(out=sm["lndt"][:], in_=sm["dt"][:], func=AF.Ln), reads=[sm["dt"].r], writes=[sm["lndt"].r])
                    S.op("dve", lambda e: e.tensor_tensor(out=sm["nb"][:], in0=sm["lndt"][:], in1=sm["acs"][:], op=ALU.subtract), reads=[sm["lndt"].r, sm["acs"].r], writes=[sm["nb"].r])
                    S.op("act", lambda e: e.activation(out=sm["eacs"][:], in_=sm["acs"][:], func=AF.Exp), reads=[sm["acs"].r], writes=[sm["eacs"].r])
                    for q in range(4):
                        p_ = ptr[trn[0] % 2]; trn[0] += 1
                        for j in range(8):
                            cc = q * 8 + j
                            S.op("pe", lambda e: e.transpose(out=p_[:, j, :], in_=xfm[:, cc, :], identity=ident_bf[:]),
                                 reads=[xfm.r], writes=[p_.r], ms=(j == 7))
                        evac_copy(xT[:, q * 1024:(q + 1) * 1024].rearrange("p (j d) -> p j d", d=128), p_, xT.r, q)
                    p_ = ptr[trn[0] % 2]; trn[0] += 1
                    for g in range(8):
                        S.op("pe", lambda e: e.transpose(out=p_[:, g, :], in_=bcm[:, g, :], identity=ident_bf[:]),
                             reads=[bcm.r], writes=[p_.r], ms=(g == 7))
                    evac_copy(BT[:].rearrange("p (j d) -> p j d", d=128), p_, BT.r, 1)
                    for q in range(2):
                        for j in range(4):
                            g = q * 4 + j
                            S.op("pe", lambda e: e.matmul(pcb[q][:, j, :], lhsT=bcm[:, g, :], rhs=bcm[:, 8 + g, :], start=True, stop=True),
                                 reads=[bcm.r], writes=[pcb[q].r], ms=(j == 3))
                        S.op("dve", lambda e: e.tensor_tensor(out=cbTm[:, q * 4:(q + 1) * 4, :], in0=pcb[q][:], in1=mask[:].unsqueeze(1).to_broadcast([128, 4, 128]), op=ALU.mult),
                             reads=[pcb[q].r], writes=[cbTm.r])
                    if ck["need_state"]:
                        S.op("dve", lambda e: e.tensor_tensor(out=xw[:].rearrange("p (h q) -> p h q", q=64), in0=xT[:].rearrange("p (h q) -> p h q", q=64),
                                                              in1=sm["wgt"][:, d * 64:(d + 1) * 64].unsqueeze(2).to_broadcast([128, 64, 64]), op=ALU.mult),
                             reads=[xT.r, sm["wgt"].r], writes=[xw.r])

                def state_update(ck):
                    cs = ck["cs"]
                    BT = BT2[cs]; xw = xw2[cs]; sm = sm2[cs]
                    if ck["need_state"]:
                        STn = STb2[(cnt_["st"] + 1) % 2]
                        S.op("dve", lambda e: e.tensor_tensor(out=ST[:].rearrange("p (h q) -> p h q", q=64), in0=ST[:].rearrange("p (h q) -> p h q", q=64),
                                                              in1=sm["etot"][:, d * 64:(d + 1) * 64].unsqueeze(2).to_broadcast([128, 64, 64]), op=ALU.mult),
                             reads=[ST.r, sm["etot"].r], writes=[ST.r])
                        for g in range(8):
                            gc0 = g * 512
                            pst = pcb[g % 2]
                            pstv = pst[:].rearrange("p j d -> p (j d)")
                            S.op("pe", lambda e: e.matmul(pstv, lhsT=BT[:, g * 128:(g + 1) * 128], rhs=xw[:, gc0:gc0 + 512], start=True, stop=True),
                                 reads=[BT.r, xw.r], writes=[pst.r], ms=True)
                            S.op("dve", lambda e: e.tensor_tensor(out=ST[:, gc0:gc0 + 512], in0=pstv, in1=ST[:, gc0:gc0 + 512], op=ALU.add), reads=[pst.r, ST.r], writes=[ST.r])
                        S.op("act", lambda e: e.copy(out=STn[:], in_=ST[:]), reads=[ST.r], writes=[STn.r])
                        cnt_["st"] += 1
                    if ck["last"]:
                        if ck["pj"] is not None:
                            seq_final(ck)
                        if not ck["need_state"]:
                            cnt_["st"] += 1

                def g_loads(it):
                    ck, g = it["ck"], it["g"]
                    c = ck["c"]; r0 = c * 128; gc0 = g * 512
                    k3 = it["i"] % 3
                    S.dma("sp", f"ld_bcs{k3}", bcs[k3][:], ACS_cm[c, d, g * 8:(g + 1) * 8, :].partition_broadcast(128), writes=[bcs[k3].r])
                    if d == 1:
                        S.dma("sp", f"ld_szg{k3}", szg[k3][:], SZ_tm[r0:r0 + 128, gc0:gc0 + 512], writes=[szg[k3].r])
                        S.dma("sp", f"ld_yg{k3}", yg[k3][:], Y0_tm[r0:r0 + 128, gc0:gc0 + 512], writes=[yg[k3].r])

                def g_exps(it):
                    ck, g = it["ck"], it["g"]
                    sm = sm2[ck["cs"]]; k = it["i"] % 2; k3 = it["i"] % 3
                    hc0 = d * 64 + g * 8
                    for h in range(8):
                        S.op("act", lambda e: e.activation(out=Eb_[k][:, h, :], in_=bcs[k3][:, h, :], func=AF.Exp, bias=sm["nb"][:, hc0 + h:hc0 + h + 1], scale=1.0),
                             reads=[bcs[k3].r, sm["nb"].r], writes=[Eb_[k].r], ms=(h == 7))

                def g_mt(it):
                    ck, g = it["ck"], it["g"]
                    cbTm = cbTm2[ck["cs"]]; k = it["i"] % 2
                    S.op("dve", lambda e: e.scalar_tensor_tensor(out=MT[k][:], in0=Eb_[k][:], scalar=1e30, in1=cbTm[:, g, :].unsqueeze(1).to_broadcast([128, 8, 128]), op0=ALU.min, op1=ALU.mult),
                         reads=[Eb_[k].r, cbTm.r], writes=[MT[k].r])

                def g_pe(it):
                    ck, g = it["ck"], it["g"]
                    cs = ck["cs"]; bcm = bcm2[cs]; xT = xT2[cs]; STb = ck["STb"]
                    k = it["i"] % 2
                    gc0 = g * 512
                    pyi = pyi2[k]; p_ = py[k]
                    S.op("pe", lambda e: e.matmul(pyi[:], lhsT=bcm[:, 8 + g, :], rhs=STb[:, gc0:gc0 + 512], start=True, stop=True), reads=[bcm.r, STb.r], writes=[pyi.r])
                    for h in range(8):
                        hh = g * 8 + h
                        S.op("pe", lambda e: e.matmul(p_[:, h * 64:(h + 1) * 64], lhsT=MT[k][:, h, :], rhs=xT[:, hh * 64:(hh + 1) * 64], start=True, stop=(d == 1)),
                             reads=[MT[k].r, xT.r], writes=[p_.r], ms=(d == 1 and h == 7))
                        if d == 0:
                            S.op("pe", lambda e: e.matmul(p_[:, h * 64:(h + 1) * 64], lhsT=Dg[:, hh, :], rhs=xT[:, hh * 64:(hh + 1) * 64], start=False, stop=True),
                                 reads=[xT.r, Dg.r], writes=[p_.r], ms=(h == 7))

                def g_evac(it):
                    ck, g = it["ck"], it["g"]
                    c = ck["c"]; cs = ck["cs"]; r0 = c * 128
                    sm = sm2[cs]
                    k = it["i"] % 2; k3 = it["i"] % 3
                    gc0 = g * 512; hc0 = d * 64 + g * 8
                    pyi = pyi2[k]; p_ = py[k]; tm = tmp[k]; y_ = yg[k3]
                    S.op("dve", lambda e: e.tensor_tensor(out=tm[:].rearrange("p (h q) -> p h q", q=64), in0=pyi[:].rearrange("p (h q) -> p h q", q=64),
                                                         in1=sm["eacs"][:, hc0:hc0 + 8].unsqueeze(2).to_broadcast([128, 8, 64]), op=ALU.mult),
                         reads=[pyi.r, sm["eacs"].r], writes=[tm.r])
                    if d == 0:
                        S.op("dve", lambda e: e.tensor_tensor(out=y_[:], in0=p_[:], in1=tm[:], op=ALU.add), reads=[p_.r, tm.r], writes=[y_.r])
                        S.dma("sp", f"st_yg{k3}", Y0_tm[r0:r0 + 128, gc0:gc0 + 512], y_[:], reads=[y_.r])
                    else:
                        sz_ = szg[k3]
                        S.op("dve", lambda e: e.tensor_tensor(out=tm[:], in0=p_[:], in1=tm[:], op=ALU.add), reads=[p_.r, tm.r], writes=[tm.r])
                        S.op("dve", lambda e: e.tensor_tensor(out=y_[:], in0=tm[:], in1=y_[:], op=ALU.add), reads=[tm.r, y_.r], writes=[y_.r])
                        S.op("dve", lambda e: e.tensor_tensor(out=sz_[:], in0=y_[:], in1=sz_[:], op=ALU.mult), reads=[y_.r, sz_.r], writes=[sz_.r])
                        S.op("act", lambda e: e.activation(out=junk[:], in_=sz_[:], func=AF.Square, accum_out=ss[:, g:g + 1]), reads=[sz_.r], writes=[junk.r, ss.r])
                        S.op("act", lambda e: e.activation(out=sd[:, g:g + 1], in_=ss[:, g:g + 1], func=AF.Ln, bias=EPS, scale=1.0 / 512), reads=[ss.r], writes=[sd.r])
                        S.op("act", lambda e: e.activation(out=rs[:, g:g + 1], in_=sd[:, g:g + 1], func=AF.Exp, scale=-0.5), reads=[sd.r], writes=[rs.r])

                def g_evac_b(it):
                    if d == 0:
                        return
                    ck, g = it["ck"], it["g"]
                    k3 = it["i"] % 3; gc0 = g * 512
                    sz_ = szg[k3]; yn = yn2[ck["cs"]]
                    S.op("dve", lambda e: e.scalar_tensor_tensor(out=yn[:, gc0:gc0 + 512], in0=sz_[:], scalar=rs[:, g:g + 1], in1=gb[:, gc0:gc0 + 512],
                                                              op0=ALU.mult, op1=ALU.mult), reads=[sz_.r, rs.r, gb.r], writes=[yn.r])

                def chunk_post(ck):
                    if d == 0:
                        return
                    c = ck["c"]; cs = ck["cs"]; r0 = c * 128
                    yn = yn2[cs]
                    for q in range(4):
                        p_ = ptr[trn[0] % 2]; trn[0] += 1
                        for j in range(8):
                            cc = q * 8 + j
                            S.op("pe", lambda e: e.transpose(out=p_[:, j, :], in_=yn[:, cc * 128:(cc + 1) * 128], identity=ident_bf[:]),
                                 reads=[yn.r], writes=[p_.r], ms=(j == 7))
                        evac_copy(ynT[:, q * 8:(q + 1) * 8, :], p_, ynT.r, q)
                    S.dma("sp", "st_ynT", Yv[:, :, r0:r0 + 128], ynT[:], reads=[ynT.r])

                items = []
                for ck in chunks:
                    for g in range(8):
                        items.append({"ck": ck, "g": g, "i": len(items)})
                NI = len(items)
                seq_init(chunks[0])
                preamble(chunks[0])
                g_loads(items[0]); g_loads(items[1])
                g_exps(items[0]); g_mt(items[0])
                for n_, ck in enumerate(chunks):
                    nxt = chunks[n_ + 1] if n_ + 1 < len(chunks) else None
                    ck["STb"] = STb2[cnt_["st"] % 2]
                    for g in range(8):
                        ii = n_ * 8 + g
                        if g == 1:
                            state_update(ck)
                        if g == 2 and nxt is not None and nxt["first"]:
                            seq_init(nxt)
                        if g == 2 and nxt is not None:
                            preamble(nxt)
                        if g > 0:
                            g_evac_b(items[ii - 1])
                        if ii + 2 < NI:
                            g_loads(items[ii + 2])
                        if ii + 1 < NI:
                            g_exps(items[ii + 1])
                        g_pe(items[ii])
                        g_evac(items[ii])
                        if ii + 1 < NI:
                            g_mt(items[ii + 1])
                    g_evac_b(items[n_ * 8 + 7])
                    chunk_post(ck)
        gemm("opr", Y_fm, 32, wblocks(ssd_w_out[0], 0, D, 512, "FM", make_resid_epi(i, 2)), 512, resid_alloc)

    def ffn(i):
        norm_phase(f"nf{i}", i, 1)
        Wgu = ffn_w_gu[i]

        def alloc(ph):
            return {"sg": [ph.sb(f"sg{j}", [128, 512]) for j in range(2)], "ot": [ph.sb(f"ot{j}", [128, 512], BF16) for j in range(3)], "n": [0]}

        def epi(ctx, blk, f, tb, ps):
            k = ctx["n"][0]; ctx["n"][0] += 1
            sg = ctx["sg"][k % 2]; ot = ctx["ot"][k % 3]
            S.op("act", lambda e: e.activation(out=sg[:], in_=ps[0][:], func=AF.Silu), reads=[ps[0].r], writes=[sg.r])
            S.op("dve", lambda e: e.tensor_tensor(out=ot[:], in0=ps[1][:], in1=sg[:], op=ALU.mult), reads=[ps[1].r, sg.r], writes=[ot.r])
            r0 = blk["c0"] + f * 128
            S.dma("sp", f"st_ot{k % 3}", A_fm[r0:r0 + 128, tb * 512:(tb + 1) * 512], ot[:], reads=[ot.r])

        blocks = [{"parts": [Wgu[:, c0:c0 + 256], Wgu[:, DFF + c0:DFF + c0 + 256]], "mode": "PAIR", "epi": epi, "c0": c0} for c0 in range(0, DFF, 256)]
        gemm(f"gu{i}", H_fm, 16, blocks, 512, alloc)
        gemm(f"dn{i}", A_fm, 44, wblocks(ffn_w_down[i], 0, D, 512, "FM", make_resid_epi(i, 5)), 512, resid_alloc)

    def attn_layer(i):
        norm_phase("n1m", i, 0)
        Wq = attn_w_qkv[0]

        def alloc(ph):
            ctx = {"sq": [ph.sb(f"sq{j}", [128, 512], BF16) for j in range(2)], "qn": [ph.sb(f"qn{j}", [128, 512]) for j in range(2)],
                   "sd": ph.sb("sd", [128, 512]), "rs": ph.sb("rs", [128, 512]), "t1": ph.sb("t1", [128, 512]), "t2": ph.sb("t2", [128, 512]),
                   "ot": [ph.sb(f"ot{j}", [128, 512], BF16) for j in range(2)], "vf": [ph.sb(f"vf{j}", [128, 512]) for j in range(2)],
                   "kk": ph.sb("kk", [128, 4, 128]),
                   "cos": ph.sb("cos", [128, TS]), "sin": ph.sb("sin", [128, TS]), "pm": ph.sb("pm", [128, 128]),
                   "pss": [ph.ps(f"pss{j}", [128, 512]) for j in range(2)], "n": [0]}
            S.dma("sp", "ld_cos", ctx["cos"][:], rope_cos, writes=[ctx["cos"].r])
            S.dma("sp", "ld_sin", ctx["sin"][:], rope_sin, writes=[ctx["sin"].r])
            S.dma("sp", "ld_pm", ctx["pm"][:], rope_pm, writes=[ctx["pm"].r])
            return ctx

        def qk_epi(ctx, blk, f, tb, ps):
            k = ctx["n"][0]; ctx["n"][0] += 1
            is_k = blk.get("is_k", False)
            hc = blk["c0"] // 128 + f
            sq = ctx["sq"][k % 2]; qn = ctx["qn"][k % 2]; ot = ctx["ot"][k % 2]; pss = ctx["pss"][k % 2]
            p0 = ps[0]
            S.op("act", lambda e: e.activation(out=sq[:], in_=p0[:], func=AF.Square), reads=[p0.r], writes=[sq.r])
            dst = (KT_fm if is_k else QT_fm)[hc * 128:(hc + 1) * 128, tb * 512:(tb + 1) * 512]

            def stage2():
                S.op("pe", lambda e: e.matmul(pss[:], lhsT=ones_bf[:], rhs=sq[:], start=True, stop=True), reads=[sq.r], writes=[pss.r])
                S.op("act", lambda e: e.activation(out=ctx["sd"][:], in_=pss[:], func=AF.Ln, bias=EPS, scale=1.0 / 128), reads=[pss.r], writes=[ctx["sd"].r])
                S.op("act", lambda e: e.activation(out=ctx["rs"][:], in_=ctx["sd"][:], func=AF.Exp, scale=-0.5), reads=[ctx["sd"].r], writes=[ctx["rs"].r])
                S.op("dve", lambda e: e.scalar_tensor_tensor(out=qn[:], in0=p0[:], scalar=qkg[:, (1 if is_k else 0):(2 if is_k else 1)], in1=ctx["rs"][:], op0=ALU.mult, op1=ALU.mult),
                     reads=[p0.r, ctx["rs"].r], writes=[qn.r])
                if tb < 4:
                    def stage3():
                        S.op("pe", lambda e: e.matmul(pss[:], lhsT=ctx["pm"][:], rhs=qn[:], start=True, stop=True), reads=[qn.r, ctx["pm"].r], writes=[pss.r])
                        S.op("dve", lambda e: e.tensor_tensor(out=ctx["t1"][:], in0=qn[:], in1=ctx["cos"][:, tb * 512:(tb + 1) * 512], op=ALU.mult), reads=[qn.r, ctx["cos"].r], writes=[ctx["t1"].r])
                        S.op("dve", lambda e: e.tensor_tensor(out=ctx["t2"][:], in0=pss[:], in1=ctx["sin"][:, tb * 512:(tb + 1) * 512], op=ALU.mult), reads=[pss.r, ctx["sin"].r], writes=[ctx["t2"].r])
                        S.op("dve", lambda e: e.tensor_tensor(out=ot[:], in0=ctx["t1"][:], in1=ctx["t2"][:], op=ALU.add), reads=[ctx["t1"].r, ctx["t2"].r], writes=[ot.r])
                        S.dma("sp", f"st_ot{k % 2}", dst, ot[:], reads=[ot.r])
                        return None
                    return stage3
                S.op("act", lambda e: e.copy(out=ot[:], in_=qn[:]), reads=[qn.r], writes=[ot.r])
                S.dma("sp", f"st_ot{k % 2}", dst, ot[:], reads=[ot.r])
                if is_k:
                    def stage3k():
                        for j in range(4):
                            S.op("pe", lambda e, j=j: e.transpose(out=pss[:].rearrange("p (j d) -> p j d", d=128)[:, j, :], in_=qn[:, j * 128:(j + 1) * 128], identity=ident_f[:]),
                                 reads=[qn.r], writes=[pss.r], ms=(j == 3))
                        S.op("dve", lambda e: e.tensor_copy(out=ctx["kk"][:], in_=pss[:].rearrange("p (j d) -> p j d", d=128)), reads=[pss.r], writes=[ctx["kk"].r])
                        S.dma("sp", "st_kk", nk_out[(tb - 4) * 512:(tb - 3) * 512, hc * 128:(hc + 1) * 128].rearrange("(j p) d -> p j d", p=128), ctx["kk"][:], reads=[ctx["kk"].r])
                        return None
                    return stage3k
                return None
            return stage2

        def v_epi(ctx, blk, t, tb, ps):
            k = ctx["n"][0]; ctx["n"][0] += 1
            vf = ctx["vf"][k % 2]; ot = ctx["ot"][k % 2]
            r0 = tb * 512 + t * 128
            S.op("act", lambda e: e.copy(out=vf[:], in_=ps[0][:]), reads=[ps[0].r], writes=[vf.r])
            S.op("dve", lambda e: e.tensor_copy(out=ot[:], in_=ps[0][:]), reads=[ps[0].r], writes=[ot.r])
            S.dma("sp", f"st_ot{k % 2}", V_tm[r0:r0 + 128, :], ot[:], reads=[ot.r])
            if tb >= 4:
                S.dma("sp", f"st_vf{k % 2}", nv_out[r0 - TS:r0 - TS + 128, :], vf[:], reads=[vf.r])

        kb_ = wblocks(Wq[:, 2048:2560], 0, 512, 512, "FM", qk_epi)
        for b_ in kb_:
            b_["is_k"] = True
        blocks = wblocks(Wq[:, 0:2048], 0, 2048, 512, "FM", qk_epi) + kb_ + wblocks(Wq[:, 2560:3072], 0, 512, 512, "TM", v_epi)
        gemm("qkv", H_fm, 16, blocks, 512, alloc)

        SCALE = 128 ** -0.5
        with Phase(S, "att") as ph:
            KTs = ph.sb("KTs", [128, 2304], BF16); Vs = ph.sb("Vs", [128, 18, 128], BF16)
            QTs = [ph.sb(f"QTs{j}", [128, TS], BF16) for j in range(2)]
            Es = [ph.sb(f"E{j}", [128, 512], BF16) for j in range(3)]
            rec = ph.sb("rec", [128, 512]); ots = [ph.sb(f"ot{j}", [128, 512], BF16) for j in range(2)]
            ckf = ph.sb("ckf", [128, 2, 512])
            psc = [ph.ps(f"psc{j}", [128, 512]) for j in range(3)]
            pot = [ph.ps(f"pot{j}", [128, 512]) for j in range(2)]
            psm = [ph.ps(f"psm{j}", [128, 512]) for j in range(2)]
            ptc = ph.ps("ptc", [128, 128])
            cnt = {"q": 0, "e": 0, "s": 0, "o": 0}
            S.dma("sp", "ld_ckf", ckf[:], ck_in.rearrange("(j p) c -> p j c", p=128), writes=[ckf.r])

            def attn_block(qt, q0, N, nkb, dst, kb0=0):
                o = cnt["o"]; cnt["o"] += 1
                po = pot[o % 2]; pm2 = psm[o % 2]; ot = ots[o % 2]

                def sc(kb):
                    p_ = psc[cnt["s"] % 3]; cnt["s"] += 1
                    S.op("pe", lambda e: e.matmul(p_[:, 0:N], lhsT=KTs[:, (kb0 + kb) * 128:(kb0 + kb + 1) * 128], rhs=qt[:, q0:q0 + N], start=True, stop=True),
                         reads=[KTs.r, qt.r], writes=[p_.r])
                    return p_
                pq = [sc(0)]
                if nkb > 1:
                    pq.append(sc(1))
                for kb in range(nkb):
                    p_cur = pq.pop(0)
                    E = Es[cnt["e"] % 3]; cnt["e"] += 1
                    S.op("act", lambda e, E=E, p_cur=p_cur: e.activation(out=E[:, 0:N], in_=p_cur[:, 0:N], func=AF.Exp, scale=SCALE), reads=[p_cur.r], writes=[E.r])
                    S.op("pe", lambda e, E=E, kb=kb: e.matmul(po[:, 0:N], lhsT=Vs[:, kb0 + kb, :], rhs=E[:, 0:N], start=(kb == 0), stop=(kb == nkb - 1)),
                         reads=[Vs.r, E.r], writes=[po.r], ms=False)
                    S.op("pe", lambda e, E=E, kb=kb: e.matmul(pm2[:, 0:N], lhsT=ones_bf[:], rhs=E[:, 0:N], start=(kb == 0), stop=(kb == nkb - 1)),
                         reads=[E.r], writes=[pm2.r, po.r], ms=True)
                    if kb + 2 < nkb:
                        pq.append(sc(kb + 2))
                S.op("dve", lambda e: e.reciprocal(out=rec[:, 0:N], in_=pm2[:, 0:N]), reads=[pm2.r], writes=[rec.r])
                S.op("dve", lambda e: e.tensor_tensor(out=ot[:, 0:N], in0=po[:, 0:N], in1=rec[:, 0:N], op=ALU.mult), reads=[po.r, rec.r], writes=[ot.r])
                S.dma("sp", f"st_ao{o % 2}", dst, ot[:, 0:N], reads=[ot.r])

            for kvh in range(4):
                for j in range(2):
                    S.op("pe", lambda e, j=j, kvh=kvh: e.transpose(out=ptc[:], in_=ckf[:, j, kvh * 128:(kvh + 1) * 128], identity=ident_f[:]), reads=[ckf.r], writes=[ptc.r])
                    S.op("dve", lambda e, j=j: e.tensor_copy(out=KTs[:, j * 128:(j + 1) * 128], in_=ptc[:]), reads=[ptc.r], writes=[KTs.r])
                S.dma("sp", "ld_kts", KTs[:, 256:2304], KT_fm[kvh * 128:(kvh + 1) * 128, 0:TS], writes=[KTs.r])
                S.dma("pool", "ld_vsc", Vs[:, 0:2, :], cv_in[:, kvh * 128:(kvh + 1) * 128].rearrange("(j p) d -> p j d", p=128), writes=[Vs.r])
                S.dma("sp", "ld_vs", Vs[:, 2:18, :], V_tm[0:TS, kvh * 128:(kvh + 1) * 128].rearrange("(j p) d -> p j d", p=128), writes=[Vs.r])
                for r in range(4):
                    h = kvh * 4 + r
                    qt = QTs[cnt["q"] % 2]; k = cnt["q"] % 2; cnt["q"] += 1
                    S.dma("sp", f"ld_qt{k}", qt[:], QT_fm[h * 128:(h + 1) * 128, 0:TS], writes=[qt.r])
                    for qb in range(4):
                        attn_block(qt, qb * 512, 512, 18, AO_fm[h * 128:(h + 1) * 128, qb * 512:(qb + 1) * 512])
            def p_scores(qt, pj):
                p_ = psc[cnt["s"] % 3]; cnt["s"] += 1
                for kb in range(2):
                    S.op("pe", lambda e: e.matmul(p_[:, kb * LP:(kb + 1) * LP], lhsT=KTs[:, (2 * pj + kb) * 128:(2 * pj + kb + 1) * 128], rhs=qt[:, pj * LP:(pj + 1) * LP], start=True, stop=True),
                         reads=[KTs.r, qt.r], writes=[p_.r], ms=(kb == 1))
                return p_

            def p_rest(p_, pj, dst):
                o = cnt["o"]; cnt["o"] += 1
                po = pot[o % 2]; pm2 = psm[o % 2]; ot = ots[o % 2]
                E = Es[cnt["e"] % 3]; cnt["e"] += 1
                S.op("act", lambda e: e.activation(out=E[:], in_=p_[:], func=AF.Exp, scale=SCALE), reads=[p_.r], writes=[E.r])
                for kb in range(2):
                    S.op("pe", lambda e: e.matmul(po[:, 0:LP], lhsT=Vs[:, 2 * pj + kb, :], rhs=E[:, kb * LP:(kb + 1) * LP], start=(kb == 0), stop=(kb == 1)),
                         reads=[Vs.r, E.r], writes=[po.r], ms=False)
                for kb in range(2):
                    S.op("pe", lambda e: e.matmul(pm2[:, 0:LP], lhsT=ones_bf[:], rhs=E[:, kb * LP:(kb + 1) * LP], start=(kb == 0), stop=(kb == 1)),
                         reads=[E.r], writes=[pm2.r, po.r], ms=(kb == 1))
                S.op("dve", lambda e: e.reciprocal(out=rec[:, 0:LP], in_=pm2[:, 0:LP]), reads=[pm2.r], writes=[rec.r])
                S.op("dve", lambda e: e.tensor_tensor(out=ot[:, 0:LP], in0=po[:, 0:LP], in1=rec[:, 0:LP], op=ALU.mult), reads=[po.r, rec.r], writes=[ot.r])
                S.dma("sp", f"st_ao{o % 2}", dst, ot[:, 0:LP], reads=[ot.r])

            work = []
            for kvh in range(4):
                for r in range(4):
                    for pj in range(NPR):
                        work.append((kvh, r, pj))
            pend = None
            for (kvh, r, pj) in work:
                h = kvh * 4 + r
                if r == 0 and pj == 0:
                    if pend is not None:
                        p_rest(*pend); pend = None
                    S.dma("sp", "ld_kts", KTs[:, 0:NPR * LP], KT_fm[kvh * 128:(kvh + 1) * 128, TS:T], writes=[KTs.r])
                    S.dma("sp", "ld_vs", Vs[:, 0:8, :], V_tm[TS:T, kvh * 128:(kvh + 1) * 128].rearrange("(j p) d -> p j d", p=128), writes=[Vs.r])
                if pj == 0:
                    qt = QTs[cnt["q"] % 2]; k = cnt["q"] % 2; cnt["q"] += 1
                    S.dma("sp", f"ld_qt{k}", qt[:, 0:NPR * LP], QT_fm[h * 128:(h + 1) * 128, TS:T], writes=[qt.r])
                p_ = p_scores(qt, pj)
                if pend is not None:
                    p_rest(*pend)
                pend = (p_, pj, AO_fm[h * 128:(h + 1) * 128, TS + pj * LP:TS + (pj + 1) * LP])
            p_rest(*pend)

        gemm("opj", AO_fm, 16, wblocks(attn_w_o[0], 0, D, 512, "FM", make_resid_epi(i, 2)), 512, resid_alloc)

    def final_phase():
        with Phase(S, "fin") as ph:
            NT = 256
            NB = T // NT
            xb = [ph.sb(f"x{j}", [128, 16, NT]) for j in range(2)]
            sq = ph.sb("sq", [128, 16, NT], BF16)
            t1 = ph.sb("t1", [128, 16, NT]); t2b = [ph.sb(f"t2{j}", [128, 16, NT]) for j in range(2)]
            sd = ph.sb("sd", [128, NT]); R = ph.sb("R", [128, NT])
            yo = [ph.sb(f"yo{j}", [128, D]) for j in range(2)]
            pss = [ph.ps(f"ss{j}", [128, NT]) for j in range(2)]
            pp = [ph.ps(f"pp{j}", [128, 4, 128]) for j in range(6)]
            cn = {"n": 0, "m": 0}

            def stage_a(tb):
                x = xb[tb % 2]; ps_ = pss[tb % 2]; t2 = t2b[tb % 2]
                S.dma("sp", f"ld_nx{tb % 2}", x[:], Xv[:, :, tb * NT:(tb + 1) * NT], writes=[x.r])
                S.op("act", lambda e: e.activation(out=sq[:], in_=x[:], func=AF.Square), reads=[x.r], writes=[sq.r])
                for kc in range(16):
                    S.op("pe", lambda e: e.matmul(ps_[:], lhsT=ones_bf[:], rhs=sq[:, kc, :], start=(kc == 0), stop=(kc == 15)),
                         reads=[sq.r], writes=[ps_.r], ms=(kc == 15))
                S.op("act", lambda e: e.activation(out=sd[:], in_=ps_[:], func=AF.Ln, bias=EPS, scale=1.0 / D), reads=[ps_.r], writes=[sd.r])
                S.op("act", lambda e: e.activation(out=R[:], in_=sd[:], func=AF.Exp, scale=-0.5), reads=[sd.r], writes=[R.r])
                S.op("dve", lambda e: e.tensor_tensor(out=t1[:], in0=x[:], in1=gfin[:, :].unsqueeze(2).to_broadcast([128, 16, NT]), op=ALU.mult), reads=[x.r], writes=[t1.r])
                S.op("dve", lambda e: e.tensor_tensor(out=t2[:], in0=t1[:], in1=R[:].unsqueeze(1).to_broadcast([128, 16, NT]), op=ALU.mult), reads=[t1.r, R.r], writes=[t2.r])

            def stage_b(tb):
                t2 = t2b[tb % 2]
                for hf in range(NT // 128):
                    y = yo[cn["m"] % 2]; km = cn["m"] % 2; cn["m"] += 1
                    for q in range(4):
                        p_ = pp[cn["n"] % 6]; cn["n"] += 1
                        for j in range(4):
                            kc = q * 4 + j
                            S.op("pe", lambda e: e.transpose(out=p_[:, j, :], in_=t2[:, kc, hf * 128:(hf + 1) * 128], identity=ident_f[:]),
                                 reads=[t2.r], writes=[p_.r], ms=(j == 3))
                        dsl = y[:, q * 512:(q + 1) * 512].rearrange("p (j d) -> p j d", d=128)
                        if q % 2 == 0:
                            S.op("dve", lambda e: e.tensor_copy(out=dsl, in_=p_[:]), reads=[p_.r], writes=[y.r])
                        else:
                            S.op("act", lambda e: e.copy(out=dsl, in_=p_[:]), reads=[p_.r], writes=[y.r])
                    tok0 = tb * NT + hf * 128
                    dst = ys_out[tok0:tok0 + 128, :] if tok0 < TS else yp_out[tok0 - TS:tok0 - TS + 128, :]
                    S.dma("sp", f"st_yo{km}", dst, y[:], reads=[y.r], writes=[OUTR])

            stage_a(0)
            for tb in range(NB):
                if tb + 1 < NB:
                    stage_a(tb + 1)
                stage_b(tb)

    OUTR = DRes("outputs")
    try:
        convert_phase()
        ssd_layer(0)
        ffn(0)
        attn_layer(1)
        ffn(1)
        final_phase()
    except StopBuild:
        pass
    S.finish([OUTR])
    return nc


def _rope_consts():
    inv = (10000.0 ** (-np.arange(0, 64, 2, dtype=np.float32) / 64)).astype(np.float32)
    t = np.arange(TS)
    row = (t // 64).astype(np.float32); col = (t % 64).astype(np.float32)
    cos = np.zeros((128, TS), np.float32); sin = np.zeros((128, TS), np.float32)
    pm = np.zeros((128, 128), np.float32)
    for m in range(128):
        pos = row if m < 64 else col
        ang = (pos * inv[m % 32]).astype(np.float32)
        first = (m % 64) < 32
        cos[m] = np.cos(ang)
        sin[m] = -np.sin(ang) if first else np.sin(ang)
        pm[(m + 32) if first else (m - 32), m] = 1.0
    return cos, sin, pm


_PROG = None


def kernel(**inp):
    global _PROG
    if _PROG is None:
        _PROG = build_program()
    nc = _PROG
    f = lambda a: np.ascontiguousarray(np.asarray(a, dtype=np.float32))
    cos, sin, pm = _rope_consts()
    shared = {k: f(inp[k]) for k in ("w_mod", "b_mod", "norm_mix", "norm_ffn", "ssd_w_in", "ssd_conv_w", "ssd_conv_b", "ssd_norm",
                                     "ssd_w_out", "attn_w_qkv", "attn_q_norm", "attn_k_norm", "attn_w_o", "ffn_w_gu", "ffn_w_down")}
    shared["ssd_a_log"] = f(inp["ssd_a_log"]).reshape(1, 128)
    shared["ssd_dt_bias"] = f(inp["ssd_dt_bias"]).reshape(1, 128)
    shared["ssd_d"] = f(inp["ssd_d"]).reshape(1, 128)
    shared["final_norm"] = f(inp["final_norm"]).reshape(1, D)
    shared["cctx"] = f(inp["c_ctx"]).reshape(1, D)
    shared["rope_cos"] = cos; shared["rope_sin"] = sin; shared["rope_pm"] = pm
    xs = f(inp["x_sample"]); xp = f(inp["x_prompt"]); sf = f(inp["state_ssd_fwd"]); sb = f(inp["state_ssd_bwd"])
    ck = f(inp["cache_k"]); cv = f(inp["cache_v"]); c = f(inp["c"])
    in_maps = []
    for b in range(8):
        m = dict(shared)
        m["xs"] = xs[b]; m["xp"] = xp[4 * b:4 * b + 4].reshape(NPR * LP, D)
        m["sf"] = sf[b, 0].reshape(DI, 128); m["sbw"] = sb[b, 0].reshape(DI, 128)
        m["ck"] = ck[b, 0].reshape(256, 512); m["cv"] = cv[b, 0].reshape(256, 512)
        m["c"] = c[b:b + 1]
        in_maps.append(m)
    res = run_bass_kernel_spmd(nc, in_maps, core_ids=list(range(8)))
    R = res.results
    y_prompt = np.concatenate([r["yp"].reshape(NPR, LP, D) for r in R], 0)
    y_sample = np.stack([r["ys"] for r in R], 0)
    new_f = np.concatenate([r["nf"].reshape(NPR, 1, 64, 64, 128) for r in R], 0)
    new_b = np.concatenate([r["nb"].reshape(NPR, 1, 64, 64, 128) for r in R], 0)
    new_k = np.concatenate([r["nk"].reshape(NPR, 1, LP, 4, 128) for r in R], 0)
    new_v = np.concatenate([r["nv"].reshape(NPR, 1, LP, 4, 128) for r in R], 0)
    return tuple(np.ascontiguousarray(a.astype(np.float32)) for a in (y_prompt, y_sample, new_f, new_b, new_k, new_v))
```

```python
import numpy as np
import concourse.bass as bass
import concourse.mybir as mybir
from concourse.bass_utils import run_bass_kernel_spmd
from contextlib import ExitStack
import os
KSKIP = os.environ.get('KSKIP', '')

F32 = mybir.dt.float32
BF16 = mybir.dt.bfloat16
AF = mybir.ActivationFunctionType
ALU = mybir.AluOpType
AX = mybir.AxisListType


class Res:
    __slots__ = ("name", "w", "r")

    def __init__(self, name):
        self.name = name
        self.w = None
        self.r = {}

    def set_w(self, tok):
        self.w = tok
        self.r = {}


class PRes(Res):
    __slots__ = ()
    excl = True


class DRes:
    __slots__ = ("name", "w", "r")

    def __init__(self, name):
        self.name = name
        self.w = {}
        self.r = {}

    def set_w(self, tok):
        k, v = tok
        if self.w.get(k, 0) < v:
            self.w[k] = v


class _Rec:
    def __init__(self):
        self.calls = []

    def __getattr__(self, name):
        def f(*a, **k):
            self.calls.append((name, a, k))
            return None
        return f


def _replayer(calls):
    def replay(e):
        r = None
        for (n, a, k) in calls:
            r = getattr(e, n)(*a, **k)
        return r
    return replay


class Sched:
    ENG = ("pe", "act", "dve", "pool", "sp")

    def __init__(self, nc):
        self.nc = nc
        self.ops = {e: [] for e in self.ENG}
        self.sems = {}
        self.cnt = {}
        self.waited = {e: {} for e in self.ENG}
        for e in self.ENG:
            self._sem("E_" + e)
        self.nwaits = 0
        self.dmap = {}

    def _sem(self, key):
        if key not in self.sems:
            self.sems[key] = self.nc.alloc_semaphore(key)
            self.cnt[key] = 0
        return self.sems[key]

    def op(self, eng, fn, reads=(), writes=(), ms=True):
        key = "E_" + eng
        rec = _Rec()
        fn(rec)
        fn = _replayer(rec.calls)
        ex = [r for r in reads if getattr(r, "excl", False)]
        if ex:
            writes = list(writes) + [r for r in ex if r not in writes]
            reads = [r for r in reads if not getattr(r, "excl", False)]
        waits = self._deps_compute(eng, reads, writes)
        tokv = self.cnt[key] + 1
        if ms:
            self.cnt[key] = tokv
        self.ops[eng].append((waits, fn, (key, 1) if ms else None))
        tok = (key, tokv)
        for r in reads:
            if r.r.get(key, 0) < tokv:
                r.r[key] = tokv
        for w in writes:
            w.set_w(tok)
        return tok

    def _deps_compute(self, eng, reads, writes, strict=False):
        deps = {}
        mykey = "E_" + eng
        def add(tok, allow_self):
            allow_self = allow_self or strict
            if tok is None:
                return
            k, v = tok
            if k == mykey and not allow_self:
                return
            if deps.get(k, 0) < v:
                deps[k] = v
        for r in reads:
            if isinstance(r.w, dict):
                for k, v in r.w.items():
                    add((k, v), True)
                continue
            add(r.w, eng != "pe")
        for w in writes:
            if not isinstance(w.w, dict):
                add(w.w, False)
            for k, v in w.r.items():
                add((k, v), False)
        out = []
        for k, v in deps.items():
            if self.waited[eng].get(k, 0) >= v:
                continue
            assert self.cnt[k] >= v, f"wait on future milestone {k}>={v} (have {self.cnt[k]}) from {eng}"
            self.waited[eng][k] = v
            out.append((k, v))
        return out

    def dma(self, eng, semkey, out, in_, reads=(), writes=(), **kw):
        if semkey not in self.dmap:
            self.dmap[semkey] = f"D{len(self.dmap)}"
        semkey = self.dmap[semkey]
        self._sem(semkey)
        waits = self._deps_compute(eng, reads, writes, strict=True)
        self.cnt[semkey] += 16
        tok = (semkey, self.cnt[semkey])
        self.ops[eng].append((waits, lambda e: e.dma_start(out=out, in_=in_, **kw), (semkey, 16)))
        for r in reads:
            if r.r.get(semkey, 0) < tok[1]:
                r.r[semkey] = tok[1]
        for w in writes:
            w.set_w(tok)
        return tok

    def barrier(self):
        self._sem("BAR")
        for e in ("pe", "act", "dve", "pool"):
            real = [o for o in self.ops[e] if o[1] is not None]
            if real:
                assert real[-1][2] is not None, f"last op on {e} is not a milestone"
        waits = []
        for k, c in self.cnt.items():
            if k == "BAR":
                continue
            if c > self.waited["sp"].get(k, 0):
                waits.append((k, c))
        self.cnt["BAR"] += 1
        v = self.cnt["BAR"]
        bar = self.sems["BAR"]
        self.ops["sp"].append((waits, lambda e: e.sem_inc(bar, 1), None))
        for e in self.ENG:
            if e != "sp":
                self.ops[e].append(([("BAR", v)], None, None))
            self.waited[e] = dict(self.cnt)
        self.dmap = {}

    def finish(self, final_res):
        waits = self._deps_compute("sp", final_res, ())
        self.ops["sp"].append((waits, None, None))
        nc = self.nc
        emap = {"pe": "tensor", "act": "scalar", "dve": "vector", "pool": "gpsimd", "sp": "sync"}
        with nc.Block() as block:
            for e in self.ENG:
                ops = self.ops[e]
                sems = self.sems

                def body(engine, ops=ops):
                    for waits, fn, inc in ops:
                        for k, v in waits:
                            engine.wait_ge(sems[k], v)
                            self.nwaits += 1
                        if fn is None:
                            continue
                        ins = fn(engine)
                        if inc is not None:
                            ins.then_inc(sems[inc[0]], inc[1])
                getattr(block, emap[e])(body)

D = 2048
T = 3072
TS = 2048
NPR = 4
LP = 256
DI = 4096
DFF = 5632
EPS = 1e-6
TPAD = T + 12


class Buf:
    def __init__(self, h, name):
        self.h = h
        self.r = Res(name)

    def __getitem__(self, k):
        return self.h[k]


class StopBuild(Exception):
    pass


class Phase:
    stop_after = None
    stopped = False

    def __init__(self, S, name):
        self.S = S
        self.nc = S.nc
        self.name = name

    def __enter__(self):
        if Phase.stopped:
            raise StopBuild()
        self.es = ExitStack()
        return self

    def sb(self, name, shape, dt=F32):
        h = self.es.enter_context(self.nc.sbuf_tensor(f"{self.name}_{name}", shape, dt))
        return Buf(h, name)

    def ps(self, name, shape, dt=F32):
        full = [128, 512] if dt == F32 else [128, 1024]
        h = self.es.enter_context(self.nc.psum_tensor(f"{self.name}_{name}", full, dt))
        n = 1
        for d_ in shape[1:]:
            n *= d_
        assert n <= full[1]
        v = h[0:shape[0], 0:n]
        if len(shape) == 3:
            v = v.rearrange("p (a b) -> p a b", b=shape[2])
        b = Buf(v, name)
        b.r = PRes(name)
        return b

    def __exit__(self, *a):
        if KSKIP == 'mem':
            print("phase", self.name, "sbuf bytes remaining", self.nc.sbuf_bytes_remaining)
        if a[0] is None:
            self.S.barrier()
            if Phase.stop_after == self.name:
                Phase.stopped = True
        self.es.close()
        return False


def build_program(stop_after=None, debug=False):
    Phase.stop_after = stop_after
    Phase.stopped = False
    nc = bass.Bass("TRN2", target_bir_lowering=False)
    S = Sched(nc)

    def din(name, shape):
        return nc.dram_tensor(name, shape, F32, kind="ExternalInput").ap()

    def dout(name, shape):
        return nc.dram_tensor(name, shape, F32, kind="ExternalOutput").ap()

    def scr(name, shape, dt):
        if debug:
            return nc.dram_tensor(name, shape, dt, kind="ExternalOutput").ap()
        return nc.dram_tensor(name, shape, dt).ap()

    xs_in = din("xs", [TS, D]); xp_in = din("xp", [NPR * LP, D])
    sf_in = din("sf", [DI, 128]); sb_in = din("sbw", [DI, 128])
    ck_in = din("ck", [256, 512]); cv_in = din("cv", [256, 512])
    c_in = din("c", [1, D]); cctx_in = din("cctx", [1, D])
    w_mod = din("w_mod", [2, D, 6 * D]); b_mod = din("b_mod", [2, 6 * D])
    norm_mix = din("norm_mix", [2, D]); norm_ffn = din("norm_ffn", [2, D])
    ssd_w_in = din("ssd_w_in", [1, D, 10368]); ssd_conv_w = din("ssd_conv_w", [1, 5, 6144])
    ssd_conv_b = din("ssd_conv_b", [1, 6144]); ssd_a_log = din("ssd_a_log", [1, 128])
    ssd_dt_bias = din("ssd_dt_bias", [1, 128]); ssd_d = din("ssd_d", [1, 128])
    ssd_norm = din("ssd_norm", [1, DI]); ssd_w_out = din("ssd_w_out", [1, DI, D])
    attn_w_qkv = din("attn_w_qkv", [1, D, 3072]); attn_q_norm = din("attn_q_norm", [1, 128])
    attn_k_norm = din("attn_k_norm", [1, 128]); attn_w_o = din("attn_w_o", [1, D, D])
    ffn_w_gu = din("ffn_w_gu", [2, D, 2 * DFF]); ffn_w_down = din("ffn_w_down", [2, DFF, D])
    final_norm = din("final_norm", [1, D])
    rope_cos = din("rope_cos", [128, TS]); rope_sin = din("rope_sin", [128, TS]); rope_pm = din("rope_pm", [128, 128])
    yp_out = dout("yp", [NPR * LP, D]); ys_out = dout("ys", [TS, D])
    nf_out = dout("nf", [NPR, DI, 128]); nb_out = dout("nb", [NPR, DI, 128])
    nk_out = dout("nk", [NPR * LP, 512]); nv_out = dout("nv", [NPR * LP, 512])
    X_fm = scr("X_fm", [D, T], F32)
    H_fm = scr("H_fm", [D, T], BF16)
    SZ_tm = scr("SZ_tm", [T, DI], F32)
    U_fm = scr("U_fm", [6144, TPAD], BF16)
    XC_cm = scr("XC_cm", [T // 128, 128, 48, 128], BF16)
    DT_tm = scr("DT_tm", [T, 128], F32)
    ACS_tm = scr("ACS_tm", [T, 128], F32)
    WGT_tm = scr("WGT_tm", [T, 128], F32)
    ETOT = scr("ETOT", [T, 128], F32)
    ACS_cm = scr("ACS_cm", [T // 128, 2, 64, 128], F32)
    Y0_tm = scr("Y0_tm", [T, DI], F32)
    Y_fm = scr("Y_fm", [DI, T], BF16)
    A_fm = scr("A_fm", [DFF, T], BF16)
    QT_fm = scr("QT_fm", [D, T], BF16)
    KT_fm = scr("KT_fm", [512, T], BF16)
    V_tm = scr("V_tm", [T, 512], BF16)
    AO_fm = scr("AO_fm", [D, T], BF16)

    def dbg(name, buf, shape, dt):
        if not debug:
            return
        t = nc.dram_tensor("dbg_" + name, shape, dt, kind="ExternalOutput").ap()
        S.dma("sp", "dbg_" + name, t, buf[:], reads=[buf.r])

    def gbuf(name, shape, dt=F32):
        return Buf(nc.alloc_sbuf_tensor("g_" + name, shape, dt), name)

    ident_bf = gbuf("ident_bf", [128, 128], BF16); ident_f = gbuf("ident_f", [128, 128])
    ones_bf = gbuf("ones_bf", [128, 128], BF16); ones_f = gbuf("ones_f", [128, 128])
    tri_f = gbuf("tri_f", [128, 128]); tri_b = gbuf("tri_b", [128, 128])
    tri_f_bf = gbuf("tri_f_bf", [128, 128], BF16); tri_b_bf = gbuf("tri_b_bf", [128, 128], BF16)
    modv = gbuf("modv", [128, 2, 96, 2])
    gmix = gbuf("gmix", [128, 2, 16]); gffn = gbuf("gffn", [128, 2, 16]); gfin = gbuf("gfin", [128, 16])
    gs = gbuf("gs", [128, 2, 2, 2, 16])
    cw = gbuf("cw", [128, 6, 48])
    qkg = gbuf("qkg", [128, 2])

    def mk_const():
        def msel(buf, pattern, cm, cmp, val=1.0):
            S.op("pool", lambda e: e.memset(buf[:], val), writes=[buf.r])
            S.op("pool", lambda e: e.affine_select(out=buf[:], in_=buf[:], pattern=pattern, compare_op=cmp,
                                                   fill=0.0, base=0, channel_multiplier=cm),
                 reads=[buf.r], writes=[buf.r])
        msel(ident_bf, [[-1, 128]], 1, ALU.is_equal); msel(ident_f, [[-1, 128]], 1, ALU.is_equal)
        msel(tri_f, [[1, 128]], -1, ALU.is_ge); msel(tri_f_bf, [[1, 128]], -1, ALU.is_ge)
        msel(tri_b, [[-1, 128]], 1, ALU.is_ge); msel(tri_b_bf, [[-1, 128]], 1, ALU.is_ge)
        S.op("pool", lambda e: e.memset(ones_bf[:], 1.0), writes=[ones_bf.r])
        S.op("pool", lambda e: e.memset(ones_f[:], 1.0), writes=[ones_f.r])

    def tok_rows(t0, n):
        if t0 < TS:
            return ("s", t0)
        return ("p", t0 - TS)

    with Phase(S, "su") as ph:
        mk_const()
        S.barrier()
        rowb = ph.sb("rowb", [1, 6144]); pc = ph.ps("pc", [128, 512])
        one11 = ones_f

        def row2col(src_row_ap, n, dst_ap, func=None):
            S.dma("sp", "ld_rowb", rowb[0:1, 0:n * 128], src_row_ap, writes=[rowb.r])
            for j in range(n):
                S.op("pe", lambda e, j=j: e.matmul(pc[:, j:j + 1], lhsT=rowb[0:1, j * 128:(j + 1) * 128],
                                                  rhs=one11[0:1, 0:1], start=True, stop=True),
                     reads=[rowb.r, one11.r], writes=[pc.r], ms=(j == n - 1))
            if func is None:
                S.op("dve", lambda e: e.tensor_copy(out=dst_ap, in_=pc[:, 0:n]), reads=[pc.r], writes=[])
            else:
                S.op("act", lambda e: e.activation(out=dst_ap, in_=pc[:, 0:n], func=func), reads=[pc.r], writes=[])

        for i in range(2):
            row2col(norm_mix[i:i + 1, :], 16, gmix[:, i, :])
            row2col(norm_ffn[i:i + 1, :], 16, gffn[:, i, :])
        row2col(final_norm[0:1, :], 16, gfin[:, :])
        for k in range(5):
            row2col(ssd_conv_w[0, k:k + 1, :], 48, cw[:, k, :])
        row2col(ssd_conv_b[0:1, :], 48, cw[:, 5, :])
        row2col(attn_q_norm[0:1, :], 1, qkg[:, 0:1])
        row2col(attn_k_norm[0:1, :], 1, qkg[:, 1:2])
        sc = ph.sb("sc", [128, 16, 2])
        row2col(c_in[0:1, :], 16, sc[:, :, 0], func=AF.Silu)
        row2col(cctx_in[0:1, :], 16, sc[:, :, 1], func=AF.Silu)
        S.barrier()
        sc_bf = ph.sb("sc_bf", [128, 16, 2], BF16)
        S.op("dve", lambda e: e.tensor_copy(out=sc_bf[:], in_=sc[:]), reads=[sc.r], writes=[sc_bf.r])
        wsl = [ph.sb(f"wm{j}", [128, 16, 512], BF16) for j in range(2)]
        wfl = [ph.sb(f"wf{j}", [128, 16, 512]) for j in range(2)]
        bsl = [ph.sb(f"bm{j}", [1, 512]) for j in range(4)]
        pm_ = ph.ps("pm", [128, 192])
        n = 0
        for i in range(2):
            for cb in range(24):
                bb = bsl[n % 4]
                src = w_mod[i, :, cb * 512:(cb + 1) * 512].rearrange("(kc p) n -> p kc n", p=128)
                if n % 2 == 0:
                    w = wsl[(n // 2) % 2]; rhs_sc = sc_bf
                    S.dma("pool", f"ld_wm{(n // 2) % 2}", w[:], src, writes=[w.r])
                else:
                    w = wfl[(n // 2) % 2]; rhs_sc = sc
                    S.dma("sp", f"ld_wf{(n // 2) % 2}", w[:], src, writes=[w.r])
                S.dma("sp", f"ld_bm{n % 4}", bb[0:1, :], b_mod[i:i + 1, cb * 512:(cb + 1) * 512], writes=[bb.r])
                for f in range(4):
                    m = cb * 4 + f
                    for kc in range(16):
                        S.op("pe", lambda e: e.matmul(pm_[:, m * 2:m * 2 + 2], lhsT=w[:, kc, f * 128:(f + 1) * 128],
                                                     rhs=rhs_sc[:, kc, :], start=(kc == 0), stop=False),
                             reads=[w.r, rhs_sc.r], writes=[pm_.r], ms=False)
                    S.op("pe", lambda e: e.matmul(pm_[:, m * 2:m * 2 + 2], lhsT=bb[0:1, f * 128:(f + 1) * 128],
                                                 rhs=ones_f[0:1, 0:2], start=False, stop=True),
                         reads=[bb.r, ones_f.r], writes=[pm_.r], ms=True)
                n += 1
            S.op("dve", lambda e: e.tensor_copy(out=modv[:, i, :, :], in_=pm_[:].rearrange("p (m c) -> p m c", c=2)),
                 reads=[pm_.r], writes=[modv.r])
        for i in range(2):
            for sub in range(2):
                g = gmix if sub == 0 else gffn
                for cnd in range(2):
                    S.op("dve", lambda e, i=i, sub=sub, cnd=cnd, g=g: e.scalar_tensor_tensor(
                        out=gs[:, i, sub, cnd, :], in0=modv[:, i, sub * 48 + 16:sub * 48 + 32, cnd], scalar=1.0,
                        in1=g[:, i, :], op0=ALU.add, op1=ALU.mult), reads=[modv.r], writes=[gs.r])

    def mod_col(i, kind, chunk, cnd):
        return modv[:, i, kind * 16 + chunk, cnd:cnd + 1]

    Xv = X_fm.rearrange("(kc p) t -> p kc t", p=128)
    Hv = H_fm.rearrange("(kc p) t -> p kc t", p=128)

    def convert_phase():
        with Phase(S, "cv") as ph:
            xin = [ph.sb(f"xin{j}", [128, D]) for j in range(2)]
            xf = [ph.sb(f"xf{j}", [128, 16, 128]) for j in range(2)]
            pp = [ph.ps(f"pp{j}", [128, 4, 128]) for j in range(8)]
            for tt in range(T // 128):
                a = xin[tt % 2]; o = xf[tt % 2]
                src = xs_in[tt * 128:(tt + 1) * 128, :] if tt < 16 else xp_in[(tt - 16) * 128:(tt - 15) * 128, :]
                S.dma("sp", f"ld_xin{tt % 2}", a[:], src, writes=[a.r])
                for q in range(4):
                    p_ = pp[(tt * 4 + q) % 8]
                    for j in range(4):
                        kc = q * 4 + j
                        S.op("pe", lambda e, p_=p_, a=a, kc=kc, j=j: e.transpose(out=p_[:, j, :], in_=a[:, kc * 128:(kc + 1) * 128], identity=ident_f[:]),
                             reads=[a.r], writes=[p_.r], ms=(j == 3))
                    eng = "dve" if q % 2 == 0 else "act"
                    if eng == "dve":
                        S.op("dve", lambda e, p_=p_, o=o, q=q: e.tensor_copy(out=o[:, q * 4:(q + 1) * 4, :], in_=p_[:]), reads=[p_.r], writes=[o.r])
                    else:
                        S.op("act", lambda e, p_=p_, o=o, q=q: e.copy(out=o[:, q * 4:(q + 1) * 4, :], in_=p_[:]), reads=[p_.r], writes=[o.r])
                S.dma("sp", f"st_xf{tt % 2}", Xv[:, :, tt * 128:(tt + 1) * 128], o[:], reads=[o.r])

    def norm_phase(tag, i, sub):
        with Phase(S, tag) as ph:
            NT = 256
            NB = T // NT
            xb = [ph.sb(f"x{j}", [128, 16, NT]) for j in range(2)]
            sq = ph.sb("sq", [128, 16, NT], BF16)
            t1b = [ph.sb(f"t1{j}", [128, 16, NT]) for j in range(2)]
            hb = [ph.sb(f"h{j}", [128, 16, NT], BF16) for j in range(2)]
            sd = ph.sb("sd", [128, NT]); R = ph.sb("R", [128, NT])
            pss = [ph.ps(f"ss{j}", [128, NT]) for j in range(2)]

            def stage_a(tb):
                x = xb[tb % 2]; ps_ = pss[tb % 2]; t1 = t1b[tb % 2]
                S.dma("sp", f"ld_nx{tb % 2}", x[:], Xv[:, :, tb * NT:(tb + 1) * NT], writes=[x.r])
                S.op("act", lambda e: e.activation(out=sq[:], in_=x[:], func=AF.Square), reads=[x.r], writes=[sq.r])
                for kc in range(16):
                    S.op("pe", lambda e: e.matmul(ps_[:], lhsT=ones_bf[:], rhs=sq[:, kc, :], start=(kc == 0), stop=(kc == 15)),
                         reads=[sq.r], writes=[ps_.r], ms=(kc == 15))
                S.op("act", lambda e: e.activation(out=sd[:], in_=ps_[:], func=AF.Ln, bias=EPS, scale=1.0 / D), reads=[ps_.r], writes=[sd.r])
                S.op("act", lambda e: e.activation(out=R[:], in_=sd[:], func=AF.Exp, scale=-0.5), reads=[sd.r], writes=[R.r])
                S.op("dve", lambda e: e.tensor_tensor(out=t1[:], in0=x[:], in1=R[:].unsqueeze(1).to_broadcast([128, 16, NT]), op=ALU.mult),
                     reads=[x.r, R.r], writes=[t1.r])

            def stage_b(tb):
                cnd = 0 if tb * NT < TS else 1
                h = hb[tb % 2]; t1 = t1b[tb % 2]
                for kc in range(16):
                    S.op("act", lambda e: e.activation(out=h[:, kc, :], in_=t1[:, kc, :], func=AF.Identity, scale=gs[:, i, sub, cnd, kc:kc + 1],
                                                       bias=modv[:, i, sub * 48 + kc, cnd:cnd + 1]),
                         reads=[t1.r], writes=[h.r], ms=(kc == 15))
                S.dma("sp", f"st_nh{tb % 2}", Hv[:, :, tb * NT:(tb + 1) * NT], h[:], reads=[h.r])

            stage_a(0)
            for tb in range(NB):
                if tb + 1 < NB:
                    stage_a(tb + 1)
                stage_b(tb)

    def gemm(tag, A_scr, KC, blocks, wcols, alloc_extra):
        Av = A_scr.rearrange("(kc p) t -> p kc t", p=128)
        with Phase(S, tag) as ph:
            resident = (KC == 16)
            asl = [ph.sb(f"a{j}", [128, KC, 512], BF16) for j in range(T // 512 if resident else 2)]
            wsl = [ph.sb(f"w{j}", [128, KC, wcols], BF16) for j in range(2)]
            wres = [[Res(f"w{j}p{q}") for q in range(4)] for j in range(2)]
            psl = [ph.ps(f"ps{j}", [128, 512]) for j in range(6)]
            ctx = alloc_extra(ph) if alloc_extra else None
            pcount = [0]
            pending = []

            def next_ps():
                p_ = psl[pcount[0] % 6]
                pcount[0] += 1
                return p_

            def load_w(bi):
                blk = blocks[bi]; w = wsl[bi % 2]
                off = 0
                for q, part in enumerate(blk["parts"]):
                    n = part.shape[1]
                    S.dma("pool", f"ld_w{bi % 2}_{q}", w[:, :, off:off + n], part.rearrange("(kc p) n -> p kc n", p=128),
                          writes=[wres[bi % 2][q]])
                    off += n

            na = [0]

            def load_a(tb):
                if resident:
                    a = asl[tb]
                    if na[0] < len(asl):
                        S.dma("sp", f"ld_a{tb}", a[:], Av[:, :, tb * 512:(tb + 1) * 512], writes=[a.r])
                    na[0] += 1
                    return a
                a = asl[na[0] % 2]
                S.dma("sp", f"ld_a{na[0] % 2}", a[:], Av[:, :, tb * 512:(tb + 1) * 512], writes=[a.r])
                na[0] += 1
                return a

            def run_pending():
                nxt = []
                for c in pending:
                    r = c()
                    if r is not None:
                        nxt.append(r)
                pending[:] = nxt

            load_w(0)
            NTB = T // 512
            a_next = load_a(0)
            for bi, blk in enumerate(blocks):
                w = wsl[bi % 2]
                wr = wres[bi % 2][:len(blk["parts"])]
                if bi + 1 < len(blocks):
                    load_w(bi + 1)
                ncols = sum(p.shape[1] for p in blk["parts"])
                for tb in range(NTB):
                    a = a_next
                    if tb + 1 < NTB or bi + 1 < len(blocks):
                        a_next = load_a((tb + 1) % NTB)
                    mode = blk["mode"]
                    if mode == "FM":
                        for f in range(ncols // 128):
                            p_ = next_ps()
                            for kc in range(KC):
                                S.op("pe", lambda e, p_=p_, w=w, a=a, kc=kc, f=f: e.matmul(p_[:], lhsT=w[:, kc, f * 128:(f + 1) * 128], rhs=a[:, kc, :],
                                                                                        start=(kc == 0), stop=(kc == KC - 1)),
                                     reads=[a.r] + wr, writes=[p_.r], ms=(kc == KC - 1))
                            run_pending()
                            r = blk["epi"](ctx, blk, f, tb, [p_])
                            if r is not None:
                                pending.append(r)
                    elif mode == "PAIR":
                        half = ncols // 2
                        for f in range(half // 128):
                            pp_ = []
                            for hh in range(2):
                                p_ = next_ps()
                                c0 = hh * half + f * 128
                                for kc in range(KC):
                                    S.op("pe", lambda e, p_=p_, w=w, a=a, kc=kc, c0=c0: e.matmul(p_[:], lhsT=w[:, kc, c0:c0 + 128], rhs=a[:, kc, :],
                                                                                              start=(kc == 0), stop=(kc == KC - 1)),
                                         reads=[a.r] + wr, writes=[p_.r], ms=(kc == KC - 1))
                                pp_.append(p_)
                            run_pending()
                            r = blk["epi"](ctx, blk, f, tb, pp_)
                            if r is not None:
                                pending.append(r)
                    else:
                        for t in range(4):
                            p_ = next_ps()
                            for kc in range(KC):
                                S.op("pe", lambda e, p_=p_, w=w, a=a, kc=kc, t=t, ncols=ncols: e.matmul(p_[:, 0:ncols], lhsT=a[:, kc, t * 128:(t + 1) * 128], rhs=w[:, kc, 0:ncols],
                                                                                                     start=(kc == 0), stop=(kc == KC - 1)),
                                     reads=[a.r] + wr, writes=[p_.r], ms=(kc == KC - 1))
                            run_pending()
                            r = blk["epi"](ctx, blk, t, tb, [p_])
                            if r is not None:
                                pending.append(r)
            while pending:
                run_pending()

    def resid_alloc(ph):
        return {"xt": [ph.sb(f"xt{j}", [128, 512]) for j in range(3)], "n": [0]}

    def make_resid_epi(i, kind):
        def epi(ctx, blk, f, tb, ps):
            fc = blk["c0"] // 128 + f
            cnd = 0 if tb < 4 else 1
            xt = ctx["xt"][ctx["n"][0] % 3]; k = ctx["n"][0] % 3
            ctx["n"][0] += 1
            dst = X_fm[fc * 128:(fc + 1) * 128, tb * 512:(tb + 1) * 512]
            S.dma("sp", f"ld_xt{k}", xt[:], dst, writes=[xt.r])
            S.op("dve", lambda e: e.scalar_tensor_tensor(out=xt[:], in0=ps[0][:], scalar=mod_col(i, kind, fc, cnd), in1=xt[:], op0=ALU.mult, op1=ALU.add),
                 reads=[ps[0].r, xt.r], writes=[xt.r])
            S.dma("sp", f"st_xt{k}", dst, xt[:], reads=[xt.r])
            return None
        return epi

    def wblocks(W, c_lo, c_hi, step, mode, epi):
        out = []
        for c0 in range(c_lo, c_hi, step):
            n = min(step, c_hi - c0)
            out.append({"parts": [W[:, c0:c0 + n]], "mode": mode, "epi": epi, "c0": c0 - c_lo})
        return out

    def inproj_alloc(ph):
        ctx = {"ot": [ph.sb(f"ot{j}", [128, 512]) for j in range(3)], "otb": [ph.sb(f"otb{j}", [128, 512], BF16) for j in range(3)], "n": [0],
               "dtb": ph.sb("dtb", [128, 128]), "t": ph.sb("tt", [128, 128])}
        S.dma("sp", "ld_dtb", ctx["dtb"][:], ssd_dt_bias[0, :].partition_broadcast(128), writes=[ctx["dtb"].r])
        return ctx

    def z_epi(ctx, blk, t, tb, ps):
        k = ctx["n"][0] % 3; ot = ctx["ot"][k]; ctx["n"][0] += 1
        S.op("act", lambda e: e.activation(out=ot[:], in_=ps[0][:], func=AF.Silu), reads=[ps[0].r], writes=[ot.r])
        r0 = tb * 512 + t * 128
        S.dma("sp", f"st_ot{k}", SZ_tm[r0:r0 + 128, blk["c0"]:blk["c0"] + 512], ot[:], reads=[ot.r])

    def u_epi(ctx, blk, f, tb, ps):
        k = ctx["n"][0] % 3; ot = ctx["otb"][k]; ctx["n"][0] += 1
        if ctx["n"][0] % 2 == 0:
            S.op("dve", lambda e: e.tensor_copy(out=ot[:], in_=ps[0][:]), reads=[ps[0].r], writes=[ot.r])
        else:
            S.op("act", lambda e: e.copy(out=ot[:], in_=ps[0][:]), reads=[ps[0].r], writes=[ot.r])
        r0 = blk["c0"] + f * 128
        if tb < 4:
            S.dma("sp", f"st_otb{k}", U_fm[r0:r0 + 128, 2 + tb * 512:2 + (tb + 1) * 512], ot[:], reads=[ot.r])
        else:
            c0 = 2052 + (tb - 4) * 516
            S.dma("sp", f"st_otb{k}", U_fm[r0:r0 + 128, c0:c0 + 516].rearrange("p (j i) -> p j i", i=258)[:, :, 0:256],
                  ot[:].rearrange("p (j i) -> p j i", i=256), reads=[ot.r])

    def dt_epi(ctx, blk, t, tb, ps):
        k = ctx["n"][0] % 3; ot = ctx["ot"][k]; ctx["n"][0] += 1
        tt_ = ctx["t"]
        S.op("dve", lambda e: e.tensor_tensor(out=tt_[:], in0=ps[0][:, 0:128], in1=ctx["dtb"][:], op=ALU.add), reads=[ps[0].r, ctx["dtb"].r], writes=[tt_.r])
        S.op("act", lambda e: e.activation(out=tt_[:], in_=tt_[:], func=AF.Exp), reads=[tt_.r], writes=[tt_.r])
        S.op("act", lambda e: e.activation(out=ot[:, 0:128], in_=tt_[:], func=AF.Ln, bias=1.0), reads=[tt_.r], writes=[ot.r])
        r0 = tb * 512 + t * 128
        S.dma("sp", f"st_ot{k}", DT_tm[r0:r0 + 128, :], ot[:, 0:128], reads=[ot.r])

    def ssd_layer(i):
        norm_phase("n0m", i, 0)
        Win = ssd_w_in[0]
        blocks = (wblocks(Win[:, 0:4096], 0, 4096, 512, "TM", z_epi)
                  + wblocks(Win[:, 4096:10240], 0, 6144, 512, "FM", u_epi)
                  + wblocks(Win[:, 10240:10368], 0, 128, 128, "TM", dt_epi))
        gemm("inp", H_fm, 16, blocks, 512, inproj_alloc)

        with Phase(S, "cnv") as ph:
            ub = [ph.sb(f"u{j}", [128, TPAD], BF16) for j in range(3)]
            xq = [ph.sb(f"xq{j}", [128, T // 128, 4, 128], BF16) for j in range(2)]
            dg = [ph.sb(f"dg{j}", [128, 5, 128], BF16) for j in range(2)]
            pcv = [ph.ps(f"pcv{j}", [128, 512]) for j in range(4)]
            npc = 0
            for cc in range(48):
                u = ub[cc % 3]; x4 = xq[(cc // 4) % 2]; dgc = dg[cc % 2]; qi = cc % 4
                S.dma("sp", f"ld_u{cc % 3}", u[:], U_fm[cc * 128:(cc + 1) * 128, :], writes=[u.r])
                S.op("dve", lambda e: e.memset(u[:, 0:2], 0.0), writes=[u.r])
                S.op("dve", lambda e: e.memset(u[:, 2050:2050 + 4 * 258].rearrange("p (q i) -> p q i", i=258)[:, :, 0:2], 0.0), writes=[u.r])
                S.op("dve", lambda e: e.memset(u[:, TPAD - 2:TPAD], 0.0), writes=[u.r])
                for k in range(5):
                    S.op("dve", lambda e: e.tensor_scalar(out=dgc[:, k, :], in0=ident_bf[:], scalar1=cw[:, k, cc:cc + 1], scalar2=None, op0=ALU.mult), writes=[dgc.r])
                blks = [(jb * 512, 512, jb * 4) for jb in range(4)] + [(2050 + 258 * j, 256, 16 + 2 * j) for j in range(NPR)]
                for (u0, n, c0) in blks:
                    p_ = pcv[npc % 4]; npc += 1
                    for k in range(5):
                        S.op("pe", lambda e: e.matmul(p_[:, 0:n], lhsT=dgc[:, k, :], rhs=u[:, u0 + k:u0 + k + n], start=(k == 0), stop=(k == 4)),
                             reads=[dgc.r, u.r], writes=[p_.r], ms=(k == 4))
                    S.op("act", lambda e: e.activation(out=x4[:, c0:c0 + n // 128, qi, :], in_=p_[:, 0:n].rearrange("p (c t) -> p c t", t=128), func=AF.Silu,
                                                       bias=cw[:, 5, cc:cc + 1], scale=1.0), reads=[p_.r], writes=[x4.r])
                if qi == 3:
                    c4 = cc - 3
                    sk = (cc // 4) % 2
                    S.dma("sp", f"st_xq{sk}", XC_cm[:, :, c4:c4 + 4, :].rearrange("c p q t -> p c (q t)"),
                          x4[:].rearrange("p c q t -> p c (q t)"), reads=[x4.r])

        with Phase(S, "cum") as ph:
            A_b = ph.sb("A_b", [128, 128])
            S.dma("sp", "ld_Ab", A_b[:], ssd_a_log[0, :].partition_broadcast(128), writes=[A_b.r])
            S.op("act", lambda e: e.activation(out=A_b[:], in_=A_b[:], func=AF.Exp), reads=[A_b.r], writes=[A_b.r])
            S.op("dve", lambda e: e.tensor_scalar(out=A_b[:], in0=A_b[:], scalar1=-1.0, scalar2=None, op0=ALU.mult), reads=[A_b.r], writes=[A_b.r])
            dts = [ph.sb(f"dt{j}", [128, 128]) for j in range(2)]
            dta = [ph.sb(f"dta{j}", [128, 128]) for j in range(2)]
            acs = [ph.sb(f"acs{j}", [128, 128]) for j in range(2)]
            acf = [ph.sb(f"acf{j}", [64, 2, 128]) for j in range(2)]
            wg = [ph.sb(f"wg{j}", [128, 128]) for j in range(2)]
            et = [ph.sb(f"et{j}", [128, 128]) for j in range(2)]
            p_acs = [ph.ps(f"pacs{j}", [128, 128]) for j in range(2)]
            p_tot = [ph.ps(f"ptot{j}", [128, 128]) for j in range(2)]
            p_acf = [ph.ps(f"pacf{j}", [64, 2, 128]) for j in range(2)]
            tris = [tri_f, tri_b]
            for c in range(T // 128):
                j = c % 2
                r0 = c * 128
                S.dma("sp", f"ld_dt{j}", dts[j][:], DT_tm[r0:r0 + 128, :], writes=[dts[j].r])
                S.op("dve", lambda e, j=j: e.tensor_tensor(out=dta[j][:], in0=dts[j][:], in1=A_b[:], op=ALU.mult), reads=[dts[j].r, A_b.r], writes=[dta[j].r])
                for d in range(2):
                    S.op("pe", lambda e, j=j, d=d: e.matmul(p_acs[j][:, d * 64:(d + 1) * 64], lhsT=tris[d][:], rhs=dta[j][:, d * 64:(d + 1) * 64], start=True, stop=True),
                         reads=[dta[j].r], writes=[p_acs[j].r], ms=(d == 1))
                for d in range(2):
                    S.op("pe", lambda e, j=j, d=d: e.matmul(p_tot[j][:, d * 64:(d + 1) * 64], lhsT=ones_f[:], rhs=dta[j][:, d * 64:(d + 1) * 64], start=True, stop=True),
                         reads=[dta[j].r], writes=[p_tot[j].r], ms=(d == 1))
                for d in range(2):
                    S.op("pe", lambda e, j=j, d=d: e.matmul(p_acf[j][:, d, :], lhsT=dta[j][:, d * 64:(d + 1) * 64], rhs=tris[d][:], start=True, stop=True),
                         reads=[dta[j].r], writes=[p_acf[j].r], ms=(d == 1))
                S.op("act", lambda e, j=j: e.copy(out=acs[j][:], in_=p_acs[j][:]), reads=[p_acs[j].r], writes=[acs[j].r])
                S.op("act", lambda e, j=j: e.copy(out=acf[j][:], in_=p_acf[j][:]), reads=[p_acf[j].r], writes=[acf[j].r])
                S.op("act", lambda e, j=j: e.activation(out=et[j][:], in_=p_tot[j][:], func=AF.Exp), reads=[p_tot[j].r], writes=[et[j].r])
                S.op("dve", lambda e, j=j: e.tensor_tensor(out=wg[j][:], in0=p_tot[j][:], in1=acs[j][:], op=ALU.subtract), reads=[p_tot[j].r, acs[j].r], writes=[wg[j].r])
                S.op("act", lambda e, j=j: e.activation(out=wg[j][:], in_=wg[j][:], func=AF.Exp), reads=[wg[j].r], writes=[wg[j].r])
                S.op("dve", lambda e, j=j: e.tensor_tensor(out=wg[j][:], in0=wg[j][:], in1=dts[j][:], op=ALU.mult), reads=[wg[j].r, dts[j].r], writes=[wg[j].r])
                S.dma("sp", f"st_acs{j}", ACS_tm[r0:r0 + 128, :], acs[j][:], reads=[acs[j].r])
                S.dma("sp", f"st_acf{j}", ACS_cm[c].rearrange("d h t -> h d t"), acf[j][:], reads=[acf[j].r])
                S.dma("sp", f"st_wg{j}", WGT_tm[r0:r0 + 128, :], wg[j][:], reads=[wg[j].r])
                S.dma("sp", f"st_et{j}", ETOT[r0:r0 + 128, :], et[j][:], reads=[et[j].r])

        seqs = [(0, 16, None)] + [(16 + 2 * j, 2, j) for j in range(NPR)]
        Yv = Y_fm.rearrange("(cc p) t -> p cc t", p=128)
        for d in range(2):
            with Phase(S, f"ssd{d}") as ph:
                xfm = ph.sb("xfm", [128, 32, 128], BF16)
                xT2 = [ph.sb(f"xT{j}", [128, DI], BF16) for j in range(2)]
                bcm2 = [ph.sb(f"bcm{j}", [128, 16, 128], BF16) for j in range(2)]
                BT2 = [ph.sb(f"BT{j}", [128, 1024], BF16) for j in range(2)]
                cbTm2 = [ph.sb(f"cbTm{j}", [128, 8, 128], BF16) for j in range(2)]
                xw2 = [ph.sb(f"xw{j}", [128, DI], BF16) for j in range(2)]
                sm2 = [{k: ph.sb(f"{k}{j}", [128, 128]) for k in ("dt", "acs", "wgt", "etot", "nb", "eacs", "lndt")} for j in range(2)]
                bcs = [ph.sb(f"bcs{j}", [128, 8, 128]) for j in range(3)]
                Eb_ = [ph.sb(f"E{j}", [128, 8, 128], BF16) for j in range(2)]
                MT = [ph.sb(f"MT{j}", [128, 8, 128], BF16) for j in range(2)]
                tmp = [ph.sb(f"tmp{j}", [128, 512]) for j in range(2)]
                yg = [ph.sb(f"yg{j}", [128, 512]) for j in range(3)]
                ST = ph.sb("ST", [128, DI]); STb2 = [ph.sb(f"STb{j}", [128, DI], BF16) for j in range(2)]
                y0 = ph.sb("y0", [128, DI])
                if d == 0:
                    dskb = ph.sb("dskb", [128, 128])
                    Dg = ph.sb("Dg", [128, 64, 128], BF16)
                    S.dma("sp", "ld_dsk", dskb[:], ssd_d[0, :].partition_broadcast(128), writes=[dskb.r])
                    S.op("dve", lambda e: e.tensor_tensor(out=dskb[:, 0:64], in0=dskb[:, 0:64], in1=dskb[:, 64:128], op=ALU.add), reads=[dskb.r], writes=[dskb.r])
                    for h in range(64):
                        S.op("dve", lambda e: e.tensor_scalar(out=Dg[:, h, :], in0=ident_bf[:], scalar1=dskb[:, h:h + 1], scalar2=None, op0=ALU.mult), reads=[dskb.r], writes=[Dg.r])
                else:
                    gb = ph.sb("gb", [128, DI], BF16)
                    szg = [ph.sb(f"szg{j}", [128, 512]) for j in range(3)]
                    yn2 = [ph.sb(f"yn{j}", [128, DI], BF16) for j in range(2)]
                    ynT = ph.sb("ynT", [128, 32, 128], BF16)
                    ss = ph.sb("ss", [128, 8]); sd = ph.sb("sd", [128, 8]); rs = ph.sb("rs", [128, 8])
                    junk = ph.sb("junk", [128, 512], BF16)
                    S.dma("pool", "ld_gb", gb[:], ssd_norm[0, :].partition_broadcast(128), writes=[gb.r])
                ptr = [ph.ps(f"ptr{j}", [128, 8, 128], BF16) for j in range(2)]
                pcb = [ph.ps(f"pcb{j}", [128, 4, 128]) for j in range(2)]
                py = [ph.ps(f"py{j}", [128, 512]) for j in range(2)]
                pyi2 = [ph.ps(f"pyi{j}", [128, 512]) for j in range(2)]
                trn = [0]
                mask = tri_f_bf if d == 0 else tri_b_bf
                st_in = sf_in if d == 0 else sb_in
                st_out = nf_out if d == 0 else nb_out
                cnt_ = {"g": 0, "st": 0}

                def evac_copy(dst_ap, p_, dst_res, k):
                    if k % 2 == 0:
                        S.op("dve", lambda e: e.tensor_copy(out=dst_ap, in_=p_[:]), reads=[p_.r], writes=[dst_res])
                    else:
                        S.op("act", lambda e: e.copy(out=dst_ap, in_=p_[:]), reads=[p_.r], writes=[dst_res])

                chunks = []
                for (c_first, nch, pj) in seqs:
                    order = list(range(c_first, c_first + nch))
                    if d == 1:
                        order = order[::-1]
                    for ci, c in enumerate(order):
                        chunks.append({"c": c, "first": ci == 0, "last": ci == nch - 1, "pj": pj,
                                       "need_state": (pj is not None) or (ci < nch - 1)})
                for n_, ck in enumerate(chunks):
                    ck["cs"] = n_ % 2

                def seq_init(ck):
                    STb = STb2[cnt_["st"] % 2]
                    if ck["pj"] is None:
                        y0v = y0[:].rearrange("p (k n) -> p k n", n=128)
                        S.dma("sp", "ld_y0", y0v, st_in.rearrange("(k p) n -> p k n", p=128), writes=[y0.r])
                        for q in range(8):
                            p_ = pcb[q % 2]
                            for j in range(4):
                                k = q * 4 + j
                                S.op("pe", lambda e: e.transpose(out=p_[:, j, :], in_=y0[:, k * 128:(k + 1) * 128], identity=ident_f[:]),
                                     reads=[y0.r], writes=[p_.r], ms=(j == 3))
                            S.op("dve", lambda e: e.tensor_copy(out=ST[:, q * 512:(q + 1) * 512].rearrange("p (j d) -> p j d", d=128), in_=p_[:]), reads=[p_.r], writes=[ST.r])
                        S.op("act", lambda e: e.copy(out=STb[:], in_=ST[:]), reads=[ST.r], writes=[STb.r])
                    else:
                        S.op("dve", lambda e: e.memset(ST[:], 0.0), writes=[ST.r])
                        S.op("dve", lambda e: e.memset(STb[:], 0.0), writes=[STb.r])

                def seq_final(ck):
                    pj = ck["pj"]
                    for q in range(8):
                        p_ = pcb[q % 2]
                        for j in range(4):
                            k = q * 4 + j
                            S.op("pe", lambda e: e.transpose(out=p_[:, j, :], in_=ST[:, k * 128:(k + 1) * 128], identity=ident_f[:]),
                                 reads=[ST.r], writes=[p_.r], ms=(j == 3))
                        S.op("dve", lambda e: e.tensor_copy(out=y0[:, q * 512:(q + 1) * 512].rearrange("p (j d) -> p j d", d=128), in_=p_[:]), reads=[p_.r], writes=[y0.r])
                    S.dma("sp", "st_y0", st_out[pj].rearrange("(k p) n -> p k n", p=128), y0[:].rearrange("p (k n) -> p k n", n=128), reads=[y0.r])

                def preamble(ck):
                    c = ck["c"]; cs = ck["cs"]; r0 = c * 128
                    bcm = bcm2[cs]; BT = BT2[cs]; cbTm = cbTm2[cs]; xw = xw2[cs]; sm = sm2[cs]; xT = xT2[cs]
                    S.dma("sp", "ld_xfm", xfm[:], XC_cm[c, :, 0:32, :], writes=[xfm.r])
                    S.dma("sp", f"ld_bcm{cs}", bcm[:], XC_cm[c, :, 32:48, :], writes=[bcm.r])
                    for k, src in (("dt", DT_tm), ("acs", ACS_tm), ("wgt", WGT_tm), ("etot", ETOT)):
                        S.dma("sp", f"ld_{k}{cs}", sm[k][:], src[r0:r0 + 128, :], writes=[sm[k].r])
                    S.op("act", lambda e: e.activation(out=sm["lndt"][:], in_=sm["dt"][:], func=AF.Ln), reads=[sm["dt"].r], writes=[sm["lndt"].r])
                    S.op("dve", lambda e: e.tensor_tensor(out=sm["nb"][:], in0=sm["lndt"][:], in1=sm["acs"][:], op=ALU.subtract), reads=[sm["lndt"].r, sm["acs"].r], writes=[sm["nb"].r])
                    S.op("act", lambda e: e.activation(out=sm["eacs"][:], in_=sm["acs"][:], func=AF.Exp), reads=[sm["acs"].r], writes=[sm["eacs"].r])
                    for q in range(4):
                        p_ = ptr[trn[0] % 2]; trn[0] += 1
                        for j in range(8):
                            cc = q * 8 + j
                            S.op("pe", lambda e: e.transpose(out=p_[:, j, :], in_=xfm[:, cc, :], identity=ident_bf[:]),
                                 reads=[xfm.r], writes=[p_.r], ms=(j == 7))
                        evac_copy(xT[:, q * 1024:(q + 1) * 1024].rearrange("p (j d) -> p j d", d=128), p_, xT.r, q)
                    p_ = ptr[trn[0] % 2]; trn[0] += 1
                    for g in range(8):
                        S.op("pe", lambda e: e.transpose(out=p_[:, g, :], in_=bcm[:, g, :], identity=ident_bf[:]),
                             reads=[bcm.r], writes=[p_.r], ms=(g == 7))
                    evac_copy(BT[:].rearrange("p (j d) -> p j d", d=128), p_, BT.r, 1)
                    for q in range(2):
                        for j in range(4):
                            g = q * 4 + j
                            S.op("pe", lambda e: e.matmul(pcb[q][:, j, :], lhsT=bcm[:, g, :], rhs=bcm[:, 8 + g, :], start=True, stop=True),
                                 reads=[bcm.r], writes=[pcb[q].r], ms=(j == 3))
                        S.op("dve", lambda e: e.tensor_tensor(out=cbTm[:, q * 4:(q + 1) * 4, :], in0=pcb[q][:], in1=mask[:].unsqueeze(1).to_broadcast([128, 4, 128]), op=ALU.mult),
                             reads=[pcb[q].r], writes=[cbTm.r])
                    if ck["need_state"]:
                        S.op("dve", lambda e: e.tensor_tensor(out=xw[:].rearrange("p (h q) -> p h q", q=64), in0=xT[:].rearrange("p (h q) -> p h q", q=64),
                                                              in1=sm["wgt"][:, d * 64:(d + 1) * 64].unsqueeze(2).to_broadcast([128, 64, 64]), op=ALU.mult),
                             reads=[xT.r, sm["wgt"].r], writes=[xw.r])

                def state_update(ck):
                    cs = ck["cs"]
                    BT = BT2[cs]; xw = xw2[cs]; sm = sm2[cs]
                    if ck["need_state"]:
                        STn = STb2[(cnt_["st"] + 1) % 2]
                        S.op("dve", lambda e: e.tensor_tensor(out=ST[:].rearrange("p (h q) -> p h q", q=64), in0=ST[:].rearrange("p (h q) -> p h q", q=64),
                                                              in1=sm["etot"][:, d * 64:(d + 1) * 64].unsqueeze(2).to_broadcast([128, 64, 64]), op=ALU.mult),
                             reads=[ST.r, sm["etot"].r], writes=[ST.r])
                        for g in range(8):
                            gc0 = g * 512
                            pst = pcb[g % 2]
                            pstv = pst[:].rearrange("p j d -> p (j d)")
                            S.op("pe", lambda e: e.matmul(pstv, lhsT=BT[:, g * 128:(g + 1) * 128], rhs=xw[:, gc0:gc0 + 512], start=True, stop=True),
                                 reads=[BT.r, xw.r], writes=[pst.r], ms=True)
                            S.op("dve", lambda e: e.tensor_tensor(out=ST[:, gc0:gc0 + 512], in0=pstv, in1=ST[:, gc0:gc0 + 512], op=ALU.add), reads=[pst.r, ST.r], writes=[ST.r])
                        S.op("act", lambda e: e.copy(out=STn[:], in_=ST[:]), reads=[ST.r], writes=[STn.r])
                        cnt_["st"] += 1
                    if ck["last"]:
                        if ck["pj"] is not None:
                            seq_final(ck)
                        if not ck["need_state"]:
                            cnt_["st"] += 1

                def g_loads(it):
                    ck, g = it["ck"], it["g"]
                    c = ck["c"]; r0 = c * 128; gc0 = g * 512
                    k3 = it["i"] % 3
                    S.dma("sp", f"ld_bcs{k3}", bcs[k3][:], ACS_cm[c, d, g * 8:(g + 1) * 8, :].partition_broadcast(128), writes=[bcs[k3].r])
                    if d == 1:
                        S.dma("sp", f"ld_szg{k3}", szg[k3][:], SZ_tm[r0:r0 + 128, gc0:gc0 + 512], writes=[szg[k3].r])
                        S.dma("sp", f"ld_yg{k3}", yg[k3][:], Y0_tm[r0:r0 + 128, gc0:gc0 + 512], writes=[yg[k3].r])

                def g_exps(it):
                    ck, g = it["ck"], it["g"]
                    sm = sm2[ck["cs"]]; k = it["i"] % 2; k3 = it["i"] % 3
                    hc0 = d * 64 + g * 8
                    for h in range(8):
                        S.op("act", lambda e: e.activation(out=Eb_[k][:, h, :], in_=bcs[k3][:, h, :], func=AF.Exp, bias=sm["nb"][:, hc0 + h:hc0 + h + 1], scale=1.0),
                             reads=[bcs[k3].r, sm["nb"].r], writes=[Eb_[k].r], ms=(h == 7))

                def g_mt(it):
                    ck, g = it["ck"], it["g"]
                    cbTm = cbTm2[ck["cs"]]; k = it["i"] % 2
                    S.op("dve", lambda e: e.scalar_tensor_tensor(out=MT[k][:], in0=Eb_[k][:], scalar=1e30, in1=cbTm[:, g, :].unsqueeze(1).to_broadcast([128, 8, 128]), op0=ALU.min, op1=ALU.mult),
                         reads=[Eb_[k].r, cbTm.r], writes=[MT[k].r])

                def g_pe(it):
                    ck, g = it["ck"], it["g"]
                    cs = ck["cs"]; bcm = bcm2[cs]; xT = xT2[cs]; STb = ck["STb"]
                    k = it["i"] % 2
                    gc0 = g * 512
                    pyi = pyi2[k]; p_ = py[k]
                    S.op("pe", lambda e: e.matmul(pyi[:], lhsT=bcm[:, 8 + g, :], rhs=STb[:, gc0:gc0 + 512], start=True, stop=True), reads=[bcm.r, STb.r], writes=[pyi.r])
                    for h in range(8):
                        hh = g * 8 + h
                        S.op("pe", lambda e: e.matmul(p_[:, h * 64:(h + 1) * 64], lhsT=MT[k][:, h, :], rhs=xT[:, hh * 64:(hh + 1) * 64], start=True, stop=(d == 1)),
                             reads=[MT[k].r, xT.r], writes=[p_.r], ms=(d == 1 and h == 7))
                        if d == 0:
                            S.op("pe", lambda e: e.matmul(p_[:, h * 64:(h + 1) * 64], lhsT=Dg[:, hh, :], rhs=xT[:, hh * 64:(hh + 1) * 64], start=False, stop=True),
                                 reads=[xT.r, Dg.r], writes=[p_.r], ms=(h == 7))

                def g_evac(it):
                    ck, g = it["ck"], it["g"]
                    c = ck["c"]; cs = ck["cs"]; r0 = c * 128
                    sm = sm2[cs]
                    k = it["i"] % 2; k3 = it["i"] % 3
                    gc0 = g * 512; hc0 = d * 64 + g * 8
                    pyi = pyi2[k]; p_ = py[k]; tm = tmp[k]; y_ = yg[k3]
                    S.op("dve", lambda e: e.tensor_tensor(out=tm[:].rearrange("p (h q) -> p h q", q=64), in0=pyi[:].rearrange("p (h q) -> p h q", q=64),
                                                         in1=sm["eacs"][:, hc0:hc0 + 8].unsqueeze(2).to_broadcast([128, 8, 64]), op=ALU.mult),
                         reads=[pyi.r, sm["eacs"].r], writes=[tm.r])
                    if d == 0:
                        S.op("dve", lambda e: e.tensor_tensor(out=y_[:], in0=p_[:], in1=tm[:], op=ALU.add), reads=[p_.r, tm.r], writes=[y_.r])
                        S.dma("sp", f"st_yg{k3}", Y0_tm[r0:r0 + 128, gc0:gc0 + 512], y_[:], reads=[y_.r])
                    else:
                        sz_ = szg[k3]
                        S.op("dve", lambda e: e.tensor_tensor(out=tm[:], in0=p_[:], in1=tm[:], op=ALU.add), reads=[p_.r, tm.r], writes=[tm.r])
                        S.op("dve", lambda e: e.tensor_tensor(out=y_[:], in0=tm[:], in1=y_[:], op=ALU.add), reads=[tm.r, y_.r], writes=[y_.r])
                        S.op("dve", lambda e: e.tensor_tensor(out=sz_[:], in0=y_[:], in1=sz_[:], op=ALU.mult), reads=[y_.r, sz_.r], writes=[sz_.r])
                        S.op("act", lambda e: e.activation(out=junk[:], in_=sz_[:], func=AF.Square, accum_out=ss[:, g:g + 1]), reads=[sz_.r], writes=[junk.r, ss.r])
                        S.op("act", lambda e: e.activation(out=sd[:, g:g + 1], in_=ss[:, g:g + 1], func=AF.Ln, bias=EPS, scale=1.0 / 512), reads=[ss.r], writes=[sd.r])
                        S.op("act", lambda e: e.activation(out=rs[:, g:g + 1], in_=sd[:, g:g + 1], func=AF.Exp, scale=-0.5), reads=[sd.r], writes=[rs.r])

                def g_evac_b(it):
                    if d == 0:
                        return
                    ck, g = it["ck"], it["g"]
                    k3 = it["i"] % 3; gc0 = g * 512
                    sz_ = szg[k3]; yn = yn2[ck["cs"]]
                    S.op("dve", lambda e: e.scalar_tensor_tensor(out=yn[:, gc0:gc0 + 512], in0=sz_[:], scalar=rs[:, g:g + 1], in1=gb[:, gc0:gc0 + 512],
                                                              op0=ALU.mult, op1=ALU.mult), reads=[sz_.r, rs.r, gb.r], writes=[yn.r])

                def chunk_post(ck):
                    if d == 0:
                        return
                    c = ck["c"]; cs = ck["cs"]; r0 = c * 128
                    yn = yn2[cs]
                    for q in range(4):
                        p_ = ptr[trn[0] % 2]; trn[0] += 1
                        for j in range(8):
                            cc = q * 8 + j
                            S.op("pe", lambda e: e.transpose(out=p_[:, j, :], in_=yn[:, cc * 128:(cc + 1) * 128], identity=ident_bf[:]),
                                 reads=[yn.r], writes=[p_.r], ms=(j == 7))
                        evac_copy(ynT[:, q * 8:(q + 1) * 8, :], p_, ynT.r, q)
                    S.dma("sp", "st_ynT", Yv[:, :, r0:r0 + 128], ynT[:], reads=[ynT.r])

                items = []
                for ck in chunks:
                    for g in range(8):
                        items.append({"ck": ck, "g": g, "i": len(items)})
                NI = len(items)
                seq_init(chunks[0])
                preamble(chunks[0])
                g_loads(items[0]); g_loads(items[1])
                g_exps(items[0]); g_mt(items[0])
                for n_, ck in enumerate(chunks):
                    nxt = chunks[n_ + 1] if n_ + 1 < len(chunks) else None
                    ck["STb"] = STb2[cnt_["st"] % 2]
                    for g in range(8):
                        ii = n_ * 8 + g
                        if g == 1:
                            state_update(ck)
                        if g == 2 and nxt is not None and nxt["first"]:
                            seq_init(nxt)
                        if g == 2 and nxt is not None:
                            preamble(nxt)
                        if g > 0:
                            g_evac_b(items[ii - 1])
                        if ii + 2 < NI:
                            g_loads(items[ii + 2])
                        if ii + 1 < NI:
                            g_exps(items[ii + 1])
                        g_pe(items[ii])
                        g_evac(items[ii])
                        if ii + 1 < NI:
                            g_mt(items[ii + 1])
                    g_evac_b(items[n_ * 8 + 7])
                    chunk_post(ck)
        gemm("opr", Y_fm, 32, wblocks(ssd_w_out[0], 0, D, 512, "FM", make_resid_epi(i, 2)), 512, resid_alloc)

    def ffn(i):
        norm_phase(f"nf{i}", i, 1)
        Wgu = ffn_w_gu[i]

        def alloc(ph):
            return {"sg": [ph.sb(f"sg{j}", [128, 512]) for j in range(2)], "ot": [ph.sb(f"ot{j}", [128, 512], BF16) for j in range(3)], "n": [0]}

        def epi(ctx, blk, f, tb, ps):
            k = ctx["n"][0]; ctx["n"][0] += 1
            sg = ctx["sg"][k % 2]; ot = ctx["ot"][k % 3]
            S.op("act", lambda e: e.activation(out=sg[:], in_=ps[0][:], func=AF.Silu), reads=[ps[0].r], writes=[sg.r])
            S.op("dve", lambda e: e.tensor_tensor(out=ot[:], in0=ps[1][:], in1=sg[:], op=ALU.mult), reads=[ps[1].r, sg.r], writes=[ot.r])
            r0 = blk["c0"] + f * 128
            S.dma("sp", f"st_ot{k % 3}", A_fm[r0:r0 + 128, tb * 512:(tb + 1) * 512], ot[:], reads=[ot.r])

        blocks = [{"parts": [Wgu[:, c0:c0 + 256], Wgu[:, DFF + c0:DFF + c0 + 256]], "mode": "PAIR", "epi": epi, "c0": c0} for c0 in range(0, DFF, 256)]
        gemm(f"gu{i}", H_fm, 16, blocks, 512, alloc)
        gemm(f"dn{i}", A_fm, 44, wblocks(ffn_w_down[i], 0, D, 512, "FM", make_resid_epi(i, 5)), 512, resid_alloc)

    def attn_layer(i):
        norm_phase("n1m", i, 0)
        Wq = attn_w_qkv[0]

        def alloc(ph):
            ctx = {"sq": [ph.sb(f"sq{j}", [128, 512], BF16) for j in range(2)], "qn": [ph.sb(f"qn{j}", [128, 512]) for j in range(2)],
                   "sd": ph.sb("sd", [128, 512]), "rs": ph.sb("rs", [128, 512]), "t1": ph.sb("t1", [128, 512]), "t2": ph.sb("t2", [128, 512]),
                   "ot": [ph.sb(f"ot{j}", [128, 512], BF16) for j in range(2)], "vf": [ph.sb(f"vf{j}", [128, 512]) for j in range(2)],
                   "kk": ph.sb("kk", [128, 4, 128]),
                   "cos": ph.sb("cos", [128, TS]), "sin": ph.sb("sin", [128, TS]), "pm": ph.sb("pm", [128, 128]),
                   "pss": [ph.ps(f"pss{j}", [128, 512]) for j in range(2)], "n": [0]}
            S.dma("sp", "ld_cos", ctx["cos"][:], rope_cos, writes=[ctx["cos"].r])
            S.dma("sp", "ld_sin", ctx["sin"][:], rope_sin, writes=[ctx["sin"].r])
            S.dma("sp", "ld_pm", ctx["pm"][:], rope_pm, writes=[ctx["pm"].r])
            return ctx

        def qk_epi(ctx, blk, f, tb, ps):
            k = ctx["n"][0]; ctx["n"][0] += 1
            is_k = blk.get("is_k", False)
            hc = blk["c0"] // 128 + f
            sq = ctx["sq"][k % 2]; qn = ctx["qn"][k % 2]; ot = ctx["ot"][k % 2]; pss = ctx["pss"][k % 2]
            p0 = ps[0]
            S.op("act", lambda e: e.activation(out=sq[:], in_=p0[:], func=AF.Square), reads=[p0.r], writes=[sq.r])
            dst = (KT_fm if is_k else QT_fm)[hc * 128:(hc + 1) * 128, tb * 512:(tb + 1) * 512]

            def stage2():
                S.op("pe", lambda e: e.matmul(pss[:], lhsT=ones_bf[:], rhs=sq[:], start=True, stop=True), reads=[sq.r], writes=[pss.r])
                S.op("act", lambda e: e.activation(out=ctx["sd"][:], in_=pss[:], func=AF.Ln, bias=EPS, scale=1.0 / 128), reads=[pss.r], writes=[ctx["sd"].r])
                S.op("act", lambda e: e.activation(out=ctx["rs"][:], in_=ctx["sd"][:], func=AF.Exp, scale=-0.5), reads=[ctx["sd"].r], writes=[ctx["rs"].r])
                S.op("dve", lambda e: e.scalar_tensor_tensor(out=qn[:], in0=p0[:], scalar=qkg[:, (1 if is_k else 0):(2 if is_k else 1)], in1=ctx["rs"][:], op0=ALU.mult, op1=ALU.mult),
                     reads=[p0.r, ctx["rs"].r], writes=[qn.r])
                if tb < 4:
                    def stage3():
                        S.op("pe", lambda e: e.matmul(pss[:], lhsT=ctx["pm"][:], rhs=qn[:], start=True, stop=True), reads=[qn.r, ctx["pm"].r], writes=[pss.r])
                        S.op("dve", lambda e: e.tensor_tensor(out=ctx["t1"][:], in0=qn[:], in1=ctx["cos"][:, tb * 512:(tb + 1) * 512], op=ALU.mult), reads=[qn.r, ctx["cos"].r], writes=[ctx["t1"].r])
                        S.op("dve", lambda e: e.tensor_tensor(out=ctx["t2"][:], in0=pss[:], in1=ctx["sin"][:, tb * 512:(tb + 1) * 512], op=ALU.mult), reads=[pss.r, ctx["sin"].r], writes=[ctx["t2"].r])
                        S.op("dve", lambda e: e.tensor_tensor(out=ot[:], in0=ctx["t1"][:], in1=ctx["t2"][:], op=ALU.add), reads=[ctx["t1"].r, ctx["t2"].r], writes=[ot.r])
                        S.dma("sp", f"st_ot{k % 2}", dst, ot[:], reads=[ot.r])
                        return None
                    return stage3
                S.op("act", lambda e: e.copy(out=ot[:], in_=qn[:]), reads=[qn.r], writes=[ot.r])
                S.dma("sp", f"st_ot{k % 2}", dst, ot[:], reads=[ot.r])
                if is_k:
                    def stage3k():
                        for j in range(4):
                            S.op("pe", lambda e, j=j: e.transpose(out=pss[:].rearrange("p (j d) -> p j d", d=128)[:, j, :], in_=qn[:, j * 128:(j + 1) * 128], identity=ident_f[:]),
                                 reads=[qn.r], writes=[pss.r], ms=(j == 3))
                        S.op("dve", lambda e: e.tensor_copy(out=ctx["kk"][:], in_=pss[:].rearrange("p (j d) -> p j d", d=128)), reads=[pss.r], writes=[ctx["kk"].r])
                        S.dma("sp", "st_kk", nk_out[(tb - 4) * 512:(tb - 3) * 512, hc * 128:(hc + 1) * 128].rearrange("(j p) d -> p j d", p=128), ctx["kk"][:], reads=[ctx["kk"].r])
                        return None
                    return stage3k
                return None
            return stage2

        def v_epi(ctx, blk, t, tb, ps):
            k = ctx["n"][0]; ctx["n"][0] += 1
            vf = ctx["vf"][k % 2]; ot = ctx["ot"][k % 2]
            r0 = tb * 512 + t * 128
            S.op("act", lambda e: e.copy(out=vf[:], in_=ps[0][:]), reads=[ps[0].r], writes=[vf.r])
            S.op("dve", lambda e: e.tensor_copy(out=ot[:], in_=ps[0][:]), reads=[ps[0].r], writes=[ot.r])
            S.dma("sp", f"st_ot{k % 2}", V_tm[r0:r0 + 128, :], ot[:], reads=[ot.r])
            if tb >= 4:
                S.dma("sp", f"st_vf{k % 2}", nv_out[r0 - TS:r0 - TS + 128, :], vf[:], reads=[vf.r])

        kb_ = wblocks(Wq[:, 2048:2560], 0, 512, 512, "FM", qk_epi)
        for b_ in kb_:
            b_["is_k"] = True
        blocks = wblocks(Wq[:, 0:2048], 0, 2048, 512, "FM", qk_epi) + kb_ + wblocks(Wq[:, 2560:3072], 0, 512, 512, "TM", v_epi)
        gemm("qkv", H_fm, 16, blocks, 512, alloc)

        SCALE = 128 ** -0.5
        with Phase(S, "att") as ph:
            KTs = ph.sb("KTs", [128, 2304], BF16); Vs = ph.sb("Vs", [128, 18, 128], BF16)
            QTs = [ph.sb(f"QTs{j}", [128, TS], BF16) for j in range(2)]
            Es = [ph.sb(f"E{j}", [128, 512], BF16) for j in range(3)]
            rec = ph.sb("rec", [128, 512]); ots = [ph.sb(f"ot{j}", [128, 512], BF16) for j in range(2)]
            ckf = ph.sb("ckf", [128, 2, 512])
            psc = [ph.ps(f"psc{j}", [128, 512]) for j in range(3)]
            pot = [ph.ps(f"pot{j}", [128, 512]) for j in range(2)]
            psm = [ph.ps(f"psm{j}", [128, 512]) for j in range(2)]
            ptc = ph.ps("ptc", [128, 128])
            cnt = {"q": 0, "e": 0, "s": 0, "o": 0}
            S.dma("sp", "ld_ckf", ckf[:], ck_in.rearrange("(j p) c -> p j c", p=128), writes=[ckf.r])

            def attn_block(qt, q0, N, nkb, dst, kb0=0):
                o = cnt["o"]; cnt["o"] += 1
                po = pot[o % 2]; pm2 = psm[o % 2]; ot = ots[o % 2]

                def sc(kb):
                    p_ = psc[cnt["s"] % 3]; cnt["s"] += 1
                    S.op("pe", lambda e: e.matmul(p_[:, 0:N], lhsT=KTs[:, (kb0 + kb) * 128:(kb0 + kb + 1) * 128], rhs=qt[:, q0:q0 + N], start=True, stop=True),
                         reads=[KTs.r, qt.r], writes=[p_.r])
                    return p_
                pq = [sc(0)]
                if nkb > 1:
                    pq.append(sc(1))
                for kb in range(nkb):
                    p_cur = pq.pop(0)
                    E = Es[cnt["e"] % 3]; cnt["e"] += 1
                    S.op("act", lambda e, E=E, p_cur=p_cur: e.activation(out=E[:, 0:N], in_=p_cur[:, 0:N], func=AF.Exp, scale=SCALE), reads=[p_cur.r], writes=[E.r])
                    S.op("pe", lambda e, E=E, kb=kb: e.matmul(po[:, 0:N], lhsT=Vs[:, kb0 + kb, :], rhs=E[:, 0:N], start=(kb == 0), stop=(kb == nkb - 1)),
                         reads=[Vs.r, E.r], writes=[po.r], ms=False)
                    S.op("pe", lambda e, E=E, kb=kb: e.matmul(pm2[:, 0:N], lhsT=ones_bf[:], rhs=E[:, 0:N], start=(kb == 0), stop=(kb == nkb - 1)),
                         reads=[E.r], writes=[pm2.r, po.r], ms=True)
                    if kb + 2 < nkb:
                        pq.append(sc(kb + 2))
                S.op("dve", lambda e: e.reciprocal(out=rec[:, 0:N], in_=pm2[:, 0:N]), reads=[pm2.r], writes=[rec.r])
                S.op("dve", lambda e: e.tensor_tensor(out=ot[:, 0:N], in0=po[:, 0:N], in1=rec[:, 0:N], op=ALU.mult), reads=[po.r, rec.r], writes=[ot.r])
                S.dma("sp", f"st_ao{o % 2}", dst, ot[:, 0:N], reads=[ot.r])

            for kvh in range(4):
                for j in range(2):
                    S.op("pe", lambda e, j=j, kvh=kvh: e.transpose(out=ptc[:], in_=ckf[:, j, kvh * 128:(kvh + 1) * 128], identity=ident_f[:]), reads=[ckf.r], writes=[ptc.r])
                    S.op("dve", lambda e, j=j: e.tensor_copy(out=KTs[:, j * 128:(j + 1) * 128], in_=ptc[:]), reads=[ptc.r], writes=[KTs.r])
                S.dma("sp", "ld_kts", KTs[:, 256:2304], KT_fm[kvh * 128:(kvh + 1) * 128, 0:TS], writes=[KTs.r])
                S.dma("pool", "ld_vsc", Vs[:, 0:2, :], cv_in[:, kvh * 128:(kvh + 1) * 128].rearrange("(j p) d -> p j d", p=128), writes=[Vs.r])
                S.dma("sp", "ld_vs", Vs[:, 2:18, :], V_tm[0:TS, kvh * 128:(kvh + 1) * 128].rearrange("(j p) d -> p j d", p=128), writes=[Vs.r])
                for r in range(4):
                    h = kvh * 4 + r
                    qt = QTs[cnt["q"] % 2]; k = cnt["q"] % 2; cnt["q"] += 1
                    S.dma("sp", f"ld_qt{k}", qt[:], QT_fm[h * 128:(h + 1) * 128, 0:TS], writes=[qt.r])
                    for qb in range(4):
                        attn_block(qt, qb * 512, 512, 18, AO_fm[h * 128:(h + 1) * 128, qb * 512:(qb + 1) * 512])
            def p_scores(qt, pj):
                p_ = psc[cnt["s"] % 3]; cnt["s"] += 1
                for kb in range(2):
                    S.op("pe", lambda e: e.matmul(p_[:, kb * LP:(kb + 1) * LP], lhsT=KTs[:, (2 * pj + kb) * 128:(2 * pj + kb + 1) * 128], rhs=qt[:, pj * LP:(pj + 1) * LP], start=True, stop=True),
                         reads=[KTs.r, qt.r], writes=[p_.r], ms=(kb == 1))
                return p_

            def p_rest(p_, pj, dst):
                o = cnt["o"]; cnt["o"] += 1
                po = pot[o % 2]; pm2 = psm[o % 2]; ot = ots[o % 2]
                E = Es[cnt["e"] % 3]; cnt["e"] += 1
                S.op("act", lambda e: e.activation(out=E[:], in_=p_[:], func=AF.Exp, scale=SCALE), reads=[p_.r], writes=[E.r])
                for kb in range(2):
                    S.op("pe", lambda e: e.matmul(po[:, 0:LP], lhsT=Vs[:, 2 * pj + kb, :], rhs=E[:, kb * LP:(kb + 1) * LP], start=(kb == 0), stop=(kb == 1)),
                         reads=[Vs.r, E.r], writes=[po.r], ms=False)
                for kb in range(2):
                    S.op("pe", lambda e: e.matmul(pm2[:, 0:LP], lhsT=ones_bf[:], rhs=E[:, kb * LP:(kb + 1) * LP], start=(kb == 0), stop=(kb == 1)),
                         reads=[E.r], writes=[pm2.r, po.r], ms=(kb == 1))
                S.op("dve", lambda e: e.reciprocal(out=rec[:, 0:LP], in_=pm2[:, 0:LP]), reads=[pm2.r], writes=[rec.r])
                S.op("dve", lambda e: e.tensor_tensor(out=ot[:, 0:LP], in0=po[:, 0:LP], in1=rec[:, 0:LP], op=ALU.mult), reads=[po.r, rec.r], writes=[ot.r])
                S.dma("sp", f"st_ao{o % 2}", dst, ot[:, 0:LP], reads=[ot.r])

            work = []
            for kvh in range(4):
                for r in range(4):
                    for pj in range(NPR):
                        work.append((kvh, r, pj))
            pend = None
            for (kvh, r, pj) in work:
                h = kvh * 4 + r
                if r == 0 and pj == 0:
                    if pend is not None:
                        p_rest(*pend); pend = None
                    S.dma("sp", "ld_kts", KTs[:, 0:NPR * LP], KT_fm[kvh * 128:(kvh + 1) * 128, TS:T], writes=[KTs.r])
                    S.dma("sp", "ld_vs", Vs[:, 0:8, :], V_tm[TS:T, kvh * 128:(kvh + 1) * 128].rearrange("(j p) d -> p j d", p=128), writes=[Vs.r])
                if pj == 0:
                    qt = QTs[cnt["q"] % 2]; k = cnt["q"] % 2; cnt["q"] += 1
                    S.dma("sp", f"ld_qt{k}", qt[:, 0:NPR * LP], QT_fm[h * 128:(h + 1) * 128, TS:T], writes=[qt.r])
                p_ = p_scores(qt, pj)
                if pend is not None:
                    p_rest(*pend)
                pend = (p_, pj, AO_fm[h * 128:(h + 1) * 128, TS + pj * LP:TS + (pj + 1) * LP])
            p_rest(*pend)

        gemm("opj", AO_fm, 16, wblocks(attn_w_o[0], 0, D, 512, "FM", make_resid_epi(i, 2)), 512, resid_alloc)

    def final_phase():
        with Phase(S, "fin") as ph:
            NT = 256
            NB = T // NT
            xb = [ph.sb(f"x{j}", [128, 16, NT]) for j in range(2)]
            sq = ph.sb("sq", [128, 16, NT], BF16)
            t1 = ph.sb("t1", [128, 16, NT]); t2b = [ph.sb(f"t2{j}", [128, 16, NT]) for j in range(2)]
            sd = ph.sb("sd", [128, NT]); R = ph.sb("R", [128, NT])
            yo = [ph.sb(f"yo{j}", [128, D]) for j in range(2)]
            pss = [ph.ps(f"ss{j}", [128, NT]) for j in range(2)]
            pp = [ph.ps(f"pp{j}", [128, 4, 128]) for j in range(6)]
            cn = {"n": 0, "m": 0}

            def stage_a(tb):
                x = xb[tb % 2]; ps_ = pss[tb % 2]; t2 = t2b[tb % 2]
                S.dma("sp", f"ld_nx{tb % 2}", x[:], Xv[:, :, tb * NT:(tb + 1) * NT], writes=[x.r])
                S.op("act", lambda e: e.activation(out=sq[:], in_=x[:], func=AF.Square), reads=[x.r], writes=[sq.r])
                for kc in range(16):
                    S.op("pe", lambda e: e.matmul(ps_[:], lhsT=ones_bf[:], rhs=sq[:, kc, :], start=(kc == 0), stop=(kc == 15)),
                         reads=[sq.r], writes=[ps_.r], ms=(kc == 15))
                S.op("act", lambda e: e.activation(out=sd[:], in_=ps_[:], func=AF.Ln, bias=EPS, scale=1.0 / D), reads=[ps_.r], writes=[sd.r])
                S.op("act", lambda e: e.activation(out=R[:], in_=sd[:], func=AF.Exp, scale=-0.5), reads=[sd.r], writes=[R.r])
                S.op("dve", lambda e: e.tensor_tensor(out=t1[:], in0=x[:], in1=gfin[:, :].unsqueeze(2).to_broadcast([128, 16, NT]), op=ALU.mult), reads=[x.r], writes=[t1.r])
                S.op("dve", lambda e: e.tensor_tensor(out=t2[:], in0=t1[:], in1=R[:].unsqueeze(1).to_broadcast([128, 16, NT]), op=ALU.mult), reads=[t1.r, R.r], writes=[t2.r])

            def stage_b(tb):
                t2 = t2b[tb % 2]
                for hf in range(NT // 128):
                    y = yo[cn["m"] % 2]; km = cn["m"] % 2; cn["m"] += 1
                    for q in range(4):
                        p_ = pp[cn["n"] % 6]; cn["n"] += 1
                        for j in range(4):
                            kc = q * 4 + j
                            S.op("pe", lambda e: e.transpose(out=p_[:, j, :], in_=t2[:, kc, hf * 128:(hf + 1) * 128], identity=ident_f[:]),
                                 reads=[t2.r], writes=[p_.r], ms=(j == 3))
                        dsl = y[:, q * 512:(q + 1) * 512].rearrange("p (j d) -> p j d", d=128)
                        if q % 2 == 0:
                            S.op("dve", lambda e: e.tensor_copy(out=dsl, in_=p_[:]), reads=[p_.r], writes=[y.r])
                        else:
                            S.op("act", lambda e: e.copy(out=dsl, in_=p_[:]), reads=[p_.r], writes=[y.r])
                    tok0 = tb * NT + hf * 128
                    dst = ys_out[tok0:tok0 + 128, :] if tok0 < TS else yp_out[tok0 - TS:tok0 - TS + 128, :]
                    S.dma("sp", f"st_yo{km}", dst, y[:], reads=[y.r], writes=[OUTR])

            stage_a(0)
            for tb in range(NB):
                if tb + 1 < NB:
                    stage_a(tb + 1)
                stage_b(tb)

    OUTR = DRes("outputs")
    try:
        convert_phase()
        ssd_layer(0)
        ffn(0)
        attn_layer(1)
        ffn(1)
        final_phase()
    except StopBuild:
        pass
    S.finish([OUTR])
    return nc


def _rope_consts():
    inv = (10000.0 ** (-np.arange(0, 64, 2, dtype=np.float32) / 64)).astype(np.float32)
    t = np.arange(TS)
    row = (t // 64).astype(np.float32); col = (t % 64).astype(np.float32)
    cos = np.zeros((128, TS), np.float32); sin = np.zeros((128, TS), np.float32)
    pm = np.zeros((128, 128), np.float32)
    for m in range(128):
        pos = row if m < 64 else col
        ang = (pos * inv[m % 32]).astype(np.float32)
        first = (m % 64) < 32
        cos[m] = np.cos(ang)
        sin[m] = -np.sin(ang) if first else np.sin(ang)
        pm[(m + 32) if first else (m - 32), m] = 1.0
    return cos, sin, pm


_PROG = None


def kernel(**inp):
    global _PROG
    if _PROG is None:
        _PROG = build_program()
    nc = _PROG
    f = lambda a: np.ascontiguousarray(np.asarray(a, dtype=np.float32))
    cos, sin, pm = _rope_consts()
    shared = {k: f(inp[k]) for k in ("w_mod", "b_mod", "norm_mix", "norm_ffn", "ssd_w_in", "ssd_conv_w", "ssd_conv_b", "ssd_norm",
                                     "ssd_w_out", "attn_w_qkv", "attn_q_norm", "attn_k_norm", "attn_w_o", "ffn_w_gu", "ffn_w_down")}
    shared["ssd_a_log"] = f(inp["ssd_a_log"]).reshape(1, 128)
    shared["ssd_dt_bias"] = f(inp["ssd_dt_bias"]).reshape(1, 128)
    shared["ssd_d"] = f(inp["ssd_d"]).reshape(1, 128)
    shared["final_norm"] = f(inp["final_norm"]).reshape(1, D)
    shared["cctx"] = f(inp["c_ctx"]).reshape(1, D)
    shared["rope_cos"] = cos; shared["rope_sin"] = sin; shared["rope_pm"] = pm
    xs = f(inp["x_sample"]); xp = f(inp["x_prompt"]); sf = f(inp["state_ssd_fwd"]); sb = f(inp["state_ssd_bwd"])
    ck = f(inp["cache_k"]); cv = f(inp["cache_v"]); c = f(inp["c"])
    in_maps = []
    for b in range(8):
        m = dict(shared)
        m["xs"] = xs[b]; m["xp"] = xp[4 * b:4 * b + 4].reshape(NPR * LP, D)
        m["sf"] = sf[b, 0].reshape(DI, 128); m["sbw"] = sb[b, 0].reshape(DI, 128)
        m["ck"] = ck[b, 0].reshape(256, 512); m["cv"] = cv[b, 0].reshape(256, 512)
        m["c"] = c[b:b + 1]
        in_maps.append(m)
    res = run_bass_kernel_spmd(nc, in_maps, core_ids=list(range(8)))
    R = res.results
    y_prompt = np.concatenate([r["yp"].reshape(NPR, LP, D) for r in R], 0)
    y_sample = np.stack([r["ys"] for r in R], 0)
    new_f = np.concatenate([r["nf"].reshape(NPR, 1, 64, 64, 128) for r in R], 0)
    new_b = np.concatenate([r["nb"].reshape(NPR, 1, 64, 64, 128) for r in R], 0)
    new_k = np.concatenate([r["nk"].reshape(NPR, 1, LP, 4, 128) for r in R], 0)
    new_v = np.concatenate([r["nv"].reshape(NPR, 1, LP, 4, 128) for r in R], 0)
    return tuple(np.ascontiguousarray(a.astype(np.float32)) for a in (y_prompt, y_sample, new_f, new_b, new_k, new_v))
```
